# Optimizing a Trainium2 kernel written in Bass

```python
import math
import jax, jax.numpy as jnp
from jax import lax
import numpy as np

D_MODEL = 1024
BATCH = 16
SEQ = 2048
DEPTH = 1

CHUNK = 64
Q_BLOCK = 128
MIX_WIDTH = D_MODEL
LRU_WIDTH = MIX_WIDTH // 2
LRU_BLOCKS = 8
LRU_BLOCK_DIM = LRU_WIDTH // LRU_BLOCKS
LRU_CONV = 4
LRU_C = 8.0
DIFF_WIDTH = MIX_WIDTH - LRU_WIDTH
DIFF_HEADS = 4
DIFF_VDIM = DIFF_WIDTH // DIFF_HEADS
DIFF_QK = DIFF_VDIM // 2
REL_BUCKETS = 32
REL_MAX_DIST = 128
FFN_MULT = 3
D_FF = FFN_MULT * D_MODEL
FFN_CONV = 3
EPS = 1e-6
SUBLN_EPS = 1e-5
NEG_INF = -1e30
IN_COLS = 2 * LRU_WIDTH + 2 * DIFF_WIDTH + DIFF_WIDTH

kernel_name = "hymba_rglru_diffattn_convffn"


def rms_norm(x, g, eps=EPS):
    x32 = x.astype(jnp.float32)
    y = x32 * lax.rsqrt(jnp.mean(x32 * x32, axis=-1, keepdims=True) + eps)
    return (y * g.astype(jnp.float32)).astype(x.dtype)


def causal_dwconv(x, w, b):
    k, c = w.shape
    y = lax.conv_general_dilated(
        x, w[:, None, :].astype(x.dtype), window_strides=(1,),
        padding=[(k - 1, 0)], dimension_numbers=("NWC", "WIO", "NWC"),
        feature_group_count=c)
    return y + b.astype(x.dtype)


def block_diag(x, w, b):
    bsz, s, _ = x.shape
    xb = x.reshape(bsz, s, LRU_BLOCKS, LRU_BLOCK_DIM)
    y = jnp.einsum("bshi,hij->bshj", xb, w) + b
    return y.reshape(bsz, s, LRU_WIDTH)


def rg_lru(x, w_a, b_a, w_x, b_x, lam):
    r = jax.nn.sigmoid(block_diag(x, w_a, b_a).astype(jnp.float32))
    i = jax.nn.sigmoid(block_diag(x, w_x, b_x).astype(jnp.float32))
    log_a = -LRU_C * r * jax.nn.softplus(-lam.astype(jnp.float32))
    a = jnp.exp(log_a)
    mult = jnp.sqrt(1.0 - jnp.exp(2.0 * log_a))
    u = mult * (i * x.astype(jnp.float32))

    def combine(left, right):
        a_l, b_l = left
        a_r, b_r = right
        return a_l * a_r, a_r * b_l + b_r

    _, h = lax.associative_scan(combine, (a, u), axis=1)
    return h.astype(x.dtype)


def rel_bucket(rel):
    half = REL_BUCKETS // 2
    max_exact = half // 2
    ret = (rel > 0).astype(jnp.int32) * half
    n = jnp.abs(rel)
    nf = jnp.maximum(n, 1).astype(jnp.float32)
    large = max_exact + (jnp.log(nf / max_exact) / math.log(REL_MAX_DIST / max_exact)
                         * (half - max_exact)).astype(jnp.int32)
    large = jnp.minimum(large, half - 1)
    return ret + jnp.where(n < max_exact, n, large)


def diff_attention(q, k, v, rel_bias, lam, subln_g, lambda_init):
    bsz, s = q.shape[0], q.shape[1]
    nb = s // Q_BLOCK
    scale = DIFF_QK ** -0.5
    k_pos = jnp.arange(s, dtype=jnp.int32)
    q_blocks = jnp.moveaxis(q.reshape(bsz, nb, Q_BLOCK, DIFF_HEADS, 2, DIFF_QK), 1, 0)

    def attend(args):
        qb, bi = args
        q_pos = bi * Q_BLOCK + jnp.arange(Q_BLOCK, dtype=jnp.int32)
        logits = jnp.einsum("bqhcd,bkhcd->bhcqk", qb, k).astype(jnp.float32) * scale
        bias = rel_bias.astype(jnp.float32)[rel_bucket(k_pos[None, :] - q_pos[:, None])]
        logits = logits + jnp.transpose(bias, (2, 0, 1))[None, :, None]
        mask = (k_pos // CHUNK)[None, :] <= (q_pos // CHUNK)[:, None]
        logits = jnp.where(mask[None, None, None], logits, NEG_INF)
        p = jax.nn.softmax(logits, axis=-1)
        w = p[:, :, 0] - lam * p[:, :, 1]
        return jnp.einsum("bhqk,bkhe->bqhe", w.astype(v.dtype), v)

    out = lax.map(attend, (q_blocks, jnp.arange(nb, dtype=jnp.int32)))
    out = jnp.moveaxis(out, 0, 1).reshape(bsz, s, DIFF_HEADS, DIFF_VDIM)
    out = rms_norm(out, subln_g, SUBLN_EPS) * (1.0 - lambda_init)
    return out.reshape(bsz, s, DIFF_WIDTH)


def setup_inputs(seed: int = 0) -> dict:
    key = jax.random.key(seed)
    ks = jax.random.split(key, 24)
    f32 = jnp.float32
    nrm = lambda k, shape, sc: jax.random.normal(k, shape, f32) * sc
    a_c = jax.random.uniform(ks[8], (DEPTH, LRU_WIDTH), f32, 0.9, 0.999)
    a = a_c ** (1.0 / LRU_C)
    lru_lambda = jnp.log(a) - jnp.log1p(-a)
    return {
        "x": nrm(ks[0], (BATCH, SEQ, D_MODEL), 1.0),
        "norm1_g": 1.0 + nrm(ks[1], (DEPTH, D_MODEL), 0.02),
        "w_in": nrm(ks[2], (DEPTH, D_MODEL, IN_COLS), D_MODEL ** -0.5),
        "lru_conv_w": nrm(ks[3], (DEPTH, LRU_CONV, LRU_WIDTH), LRU_CONV ** -0.5),
        "lru_conv_b": nrm(ks[4], (DEPTH, LRU_WIDTH), 0.01),
        "lru_wa": nrm(ks[5], (DEPTH, LRU_BLOCKS, LRU_BLOCK_DIM, LRU_BLOCK_DIM), LRU_BLOCK_DIM ** -0.5),
        "lru_ba": nrm(ks[6], (DEPTH, LRU_BLOCKS, LRU_BLOCK_DIM), 0.01),
        "lru_wx": nrm(ks[7], (DEPTH, LRU_BLOCKS, LRU_BLOCK_DIM, LRU_BLOCK_DIM), LRU_BLOCK_DIM ** -0.5),
        "lru_bx": nrm(ks[9], (DEPTH, LRU_BLOCKS, LRU_BLOCK_DIM), 0.01),
        "lru_lambda": lru_lambda,
        "diff_lq1": nrm(ks[10], (DEPTH, DIFF_QK), 0.1),
        "diff_lk1": nrm(ks[11], (DEPTH, DIFF_QK), 0.1),
        "diff_lq2": nrm(ks[12], (DEPTH, DIFF_QK), 0.1),
        "diff_lk2": nrm(ks[13], (DEPTH, DIFF_QK), 0.1),
        "diff_subln_g": 1.0 + nrm(ks[14], (DEPTH, DIFF_VDIM), 0.02),
        "rel_bias": nrm(ks[15], (REL_BUCKETS, DIFF_HEADS), 0.5),
        "w_out": nrm(ks[16], (DEPTH, MIX_WIDTH, D_MODEL), MIX_WIDTH ** -0.5),
        "norm2_g": 1.0 + nrm(ks[17], (DEPTH, D_MODEL), 0.02),
        "ffn_w_up": nrm(ks[18], (DEPTH, D_MODEL, 2 * D_FF), D_MODEL ** -0.5),
        "ffn_conv_w": nrm(ks[19], (DEPTH, FFN_CONV, D_FF), FFN_CONV ** -0.5),
        "ffn_conv_b": nrm(ks[20], (DEPTH, D_FF), 0.01),
        "ffn_w_down": nrm(ks[21], (DEPTH, D_FF, D_MODEL), D_FF ** -0.5),
        "final_norm_g": 1.0 + nrm(ks[22], (D_MODEL,), 0.02),
    }


def reference(x, norm1_g, w_in, lru_conv_w, lru_conv_b, lru_wa, lru_ba, lru_wx, lru_bx,
              lru_lambda, diff_lq1, diff_lk1, diff_lq2, diff_lk2, diff_subln_g, rel_bias,
              w_out, norm2_g, ffn_w_up, ffn_conv_w, ffn_conv_b, ffn_w_down, final_norm_g):
    bsz, s, _ = x.shape
    for l in range(DEPTH):
        lambda_init = 0.8 - 0.6 * math.exp(-0.3 * l)
        h = rms_norm(x, norm1_g[l])
        proj = h @ w_in[l]
        lru_x, lru_gate, q, k, v = jnp.split(
            proj, np.cumsum([LRU_WIDTH, LRU_WIDTH, DIFF_WIDTH, DIFF_WIDTH]).tolist(), axis=-1)
        lru_x = causal_dwconv(lru_x, lru_conv_w[l], lru_conv_b[l])
        lru_h = rg_lru(lru_x, lru_wa[l], lru_ba[l], lru_wx[l], lru_bx[l], lru_lambda[l])
        out_a = jax.nn.gelu(lru_gate, approximate=True) * lru_h
        lam = (jnp.exp(jnp.sum(diff_lq1[l] * diff_lk1[l]).astype(jnp.float32))
               - jnp.exp(jnp.sum(diff_lq2[l] * diff_lk2[l]).astype(jnp.float32))
               + lambda_init)
        q = q.reshape(bsz, s, DIFF_HEADS, 2, DIFF_QK)
        k = k.reshape(bsz, s, DIFF_HEADS, 2, DIFF_QK)
        v = v.reshape(bsz, s, DIFF_HEADS, DIFF_VDIM)
        out_b = diff_attention(q, k, v, rel_bias, lam, diff_subln_g[l], lambda_init)
        mixed = jnp.concatenate([out_a, out_b], axis=-1)
        x = x + mixed @ w_out[l]
        h = rms_norm(x, norm2_g[l])
        gate, val = jnp.split(h @ ffn_w_up[l], 2, axis=-1)
        gate = causal_dwconv(gate, ffn_conv_w[l], ffn_conv_b[l])
        x = x + (jax.nn.gelu(gate, approximate=True) * val) @ ffn_w_down[l]
    return rms_norm(x, final_norm_g)
```

```python
import math
from contextlib import ExitStack

import numpy as np
import concourse.bass as bass
import concourse.mybir as mybir
from concourse.bass_utils import run_bass_kernel_spmd

F32 = mybir.dt.float32
BF16 = mybir.dt.bfloat16
AF = mybir.ActivationFunctionType
ALU = mybir.AluOpType
AX = mybir.AxisListType

NCORES = 8
D = 1024
G = 512
NT = 4
LAMBDA_INIT = 0.8 - 0.6 * math.exp(-0.3 * 0)
EPS = 1e-6
SUBLN_EPS = 1e-5
NRING = 3
DBG_G = 0


class Buf:
    __slots__ = ("name", "w", "r")

    def __init__(self, name):
        self.name = name
        self.w = None
        self.r = {}


class Sched:
    NDS = 32

    def __init__(self, nc, stack):
        self.nc = nc
        self.eng = {"pe": nc.tensor, "act": nc.scalar, "dve": nc.vector, "pool": nc.gpsimd, "sp": nc.sync}
        self.prog = {k: [] for k in self.eng}
        self.sems = {}
        for k in ["pe", "act", "dve", "pool"]:
            self.sems[("e", k)] = stack.enter_context(nc.semaphore("s_" + k))
        for i in range(self.NDS):
            self.sems[("d", i)] = stack.enter_context(nc.semaphore("d%d" % i))
        self.ecount = {k: 0 for k in self.eng}
        self.dcount = [0] * self.NDS
        self.dpool = {"sp": list(range(0, 24)), "pool": list(range(24, 32)), "act": list(range(24, 32))}
        self.dnext = {"sp": 0, "pool": 0, "act": 0}
        self.waited = {k: {} for k in self.eng}

    def _deps(self, eng, reads, writes):
        need = {}

        def add(s, v):
            if need.get(s, 0) < v:
                need[s] = v

        for b in reads:
            if b.w is not None:
                add(*b.w)
        for b in writes:
            if b.w is not None:
                add(*b.w)
            for s, v in b.r.items():
                add(s, v)
        out = []
        for s, v in need.items():
            if eng == "pe" and s == ("e", "pe"):
                continue
            if self.waited[eng].get(s, 0) >= v:
                continue
            self.waited[eng][s] = v
            out.append((s, v))
        return out

    def _update(self, reads, writes, tok):
        s, v = tok
        for b in reads:
            if b.r.get(s, 0) < v:
                b.r[s] = v
        for b in writes:
            b.w = tok
            b.r = {}

    def op(self, eng, fn, reads=(), writes=()):
        waits = self._deps(eng, reads, writes)
        self.ecount[eng] += 1
        tok = (("e", eng), self.ecount[eng])
        self.prog[eng].append((fn, waits, tok[0], 1))
        self._update(reads, writes, tok)
        return tok

    def dma(self, q, out, in_, reads=(), writes=(), **kw):
        waits = self._deps(q, reads, writes)
        pool_ = self.dpool[q]
        i = pool_[self.dnext[q] % len(pool_)]
        self.dnext[q] += 1
        s = ("d", i)
        prev = self.dcount[i]
        if prev and self.waited[q].get(s, 0) < prev:
            self.waited[q][s] = prev
            waits.append((s, prev))
        self.dcount[i] += 16
        tok = (s, self.dcount[i])
        self.prog[q].append((lambda e: e.dma_start(out=out, in_=in_, **kw), waits, s, 16))
        self._update(reads, writes, tok)
        return tok

    def wait_all(self, eng, toks):
        waits = []
        for s, v in toks:
            if self.waited[eng].get(s, 0) < v:
                self.waited[eng][s] = v
                waits.append((s, v))
        self.prog[eng].append((None, waits, None, 0))

    def emit(self, block):
        def mk(k):
            def body(e):
                for fn, waits, sem, inc in self.prog[k]:
                    for s, v in waits:
                        e.wait_ge(self.sems[s], v)
                    if fn is not None:
                        ins = fn(e)
                        ins.then_inc(self.sems[sem], inc)
            return body

        block.tensor(mk("pe"))
        block.scalar(mk("act"))
        block.vector(mk("dve"))
        block.gpsimd(mk("pool"))
        block.sync(mk("sp"))


class Pack:
    def __init__(self):
        self.cols = []
        self.off = {}
        self.n = 0

    def add(self, name, arr):
        arr = np.ascontiguousarray(arr, dtype=np.float32).reshape(128, -1)
        self.off[name] = (self.n, arr.shape[1])
        self.cols.append(arr)
        self.n += arr.shape[1]

    def build(self):
        return np.ascontiguousarray(np.concatenate(self.cols, axis=1))


def _rel_bucket_np(rel):
    half = 16
    max_exact = 8
    ret = (rel > 0).astype(np.int32) * half
    n = np.abs(rel)
    nf = np.maximum(n, 1).astype(np.float32)
    large = max_exact + (np.log(nf / max_exact) / math.log(128 / max_exact) * (half - max_exact)).astype(np.int32)
    large = np.minimum(large, half - 1)
    return ret + np.where(n < max_exact, n, large)


def _chunkT(v, nchunk):
    return np.asarray(v, np.float32).reshape(nchunk, 128).T


def pack_params(inp):
    pk = Pack()
    rep = lambda v: np.broadcast_to(np.asarray(v, np.float32).reshape(1, -1), (128, np.asarray(v).size))
    pk.add("g1T", _chunkT(inp["norm1_g"][0], 8))
    pk.add("g2T", _chunkT(inp["norm2_g"][0], 8))
    pk.add("gfb", rep(inp["final_norm_g"]))
    lcw = np.asarray(inp["lru_conv_w"][0], np.float32)
    pk.add("lcw", lcw.reshape(4, 4, 128).transpose(2, 1, 0))
    pk.add("lcb", _chunkT(inp["lru_conv_b"][0], 4))
    pk.add("lba", _chunkT(np.asarray(inp["lru_ba"][0]).reshape(-1), 4))
    pk.add("lbx", _chunkT(np.asarray(inp["lru_bx"][0]).reshape(-1), 4))
    pk.add("llam", _chunkT(inp["lru_lambda"][0], 4))
    fcw = np.asarray(inp["ffn_conv_w"][0], np.float32)
    pk.add("fcw", fcw.reshape(3, 24, 128).transpose(2, 1, 0))
    pk.add("fcb", _chunkT(inp["ffn_conv_b"][0], 24))
    pk.add("subg", np.asarray(inp["diff_subln_g"][0], np.float32).reshape(128, 1))
    rb = np.asarray(inp["rel_bias"], np.float32)
    pk.add("farb", rep(rb[15, :]))
    kk = np.arange(128)[:, None]
    qq = np.arange(128)[None, :]
    bt = np.zeros((128, 4, 256), np.float32)
    rel_d = kk - qq
    rel_s = kk - (qq + 128)
    bd = _rel_bucket_np(rel_d)
    bs = _rel_bucket_np(rel_s)
    masked = (kk // 64) > (qq // 64)
    for h in range(4):
        t = rb[bd, h].copy()
        t[masked] = -30000.0
        bt[:, h, 0:128] = t
        bt[:, h, 128:256] = rb[bs, h]
    pk.add("biasT", bt)
    st = Pack()
    for nm in ("lru_wa", "lru_wx"):
        w = np.asarray(inp[nm][0], np.float32)
        m = np.zeros((128, 4, 128), np.float32)
        for c in range(4):
            m[0:64, c, 0:64] = w[2 * c]
            m[64:128, c, 64:128] = w[2 * c + 1]
        st.add(nm, m)
    for nm in ("diff_lq1", "diff_lk1", "diff_lq2", "diff_lk2"):
        st.add(nm, rep(inp[nm][0]))
    st.add("ident", np.eye(128, dtype=np.float32))
    return pk, st


def build(nseq, S, pk_off, npar, st_off, nst, dbg=False):
    NG = S // G
    NKT = S // 128
    ntok = nseq * S
    nc = bass.Bass("TRN2", target_bir_lowering=False)
    x_d = nc.dram_tensor("x", [ntok, D], F32, kind="ExternalInput").ap()
    win_d = nc.dram_tensor("w_in", [D, 2560], F32, kind="ExternalInput").ap()
    wout_d = nc.dram_tensor("w_out", [D, D], F32, kind="ExternalInput").ap()
    wup_d = nc.dram_tensor("w_up", [D, 6144], F32, kind="ExternalInput").ap()
    wdn_d = nc.dram_tensor("w_down", [3072, D], F32, kind="ExternalInput").ap()
    par_d = nc.dram_tensor("params", [128, npar], F32, kind="ExternalInput").ap()
    stg_d = nc.dram_tensor("stage", [128, nst], F32, kind="ExternalInput").ap()
    out_d = nc.dram_tensor("out", [ntok, D], F32, kind="ExternalOutput").ap()
    NSLAB = 25
    wsl_d = nc.dram_tensor("wslab", [NSLAB, 128, 8, 512], BF16).ap()
    dbg_d = nc.dram_tensor("dbg", [8, 128, 4096], F32, kind="ExternalOutput").ap() if dbg else None

    with ExitStack() as st:
        S_ = Sched(nc, st)
        T = lambda name, shape, dt: st.enter_context(nc.sbuf_tensor(name, shape, dt))
        prm = T("prm", [128, npar], F32)
        ring = T("ring", [128, NRING, 8, 512], BF16)
        KT = T("KT", [128, 4, S], BF16)
        Va = T("Va", [128, NKT, 4, 130], BF16)
        xg = T("xg", [128, 2, NT, 1024], F32)
        hb = T("hb", [128, NT, 1024], BF16)
        hb2 = T("hb2", [128, NT, 1024], BF16)
        hT = T("hT", [128, 8, G], BF16)
        hT2 = T("hT2", [128, 8, G], BF16)
        qT = T("qT", [128, 4, G], BF16)
        mixT = T("mixT", [128, 8, G], BF16)
        aT = T("aT", [128, 24, G], BF16)
        gg = T("gg", [128, 4, G], BF16)
        bdw = T("bdw", [128, 2, 4, 128], BF16)
        ident = T("ident", [128, 128], BF16)
        sm = T("sm", [128, 96], F32)
        xc4 = T("xc4", [128, 4, G], F32)
        xcb = T("xcb", [128, G], BF16)
        thr = T("thr", [128, G], F32)
        thi = T("thi", [128, G], F32)
        la2 = T("la2", [128, G], F32)
        lh = T("lh", [128, G], F32)
        lhalo = T("lhalo", [128, 4, 4], F32)
        lstate = T("lstate", [128, 4], F32)
        PT = T("PT", [128, 2, G], BF16)
        sbn = T("sbn", [128, 3, 256], F32)
        acp = T("acp", [128, 8, 129], F32)
        t1 = T("t1", [128, 128], F32)
        ob = T("ob", [128, NT, 4, 128], BF16)
        facc = T("facc", [128, 2, G], F32)
        fhalo = T("fhalo", [128, 24, 2], F32)
        banks = [st.enter_context(nc.psum_tensor("pb%d" % i, [128, 512], F32)) for i in range(8)]
        block = st.enter_context(nc.Block())

        def pv(name):
            o, n = pk_off[name]
            return prm[:, o:o + n]

        def sv(name):
            o, n = st_off[name]
            return xg[:, 1, 0:2, :].rearrange("p a b -> p (a b)")[:, o:o + n]

        SM = {}
        smn = [0]

        def smalloc(name, n):
            SM[name] = (smn[0], n)
            smn[0] += n
            assert smn[0] <= 96

        def smv(name, a=0, b=None):
            o, n = SM[name]
            if b is None:
                b = n
            return sm[:, o + a:o + b]

        for name, n in [("lam", 1), ("neglam", 1), ("e1", 1), ("e2", 1), ("s1", 1), ("s2", 1), ("cneg", 4),
                        ("nba", 4), ("nbx", 4), ("c2neg", 4), ("dummy", 1), ("subg2", 1), ("mhalf", 1), ("half", 1), ("ss", 4), ("ms", 4), ("rstd", 4),
                        ("rs", 8), ("rs2n", 4), ("sso", 4), ("mso", 4), ("rstdo", 4), ("tmp4", 4), ("ssf", 4), ("msf", 4), ("rstdf", 4), ("ss2", 2)]:
            smalloc(name, n)

        b_prm = Buf("prm")
        b_ring = [Buf("ring%d" % i) for i in range(NRING)]
        b_slabd = [[Buf("slabd%d_%d" % (i, k)) for k in range(2)] for i in range(NSLAB)]
        b_x = [[Buf("x%d_%d" % (i, t)) for t in range(NT)] for i in range(2)]
        b_stage_l = b_x[1][0:2]
        b_hb = [Buf("hb%d" % t) for t in range(NT)]
        b_hb2 = [Buf("hb2_%d" % t) for t in range(NT)]
        b_hT = [Buf("hT%d" % k) for k in range(8)]
        b_hT2 = [Buf("hT2_%d" % k) for k in range(8)]
        b_acp = Buf("acp")
        b_qT = [Buf("qT%d" % h) for h in range(4)]
        b_KT = [[Buf("KT%d_%d" % (h, g)) for g in range(NG)] for h in range(4)]
        b_V = [Buf("V%d" % j) for j in range(NKT)]
        b_mix = [Buf("mix%d" % k) for k in range(8)]
        b_aT = [Buf("aT%d" % k) for k in range(24)]
        b_gg = [Buf("gg%d" % c) for c in range(4)]
        b_ps = [Buf("ps%d" % i) for i in range(8)]
        b_sm = {k: Buf("sm_" + k) for k in SM}
        b_bdw = Buf("bdw")
        b_ident = Buf("ident")
        b_xcb, b_thr, b_thi, b_la, b_la2, b_lu, b_lh = [Buf(n) for n in "xcb thr thi la la2 lu lh".split()]
        b_xc = [Buf("xc%d" % c) for c in range(4)]
        b_lhalo = [Buf("lhalo%d" % c) for c in range(4)]
        b_lstate = [Buf("lstate%d" % c) for c in range(4)]
        b_PT = [Buf("PT0"), Buf("PT1")]
        b_sbn = [Buf("sbn0"), Buf("sbn1"), Buf("sbn2")]
        b_t1 = Buf("t1")
        b_ob = [[Buf("ob%d_%d" % (i, h)) for h in range(4)] for i in range(NT)]
        b_facc = [Buf("facc0"), Buf("facc1")]
        b_fgl = [Buf("fgl0"), Buf("fgl1")]
        b_fhalo = [Buf("fhalo%d" % k) for k in range(24)]
        b_out = Buf("outd")

        op = S_.op
        out_toks = []

        def dump(slot, ap2d, n, bufs):
            if dbg_d is None:
                return
            out_toks.append(S_.dma("pool", dbg_d[slot, :, 0:n], ap2d, reads=bufs, writes=[Buf("dbgw")]))

        rot = [0]
        nbanks = [4]

        def tbank():
            i = rot[0] % nbanks[0]
            rot[0] = i + 1
            return i

        S_.dma("sp", prm[:], par_d, writes=[b_prm])
        S_.dma("sp", xg[:, 1, 0:2, :].rearrange("p a b -> p (a b)")[:, 0:nst], stg_d, writes=b_stage_l)

        op("pool", lambda e: e.memset(smv("mhalf"), -0.5), writes=[b_sm["mhalf"]])
        op("pool", lambda e: e.memset(smv("half"), 0.5), writes=[b_sm["half"]])
        op("pool", lambda e: e.memset(Va[:, :, :, 128:130], 1.0), writes=b_V)
        def cast_slab(si, parts):
            for pi, (c0, ncol, src) in enumerate(parts):
                S_.dma("pool", wsl_d[si, :, :, c0:c0 + ncol], src.rearrange("(kk p) c -> p kk c", p=128),
                       writes=[b_slabd[si][pi]])

        SL_IN, SL_OUT, SL_UP, SL_DN = 0, 5, 7, 19
        for i in range(5):
            cast_slab(SL_IN + i, [(0, 512, win_d[:, i * 512:(i + 1) * 512])])
        deferred_casts = []
        _DEFER = False
        for i in range(2):
            deferred_casts.append(lambda i=i: cast_slab(SL_OUT + i, [(0, 512, wout_d[:, i * 512:(i + 1) * 512])]))
        for s_ in range(6):
            deferred_casts.append(lambda s_=s_: cast_slab(SL_UP + s_, [(0, 512, wup_d[:, s_ * 512:(s_ + 1) * 512])]))
        for s_ in range(6):
            deferred_casts.append(lambda s_=s_: cast_slab(SL_UP + 6 + s_, [(0, 512, wup_d[:, 3072 + s_ * 512:3072 + (s_ + 1) * 512])]))
        for c2 in range(2):
            for ks in range(3):
                deferred_casts.append(lambda c2=c2, ks=ks: cast_slab(
                    SL_DN + c2 * 3 + ks, [(0, 512, wdn_d[ks * 1024:(ks + 1) * 1024, c2 * 512:(c2 + 1) * 512])]))

        if not _DEFER:
            for dc_ in deferred_casts:
                dc_()
            deferred_casts = []
        op("dve", lambda e: e.tensor_copy(out=ident[:], in_=sv("ident")), reads=b_stage_l, writes=[b_ident])
        op("dve", lambda e: e.tensor_copy(out=bdw[:, 0, :, :], in_=sv("lru_wa").rearrange("p (c m) -> p c m", c=4)),
           reads=b_stage_l, writes=[b_bdw])
        op("dve", lambda e: e.tensor_copy(out=bdw[:, 1, :, :], in_=sv("lru_wx").rearrange("p (c m) -> p c m", c=4)),
           reads=b_stage_l, writes=[b_bdw])
        op("dve", lambda e: e.tensor_tensor(out=t1[:, 0:64], in0=sv("diff_lq1"), in1=sv("diff_lk1"), op=ALU.mult),
           reads=b_stage_l, writes=[b_t1])
        op("dve", lambda e: e.tensor_reduce(out=smv("s1"), in_=t1[:, 0:64], axis=AX.X, op=ALU.add),
           reads=[b_t1], writes=[b_sm["s1"]])
        op("dve", lambda e: e.tensor_tensor(out=t1[:, 64:128], in0=sv("diff_lq2"), in1=sv("diff_lk2"), op=ALU.mult),
           reads=b_stage_l, writes=[b_t1])
        op("dve", lambda e: e.tensor_reduce(out=smv("s2"), in_=t1[:, 64:128], axis=AX.X, op=ALU.add),
           reads=[b_t1], writes=[b_sm["s2"]])
        op("act", lambda e: e.activation(out=smv("e1"), in_=smv("s1"), func=AF.Exp), reads=[b_sm["s1"]], writes=[b_sm["e1"]])
        op("act", lambda e: e.activation(out=smv("e2"), in_=smv("s2"), func=AF.Exp), reads=[b_sm["s2"]], writes=[b_sm["e2"]])
        op("dve", lambda e: e.tensor_tensor(out=smv("lam"), in0=smv("e1"), in1=smv("e2"), op=ALU.subtract),
           reads=[b_sm["e1"], b_sm["e2"]], writes=[b_sm["lam"]])
        op("dve", lambda e: e.tensor_scalar(out=smv("neglam"), in0=smv("lam"), scalar1=-1.0, scalar2=-LAMBDA_INIT,
                                            op0=ALU.mult, op1=ALU.add),
           reads=[b_sm["lam"]], writes=[b_sm["neglam"]])
        op("act", lambda e: e.activation(out=smv("tmp4"), in_=pv("llam"), func=AF.Exp, scale=-1.0),
           reads=[b_prm], writes=[b_sm["tmp4"]])
        op("act", lambda e: e.activation(out=smv("tmp4"), in_=smv("tmp4"), func=AF.Ln, bias=1.0, scale=1.0),
           reads=[b_sm["tmp4"]], writes=[b_sm["tmp4"]])
        op("dve", lambda e: e.tensor_scalar(out=smv("cneg"), in0=smv("tmp4"), scalar1=-8.0, scalar2=None, op0=ALU.mult),
           reads=[b_sm["tmp4"]], writes=[b_sm["cneg"]])
        op("dve", lambda e: e.tensor_scalar(out=smv("c2neg"), in0=smv("tmp4"), scalar1=-16.0, scalar2=None, op0=ALU.mult),
           reads=[b_sm["tmp4"]], writes=[b_sm["c2neg"]])
        op("dve", lambda e: e.tensor_scalar(out=smv("nba"), in0=pv("lba"), scalar1=-1.0, scalar2=None, op0=ALU.mult),
           reads=[b_prm], writes=[b_sm["nba"]])
        op("dve", lambda e: e.tensor_scalar(out=smv("nbx"), in0=pv("lbx"), scalar1=-1.0, scalar2=None, op0=ALU.mult),
           reads=[b_prm], writes=[b_sm["nbx"]])
        op("dve", lambda e: e.tensor_scalar(out=smv("subg2"), in0=pv("subg"), scalar1=1.0 - LAMBDA_INIT, scalar2=None,
                                            op0=ALU.mult),
           reads=[b_prm], writes=[b_sm["subg2"]])

        ring_i = [0]

        def load_slab(si):
            r = ring_i[0]
            ring_i[0] = (r + 1) % NRING
            S_.dma("sp", ring[:, r, :, :], wsl_d[si], reads=b_slabd[si], writes=[b_ring[r]])
            return r

        def load_x(seq, g, xi):
            tok0 = seq * S + g * G
            for t in range(NT):
                S_.dma("sp", xg[:, xi, t, :], x_d[tok0 + t * 128:tok0 + (t + 1) * 128, :], writes=[b_x[xi][t]])

        evac_flip = [0]

        def evac_eng():
            evac_flip[0] ^= 1
            return "act" if evac_flip[0] else "dve"

        def rstd_from_ss(ssn, msn, rsn, n, inv_n, eps, use_act=False):
            op("dve", lambda e: e.tensor_scalar(out=smv(msn, 0, n), in0=smv(ssn, 0, n), scalar1=inv_n, scalar2=eps,
                                                op0=ALU.mult, op1=ALU.add),
               reads=[b_sm[ssn]], writes=[b_sm[msn]])
            if use_act:
                op("act", lambda e: e.activation(out=smv(msn, 0, n), in_=smv(msn, 0, n), func=AF.Ln),
                   reads=[b_sm[msn]], writes=[b_sm[msn]])
                op("act", lambda e: e.activation(out=smv(rsn, 0, n), in_=smv(msn, 0, n), func=AF.Exp, scale=-0.5),
                   reads=[b_sm[msn]], writes=[b_sm[rsn]])
                return
            op("pool", lambda e: e.tensor_tensor(out=smv(rsn, 0, n), in0=smv(msn, 0, n),
                                                 in1=smv("mhalf").to_broadcast([128, n]), op=ALU.pow),
               reads=[b_sm[msn], b_sm["mhalf"]], writes=[b_sm[rsn]])

        def norm_chain(xi, hb=hb, b_hb=b_hb):
            for t in range(NT):
                if t < 2:
                    op("act", lambda e, t=t: e.activation(out=hb[:, t, :], in_=xg[:, xi, t, :], func=AF.Square,
                                                          accum_out=smv("ss", t, t + 1)),
                       reads=[b_x[xi][t]], writes=[b_hb[t], b_sm["ss"]])
                else:
                    op("dve", lambda e, t=t: e.scalar_tensor_tensor(out=hb[:, t, :], in0=xg[:, xi, t, :], scalar=1.0,
                                                                    in1=xg[:, xi, t, :], op0=ALU.mult, op1=ALU.mult,
                                                                    accum_out=smv("ss2", t - 2, t - 1)),
                       reads=[b_x[xi][t]], writes=[b_hb[t], b_sm["ss2"]])
            op("dve", lambda e: e.tensor_scalar(out=smv("ms", 0, 2), in0=smv("ss", 0, 2), scalar1=1.0 / D, scalar2=EPS,
                                                op0=ALU.mult, op1=ALU.add),
               reads=[b_sm["ss"]], writes=[b_sm["ms"]])
            op("dve", lambda e: e.tensor_scalar(out=smv("ms", 2, 4), in0=smv("ss2", 0, 2), scalar1=1.0 / D, scalar2=EPS,
                                                op0=ALU.mult, op1=ALU.add),
               reads=[b_sm["ss2"]], writes=[b_sm["ms"]])
            op("act", lambda e: e.activation(out=smv("ms"), in_=smv("ms"), func=AF.Ln), reads=[b_sm["ms"]],
               writes=[b_sm["ms"]])
            op("act", lambda e: e.activation(out=smv("rstd"), in_=smv("ms"), func=AF.Exp, scale=-0.5),
               reads=[b_sm["ms"]], writes=[b_sm["rstd"]])
            for t in range(NT):
                if t < 2:
                    op("act", lambda e, t=t: e.activation(out=hb[:, t, :], in_=xg[:, xi, t, :], func=AF.Identity,
                                                          scale=smv("rstd", t, t + 1)),
                       reads=[b_x[xi][t], b_sm["rstd"]], writes=[b_hb[t]])
                else:
                    op("dve", lambda e, t=t: e.tensor_scalar(out=hb[:, t, :], in0=xg[:, xi, t, :],
                                                             scalar1=smv("rstd", t, t + 1), scalar2=None, op0=ALU.mult),
                       reads=[b_x[xi][t], b_sm["rstd"]], writes=[b_hb[t]])
        def norm_tr_P(k, bi, hb=hb, b_hb=b_hb):
            pbf = banks[bi][:].bitcast(BF16)

            def f(e):
                for t in range(NT):
                    last = e.transpose(pbf[:, t * 128:(t + 1) * 128], hb[:, t, k * 128:(k + 1) * 128], ident[:])
                return last

            op("pe", f, reads=b_hb + [b_ident], writes=[b_ps[bi]])

        def norm_tr_C(k, bi, gname, dstT, b_dst):
            pbf = banks[bi][:].bitcast(BF16)
            gcol = pv(gname)[:, k:k + 1]
            if evac_eng() == "act":
                op("act", lambda e: e.activation(out=dstT[:, k, :], in_=pbf[:, 0:G], func=AF.Identity, scale=gcol),
                   reads=[b_ps[bi], b_prm], writes=[b_dst[k]])
            else:
                op("dve", lambda e: e.tensor_scalar(out=dstT[:, k, :], in0=pbf[:, 0:G], scalar1=gcol, scalar2=None,
                                                    op0=ALU.mult),
                   reads=[b_ps[bi], b_prm], writes=[b_dst[k]])

        def norm_transposes(gname, dstT, b_dst, hb=hb, b_hb=b_hb):
            for k in range(8):
                bi = tbank()
                norm_tr_P(k, bi, hb, b_hb)
                norm_tr_C(k, bi, gname, dstT, b_dst)

        def mm_fm(bi, r, c, src, b_src):
            def f(e):
                for kk in range(8):
                    last = e.matmul(banks[bi][:], lhsT=ring[:, r, kk, c * 128:(c + 1) * 128], rhs=src[:, kk, :],
                                    start=(kk == 0), stop=(kk == 7))
                return last

            op("pe", f, reads=[b_ring[r]] + list(b_src), writes=[b_ps[bi]])

        def conv_from_psum(bi, acc, b_acc, w_ap, b_ap, ntap, halo, b_halo, first, last_out=None, b_last=None):
            ps = banks[bi]
            nh = ntap - 1
            op("act", lambda e: e.activation(out=acc, in_=ps[:], func=AF.Identity, scale=w_ap[:, nh:nh + 1], bias=b_ap),
               reads=[b_ps[bi], b_prm], writes=[b_acc])
            for sft in range(1, ntap):
                j = nh - sft
                if last_out is not None and sft == nh:
                    op("dve", lambda e, sft=sft, j=j: e.scalar_tensor_tensor(out=last_out[:, sft:G], in0=ps[:, 0:G - sft],
                                                                             scalar=w_ap[:, j:j + 1], in1=acc[:, sft:G],
                                                                             op0=ALU.mult, op1=ALU.add),
                       reads=[b_ps[bi], b_prm, b_acc], writes=[b_last])
                else:
                    op("dve", lambda e, sft=sft, j=j: e.scalar_tensor_tensor(out=acc[:, sft:G], in0=ps[:, 0:G - sft],
                                                                             scalar=w_ap[:, j:j + 1], in1=acc[:, sft:G],
                                                                             op0=ALU.mult, op1=ALU.add),
                       reads=[b_ps[bi], b_prm], writes=[b_acc])
            if not first:
                for sft in range(1, ntap):
                    j = nh - sft
                    op("dve", lambda e, sft=sft, j=j: e.scalar_tensor_tensor(out=acc[:, 0:sft], in0=halo[:, nh - sft:nh],
                                                                             scalar=w_ap[:, j:j + 1], in1=acc[:, 0:sft],
                                                                             op0=ALU.mult, op1=ALU.add),
                       reads=[b_halo, b_prm], writes=[b_acc])
            if last_out is not None:
                op("dve", lambda e: e.tensor_copy(out=last_out[:, 0:nh], in_=acc[:, 0:nh]),
                   reads=[b_acc], writes=[b_last])
            op("dve", lambda e: e.tensor_copy(out=halo[:, 0:nh], in_=ps[:, G - nh:G]),
               reads=[b_ps[bi]], writes=[b_halo])

        ngrp = nseq * NG
        groups = [(sq, g) for sq in range(nseq) for g in range(NG)]

        def stage_A(gi):
            seq, g = groups[gi]
            xi = gi % 2
            first = (g == 0)
            nbanks[0] = 8
            if gi == 0:
                norm_chain(xi, hb2, b_hb2)
            norm_transposes("g1T", hT, b_hT, hb2, b_hb2)
            if gi > 0:
                norm_transposes("g2T", hT2, b_hT2)
            if gi == 0:
                dump(0, hT[:, :, :].rearrange("p a b -> p (a b)"), 4096, b_hT)
            r = load_slab(SL_IN + 1)
            for c in range(4):
                bg = tbank()
                mm_fm(bg, r, c, hT, b_hT)
                op("act", lambda e, c=c, bg=bg: e.activation(out=gg[:, c, :], in_=banks[bg][:], func=AF.Gelu_apprx_tanh),
                   reads=[b_ps[bg]], writes=[b_gg[c]])
            op("act", lambda e: e.activation(out=smv("dummy"), in_=smv("half"), func=AF.Ln),
               reads=[b_sm["half"]], writes=[b_sm["dummy"]])
            r = load_slab(SL_IN + 0)
            for c in range(4):
                bi = tbank()
                mm_fm(bi, r, c, hT, b_hT)
                w_ap = pv("lcw")[:, c * 4:(c + 1) * 4]
                conv_from_psum(bi, xc4[:, c, :], b_xc[c], w_ap, pv("lcb")[:, c:c + 1], 4, lhalo[:, c, :], b_lhalo[c], first)
            r = load_slab(SL_IN + 2)
            for h in range(4):
                bi = tbank()
                mm_fm(bi, r, h, hT, b_hT)
                if evac_eng() == "act":
                    op("act", lambda e, h=h, bi=bi: e.activation(out=qT[:, h, :], in_=banks[bi][:], func=AF.Copy,
                                                                 scale=0.125),
                       reads=[b_ps[bi]], writes=[b_qT[h]])
                else:
                    op("dve", lambda e, h=h, bi=bi: e.tensor_scalar(out=qT[:, h, :], in0=banks[bi][:], scalar1=0.125,
                                                                    scalar2=None, op0=ALU.mult),
                       reads=[b_ps[bi]], writes=[b_qT[h]])
            r = load_slab(SL_IN + 3)
            for h in range(4):
                bi = tbank()
                mm_fm(bi, r, h, hT, b_hT)
                dst = KT[:, h, g * G:(g + 1) * G]
                if evac_eng() == "act":
                    op("act", lambda e, dst=dst, bi=bi: e.activation(out=dst, in_=banks[bi][:], func=AF.Copy),
                       reads=[b_ps[bi]], writes=[b_KT[h][g]])
                else:
                    op("dve", lambda e, dst=dst, bi=bi: e.tensor_copy(out=dst, in_=banks[bi][:]),
                       reads=[b_ps[bi]], writes=[b_KT[h][g]])
            r = load_slab(SL_IN + 4)
            for t in range(NT):
                bi = tbank()
                jt = g * NT + t

                def f(e, t=t, bi=bi, r=r):
                    for kk in range(8):
                        last = e.matmul(banks[bi][:], lhsT=hT[:, kk, t * 128:(t + 1) * 128], rhs=ring[:, r, kk, :],
                                        start=(kk == 0), stop=(kk == 7))
                    return last

                op("pe", f, reads=[b_ring[r]] + b_hT, writes=[b_ps[bi]])
                src = banks[bi][:].rearrange("p (h e) -> p h e", h=4)
                dst = Va[:, jt, :, 0:128]
                if evac_eng() == "act":
                    op("act", lambda e, dst=dst, src=src: e.activation(out=dst, in_=src, func=AF.Copy),
                       reads=[b_ps[bi]], writes=[b_V[jt]])
                else:
                    op("dve", lambda e, dst=dst, src=src: e.tensor_copy(out=dst, in_=src),
                       reads=[b_ps[bi]], writes=[b_V[jt]])

        ACC0 = 2

        def acc_region(il, c):
            rr = c * 4 + il
            return ACC0 + rr // 3, (rr % 3) * 130, rr

        def make_B(gi):
            seq, g = groups[gi]
            first = (g == 0)
            nj = g * NT + NT
            info = {}
            aux = "dve" if gi == 0 else "pool"

            def lru_P(c, which, bk):
                if which == 0:
                    op(aux, lambda e: e.tensor_copy(out=xcb[:], in_=xc4[:, c, :]), reads=[b_xc[c]], writes=[b_xcb])
                op("pe", lambda e: e.matmul(banks[bk][:], lhsT=bdw[:, which, c, :], rhs=xcb[:], start=True, stop=True),
                   reads=[b_bdw, b_xcb], writes=[b_ps[bk]])

            def lru_C(c, which, bk):
                if which == 0:
                    op("act", lambda e: e.activation(out=thr[:], in_=banks[bk][:], func=AF.Exp, scale=-1.0,
                                                     bias=smv("nba", c, c + 1)),
                       reads=[b_ps[bk], b_sm["nba"]], writes=[b_thr])
                    op("act", lambda e: e.activation(out=thr[:], in_=thr[:], func=AF.Ln, bias=1.0, scale=1.0),
                       reads=[b_thr], writes=[b_thr])
                    op("act", lambda e: e.activation(out=thr[:], in_=thr[:], func=AF.Exp, scale=-1.0),
                       reads=[b_thr], writes=[b_thr])
                    op("act", lambda e: e.activation(out=la2[:], in_=thr[:], func=AF.Exp, scale=smv("c2neg", c, c + 1)),
                       reads=[b_thr, b_sm["c2neg"]], writes=[b_la2])
                    op("act", lambda e: e.activation(out=thr[:], in_=thr[:], func=AF.Exp, scale=smv("cneg", c, c + 1)),
                       reads=[b_thr, b_sm["cneg"]], writes=[b_thr])
                    op("act", lambda e: e.activation(out=la2[:], in_=la2[:], func=AF.Ln, bias=1.0, scale=-1.0),
                       reads=[b_la2], writes=[b_la2])
                    op("act", lambda e: e.activation(out=la2[:], in_=la2[:], func=AF.Exp, scale=0.5),
                       reads=[b_la2], writes=[b_la2])
                    return
                op("act", lambda e: e.activation(out=thi[:], in_=banks[bk][:], func=AF.Exp, scale=-1.0,
                                                 bias=smv("nbx", c, c + 1)),
                   reads=[b_ps[bk], b_sm["nbx"]], writes=[b_thi])
                op("act", lambda e: e.activation(out=thi[:], in_=thi[:], func=AF.Ln, bias=1.0, scale=1.0),
                   reads=[b_thi], writes=[b_thi])
                op("act", lambda e: e.activation(out=thi[:], in_=thi[:], func=AF.Exp, scale=-1.0),
                   reads=[b_thi], writes=[b_thi])
                op(aux, lambda e: e.tensor_tensor(out=thi[:], in0=thi[:], in1=xc4[:, c, :], op=ALU.mult),
                   reads=[b_thi, b_xc[c]], writes=[b_thi])
                op("dve", lambda e: e.tensor_tensor(out=thi[:], in0=thi[:], in1=la2[:], op=ALU.mult),
                   reads=[b_thi, b_la2], writes=[b_thi])
                init = 0.0 if first else lstate[:, c:c + 1]
                op("dve", lambda e: e.tensor_tensor_scan(out=lh[:], data0=thr[:], data1=thi[:], initial=init,
                                                         op0=ALU.mult, op1=ALU.add),
                   reads=[b_thr, b_thi, b_lstate[c]], writes=[b_lh])
                op("dve", lambda e: e.tensor_copy(out=lstate[:, c:c + 1], in_=lh[:, G - 1:G]),
                   reads=[b_lh], writes=[b_lstate[c]])
                op(aux, lambda e: e.tensor_tensor(out=mixT[:, c, :], in0=gg[:, c, :], in1=lh[:], op=ALU.mult),
                   reads=[b_gg[c], b_lh], writes=[b_mix[c]])

            def emit_qk(key, bk, p):
                h, j, c = key
                i0 = max(0, j - g * NT)
                N = (NT - i0) * 128
                q0 = i0 * 128
                gk = j // NT
                op("pe", lambda e: e.matmul(banks[bk][:, 0:N], lhsT=KT[c * 64:(c + 1) * 64, h, j * 128:(j + 1) * 128],
                                            rhs=qT[c * 64:(c + 1) * 64, h, q0:G], start=True, stop=True),
                   reads=[b_KT[h][gk], b_qT[h]], writes=[b_ps[bk]])
                d0 = g * NT + i0 - j
                if d0 == 0:
                    boff, wn = 0, (256 if i0 < NT - 1 else 128)
                elif d0 == 1:
                    boff, wn = 128, 128
                else:
                    boff, wn = 0, 0
                if wn:
                    bo, nn = pk_off["biasT"]
                    bias_ap = prm[:, bo + h * 256 + boff: bo + h * 256 + boff + wn]
                    op("dve", lambda e: e.tensor_tensor(out=sbn[:, p, 0:wn], in0=banks[bk][:, 0:wn], in1=bias_ap,
                                                        op=ALU.add),
                       reads=[b_ps[bk], b_prm], writes=[b_sbn[p]])
                info[key] = (i0, N, wn)

            def emit_softmax_pv(key, bk, p, p3):
                h, j, c = key
                i0, N, wn = info.pop(key)
                if wn:
                    op("act", lambda e: e.activation(out=PT[:, p, 0:wn], in_=sbn[:, p3, 0:wn], func=AF.Exp),
                       reads=[b_sbn[p3]], writes=[b_PT[p]])
                if N > wn:
                    fo, fn_ = pk_off["farb"]
                    op("act", lambda e: e.activation(out=PT[:, p, wn:N], in_=banks[bk][:, wn:N], func=AF.Exp,
                                                     bias=prm[:, fo + h:fo + h + 1]),
                       reads=[b_ps[bk], b_prm], writes=[b_PT[p]])

                def f(e):
                    last = None
                    for il in range(i0, NT):
                        qi = g * NT + il
                        abk, off, rr = acc_region(il, c)
                        last = e.matmul(banks[abk][:, off:off + 129],
                                        lhsT=PT[:, p, (il - i0) * 128:(il - i0 + 1) * 128],
                                        rhs=Va[:, j, h, 0:129], start=(j == 0 and rr % 3 == 0), stop=(j == qi),
                                        skip_group_check=True)
                    return last

                wb = sorted(set(acc_region(il, c)[0] for il in range(i0, NT)))
                op("pe", f, reads=[b_PT[p], b_V[j]], writes=[b_ps[k_] for k_ in wb])

            def emit_norm(h):
                for abk, r0, nr in ((ACC0, 0, 3), (ACC0 + 1, 3, 3), (ACC0 + 2, 6, 2)):
                    src = banks[abk][:, 0:nr * 130].rearrange("p (r w) -> p r w", w=130)[:, :, 0:129]
                    op("dve", lambda e, src=src, r0=r0, nr=nr: e.tensor_copy(out=acp[:, r0:r0 + nr, :], in_=src),
                       reads=[b_ps[abk]], writes=[b_acp])
                op("dve", lambda e: e.reciprocal(out=smv("rs").rearrange("p (r o) -> p r o", o=1),
                                                 in_=acp[:, :, 128:129]),
                   reads=[b_acp], writes=[b_sm["rs"]])
                op("dve", lambda e: e.tensor_scalar(out=smv("rs2n"), in0=smv("rs", 4, 8), scalar1=smv("neglam"),
                                                    scalar2=None, op0=ALU.mult),
                   reads=[b_sm["rs"], b_sm["neglam"]], writes=[b_sm["rs2n"]])
                for il in range(NT):
                    o1 = acp[:, il, 0:128]
                    o2 = acp[:, 4 + il, 0:128]
                    op("dve", lambda e, il=il, o1=o1: e.tensor_scalar(out=o1, in0=o1, scalar1=smv("rs", il, il + 1),
                                                                      scalar2=None, op0=ALU.mult),
                       reads=[b_acp, b_sm["rs"]], writes=[b_acp])
                    op("dve", lambda e, il=il, o1=o1, o2=o2: e.scalar_tensor_tensor(out=o1, in0=o2,
                                                                                    scalar=smv("rs2n", il, il + 1),
                                                                                    in1=o1, op0=ALU.mult, op1=ALU.add),
                       reads=[b_acp, b_sm["rs2n"]], writes=[b_acp])
                    op("dve", lambda e, il=il, o1=o1, o2=o2: e.scalar_tensor_tensor(out=o2, in0=o1, scalar=1.0, in1=o1,
                                                                                    op0=ALU.mult, op1=ALU.mult,
                                                                                    accum_out=smv("sso", il, il + 1)),
                       reads=[b_acp], writes=[b_acp, b_sm["sso"]])
                rstd_from_ss("sso", "mso", "rstdo", NT, 1.0 / 128.0, SUBLN_EPS, use_act=(gi == 0))
                for il in range(NT):
                    op("dve", lambda e, il=il: e.tensor_scalar(out=ob[:, il, h, :], in0=acp[:, il, 0:128],
                                                               scalar1=smv("rstdo", il, il + 1), scalar2=None,
                                                               op0=ALU.mult),
                       reads=[b_acp, b_sm["rstdo"]], writes=[b_ob[il][h]])
            users = []
            for h in range(4):
                users.append(("lru", h, 0, True))
                users.append(("lru", h, 1, True))
                for j in range(nj):
                    for c in range(2):
                        users.append(("qk", h, (h, j, c), True))
                users.append(("norm", h, None, True))

            def emitP(n, bk):
                kind, h, key, _ = users[n]
                if kind == "lru":
                    lru_P(h, key, bk)
                elif kind == "qk":
                    emit_qk(key, bk, n % 3)


            def emitC(n, bk):
                kind, h, key, _ = users[n]
                if kind == "lru":
                    lru_C(h, key, bk)
                elif kind == "qk":
                    emit_softmax_pv(key, bk, n % 2, n % 3)
                else:
                    emit_norm(h)

            def weight(n):
                kind, h, key, _ = users[n]
                if kind == "lru":
                    return 4.0 if key == 0 else 2.5
                if kind == "norm":
                    return 3.0
                hh, j, c = key
                return 1.6 if j >= g * NT - 1 else 1.0

            return [(lambda bk, n=n: emitP(n, bk), lambda bk, n=n: emitC(n, bk), users[n][3], weight(n))
                    for n in range(len(users))]

        def make_C(gi):
            xi = gi % 2
            ul = []
            nop1 = lambda bk: None

            def tr_P(h, bk):
                pbf = banks[bk][:].bitcast(BF16)

                def f(e):
                    for il in range(NT):
                        last = e.transpose(pbf[:, il * 128:(il + 1) * 128], ob[:, il, h, :], ident[:])
                    return last

                op("pe", f, reads=[b_ob[il][h] for il in range(NT)] + [b_ident], writes=[b_ps[bk]])

            def tr_C(h, bk):
                pbf = banks[bk][:].bitcast(BF16)
                op("dve", lambda e: e.tensor_scalar(out=mixT[:, 4 + h, :], in0=pbf[:, 0:G], scalar1=smv("subg2"),
                                                    scalar2=None, op0=ALU.mult),
                   reads=[b_ps[bk], b_sm["subg2"]], writes=[b_mix[4 + h]])

            if gi + 1 < ngrp:
                ul.append((nop1, lambda bk: norm_chain(1 - xi, hb2, b_hb2), True))
            for h in range(4):
                ul.append((lambda bk, h=h: tr_P(h, bk), lambda bk, h=h: tr_C(h, bk), True))
            cur = {}

            def wo_P(c2, t, bk):
                if t == 0:
                    cur["r"] = load_slab(SL_OUT + c2)
                r = cur["r"]

                def f(e):
                    for kk in range(8):
                        last = e.matmul(banks[bk][:], lhsT=mixT[:, kk, t * 128:(t + 1) * 128], rhs=ring[:, r, kk, :],
                                        start=(kk == 0), stop=(kk == 7))
                    return last

                op("pe", f, reads=[b_ring[r]] + b_mix, writes=[b_ps[bk]])

            def wo_C(c2, t, bk):
                xs = xg[:, xi, t, c2 * 512:(c2 + 1) * 512]
                op("dve", lambda e: e.tensor_tensor(out=xs, in0=banks[bk][:], in1=xs, op=ALU.add),
                   reads=[b_ps[bk], b_x[xi][t]], writes=[b_x[xi][t]])

            for c2 in range(2):
                for t in range(NT):
                    ul.append((lambda bk, c2=c2, t=t: wo_P(c2, t, bk), lambda bk, c2=c2, t=t: wo_C(c2, t, bk),
                               not (c2 == 0 and t == 0)))
            ul.append((nop1, lambda bk: norm_chain(xi), True))
            return ul

        def ffn_units(gi, up_banks, acc_banks):
            seq, g = groups[gi]
            xi = gi % 2
            tok0 = seq * S + g * G
            first = (g == 0)
            units = []
            rot_u = [0]

            def ubank():
                bk = up_banks[rot_u[0] % len(up_banks)]
                rot_u[0] += 1
                return bk

            cur = {}

            def gate_pe(fc):
                if fc % 4 == 0:
                    cur["r"] = load_slab(SL_UP + fc // 4)
                r = cur["r"]
                bG = ubank()
                cur[("g", fc)] = bG
                mm_fm(bG, r, fc % 4, hT2, b_hT2)

            def gate_rest(fc):
                bG = cur.pop(("g", fc))
                cc = fc % 2
                w_ap = pv("fcw")[:, fc * 3:(fc + 1) * 3]
                conv_from_psum(bG, facc[:, cc, :], b_facc[cc], w_ap, pv("fcb")[:, fc:fc + 1], 3, fhalo[:, fc, :],
                               b_fhalo[fc], first, last_out=aT[:, fc, :], b_last=b_aT[fc])

            def unit_gelu(half):
                for k4 in range(3 * half, 3 * half + 3):
                    op("act", lambda e, k4=k4: e.activation(out=aT[:, 4 * k4:4 * k4 + 4, :], in_=aT[:, 4 * k4:4 * k4 + 4, :],
                                                            func=AF.Gelu_apprx_tanh),
                       reads=b_aT[4 * k4:4 * k4 + 4], writes=b_aT[4 * k4:4 * k4 + 4])
                op("act", lambda e: e.activation(out=smv("dummy"), in_=smv("half"), func=AF.Ln),
                   reads=[b_sm["half"]], writes=[b_sm["dummy"]])

            def val_pe(fc):
                if fc % 4 == 0:
                    cur["r"] = load_slab(SL_UP + 6 + fc // 4)
                r = cur["r"]
                bV = ubank()
                cur[("v", fc)] = bV
                mm_fm(bV, r, fc % 4, hT2, b_hT2)

            def val_rest(fc):
                bV = cur.pop(("v", fc))
                op("dve", lambda e: e.tensor_tensor(out=aT[:, fc, :], in0=banks[bV][:], in1=aT[:, fc, :], op=ALU.mult),
                   reads=[b_ps[bV], b_aT[fc]], writes=[b_aT[fc]])

            def unit_down(c2, ks, ts, accs):
                r = load_slab(SL_DN + c2 * 3 + ks)
                for t, bk in zip(ts, accs):
                    def f(e, t=t, bk=bk):
                        for kk in range(8):
                            last = e.matmul(banks[bk][:], lhsT=aT[:, ks * 8 + kk, t * 128:(t + 1) * 128],
                                            rhs=ring[:, r, kk, :], start=(ks == 0 and kk == 0),
                                            stop=(ks == 2 and kk == 7))
                        return last

                    op("pe", f, reads=[b_ring[r]] + b_aT[ks * 8:(ks + 1) * 8], writes=[b_ps[bk]])

            def unit_down_evac(c2, ts, accs):
                for t, bk in zip(ts, accs):
                    xs = xg[:, xi, t, c2 * 512:(c2 + 1) * 512]
                    op("dve", lambda e, xs=xs, bk=bk: e.tensor_tensor(out=xs, in0=banks[bk][:], in1=xs, op=ALU.add),
                       reads=[b_ps[bk], b_x[xi][t]], writes=[b_x[xi][t]])

            def unit_final():
                jk = aT[:, 0:2, :].rearrange("p a b -> p (a b)")
                for t in range(NT):
                    op("act", lambda e, t=t: e.activation(out=jk, in_=xg[:, xi, t, :], func=AF.Square,
                                                          accum_out=smv("ssf", t, t + 1)),
                       reads=[b_x[xi][t]], writes=[b_aT[0], b_aT[1], b_sm["ssf"]])
                rstd_from_ss("ssf", "msf", "rstdf", NT, 1.0 / D, EPS)
                for t in range(NT):
                    op("dve", lambda e, t=t: e.scalar_tensor_tensor(out=xg[:, xi, t, :], in0=xg[:, xi, t, :],
                                                                    scalar=smv("rstdf", t, t + 1), in1=pv("gfb"),
                                                                    op0=ALU.mult, op1=ALU.mult),
                       reads=[b_x[xi][t], b_sm["rstdf"], b_prm], writes=[b_x[xi][t]])
                    tk = S_.dma("sp", out_d[tok0 + t * 128:tok0 + (t + 1) * 128, :], xg[:, xi, t, :],
                                reads=[b_x[xi][t]], writes=[b_out])
                    out_toks.append(tk)
                if gi + 2 < ngrp:
                    load_x(groups[gi + 2][0], groups[gi + 2][1], xi)

            nop = lambda: None
            for half in range(2):
                for fc in range(12 * half, 12 * half + 12):
                    units.append((lambda fc=fc: gate_pe(fc), lambda fc=fc: gate_rest(fc), True))
                units.append((nop, lambda half=half: unit_gelu(half), False, 3))
                for fc in range(12 * half, 12 * half + 12):
                    units.append((lambda fc=fc: val_pe(fc), lambda fc=fc: val_rest(fc), True))
            if gi == DBG_G:
                units.append((nop, lambda: dump(3, aT[:, 0:8, :].rearrange("p a b -> p (a b)"), 4096, b_aT[0:8]), False))
            na = len(acc_banks)
            passes = [list(range(NT))[i:i + na] for i in range(0, NT, na)]
            for c2 in range(2):
                for ts in passes:
                    accs = acc_banks[:len(ts)]
                    for ks in range(3):
                        units.append((lambda c2=c2, ks=ks, ts=ts, accs=accs: unit_down(c2, ks, ts, accs), nop, False))
                    units.append((nop, lambda c2=c2, ts=ts, accs=accs: unit_down_evac(c2, ts, accs), False))
            units.append((nop, unit_final, False))
            return units

        PAIR_A = (0, 1)
        PAIR_B = (5, 6)

        def run_window(B, fu, extra=None, n_fill=None, SLOTS=(0, 1), DEPTH=1):
            extra = list(extra or [])
            nF = len(fu)
            fi = 0
            if B is None:
                for pe_, rest_, hoist_ in [u_[:3] for u_ in fu]:
                    pe_()
                    rest_()
                return
            users = B
            nU = len(users)
            st_ = {"pi": 0}
            NS = len(SLOTS)
            cumw = [0.0]
            for u_ in users:
                cumw.append(cumw[-1] + (u_[3] if len(u_) > 3 else 1.0))

            T_ = nU + 8
            if n_fill and NS == 2:
                T_ = n_fill + 2 + ((n_fill + 2) % 2)

            def slot(m):
                return SLOTS[m % NS] if m < T_ else (5, 0, 1)[(m - T_) % 3]

            def issue(limit, done):
                while st_["pi"] < nU and st_["pi"] <= limit:
                    m = st_["pi"]
                    if users[m][2] or m - 1 <= done:
                        users[m][0](slot(m))
                        st_["pi"] += 1
                    else:
                        break

            for cur in range(nU):
                dpt = 2 if cur >= T_ - 2 else DEPTH
                issue(cur + dpt, cur - 1)
                if extra:
                    extra.pop(0)()
                nfu = n_fill if n_fill else nU
                if cur + 1 >= nfu:
                    target = nF
                else:
                    target = int(math.ceil(cumw[cur + 1] / cumw[nfu] * nF - 1e-9))
                if any(len(u_) > 3 for u_ in fu[fi:target]):
                    target = min(nF, target + 3)
                batch = fu[fi:target]
                fi = max(fi, target)
                deferred = []
                for pe_, rest_, hoist_ in [u_[:3] for u_ in batch]:
                    if hoist_ and len(deferred) < 3:
                        pe_()
                        deferred.append(rest_)
                    else:
                        for r_ in deferred:
                            r_()
                        deferred = []
                        pe_()
                        if hoist_:
                            deferred.append(rest_)
                        else:
                            rest_()
                users[cur][1](slot(cur))
                for r_ in deferred:
                    r_()
                issue(cur + dpt, cur)
            while fi < nF:
                fu[fi][0]()
                fu[fi][1]()
                fi += 1
            for u in extra:
                u()

        load_x(groups[0][0], groups[0][1], 0)
        if ngrp > 1:
            load_x(groups[1][0], groups[1][1], 1)
        for gi in range(ngrp):
            stage_A(gi)
            fu = ffn_units(gi - 1, [5, 6, 7], [5, 6, 7]) if gi > 0 else []
            ub = make_B(gi)
            if gi == 0:
                run_window(ub + make_C(gi), fu, extra=deferred_casts, SLOTS=(0, 1, 5), DEPTH=2)
            else:
                run_window(ub + make_C(gi), fu, n_fill=max(1, len(ub) - 16))
        nbanks[0] = 8
        norm_transposes("g2T", hT2, b_hT2)
        run_window(None, ffn_units(ngrp - 1, [0, 1, 2, 3], [4, 5, 6, 7]))
        S_.wait_all("sp", out_toks)
        S_.emit(block)
    return nc


_CACHE = {}


def _run(inputs, nseq, S, ncores, dbg=False):
    pk, stp = pack_params(inputs)
    params = pk.build()
    stage = stp.build()
    key = (nseq, S)
    if key not in _CACHE:
        _CACHE[key] = build(nseq, S, pk.off, pk.n, stp.off, stp.n, dbg=dbg)
    nc = _CACHE[key]
    x = np.ascontiguousarray(np.asarray(inputs["x"], np.float32))
    f32c = lambda a: np.ascontiguousarray(np.asarray(a, np.float32))
    w_in = f32c(inputs["w_in"][0])
    w_out = f32c(inputs["w_out"][0])
    w_up = f32c(inputs["ffn_w_up"][0])
    w_dn = f32c(inputs["ffn_w_down"][0])
    in_maps = []
    for c in range(ncores):
        xs = x[c * nseq:(c + 1) * nseq].reshape(nseq * S, D)
        in_maps.append({"x": np.ascontiguousarray(xs), "w_in": w_in, "w_out": w_out, "w_up": w_up, "w_down": w_dn,
                        "params": params, "stage": stage})
    res = run_bass_kernel_spmd(nc, in_maps, core_ids=list(range(ncores)))
    outs = [np.asarray(r["out"], np.float32).reshape(nseq, S, D) for r in res.results]
    if dbg:
        global DBG_OUT
        DBG_OUT = np.asarray(res.results[0]["dbg"])
    return np.concatenate(outs, axis=0)


def kernel(**inputs):
    x = inputs["x"]
    B, S, _ = x.shape
    nseq = B // NCORES
    return _run(inputs, nseq, S, NCORES)
```

```python
import math
from contextlib import ExitStack

import numpy as np
import concourse.bass as bass
import concourse.mybir as mybir
from concourse.bass_utils import run_bass_kernel_spmd

F32 = mybir.dt.float32
BF16 = mybir.dt.bfloat16
AF = mybir.ActivationFunctionType
ALU = mybir.AluOpType
AX = mybir.AxisListType

NCORES = 8
D = 1024
G = 512
NT = 4
LAMBDA_INIT = 0.8 - 0.6 * math.exp(-0.3 * 0)
EPS = 1e-6
SUBLN_EPS = 1e-5
NRING = 3
DBG_G = 0


class Buf:
    __slots__ = ("name", "w", "r")

    def __init__(self, name):
        self.name = name
        self.w = None
        self.r = {}


class Sched:
    NDS = 32

    def __init__(self, nc, stack):
        self.nc = nc
        self.eng = {"pe": nc.tensor, "act": nc.scalar, "dve": nc.vector, "pool": nc.gpsimd, "sp": nc.sync}
        self.prog = {k: [] for k in self.eng}
        self.sems = {}
        for k in ["pe", "act", "dve", "pool"]:
            self.sems[("e", k)] = stack.enter_context(nc.semaphore("s_" + k))
        for i in range(self.NDS):
            self.sems[("d", i)] = stack.enter_context(nc.semaphore("d%d" % i))
        self.ecount = {k: 0 for k in self.eng}
        self.dcount = [0] * self.NDS
        self.dpool = {"sp": list(range(0, 24)), "pool": list(range(24, 32)), "act": list(range(24, 32))}
        self.dnext = {"sp": 0, "pool": 0, "act": 0}
        self.waited = {k: {} for k in self.eng}

    def _deps(self, eng, reads, writes):
        need = {}

        def add(s, v):
            if need.get(s, 0) < v:
                need[s] = v

        for b in reads:
            if b.w is not None:
                add(*b.w)
        for b in writes:
            if b.w is not None:
                add(*b.w)
            for s, v in b.r.items():
                add(s, v)
        out = []
        for s, v in need.items():
            if eng == "pe" and s == ("e", "pe"):
                continue
            if self.waited[eng].get(s, 0) >= v:
                continue
            self.waited[eng][s] = v
            out.append((s, v))
        return out

    def _update(self, reads, writes, tok):
        s, v = tok
        for b in reads:
            if b.r.get(s, 0) < v:
                b.r[s] = v
        for b in writes:
            b.w = tok
            b.r = {}

    def op(self, eng, fn, reads=(), writes=()):
        waits = self._deps(eng, reads, writes)
        self.ecount[eng] += 1
        tok = (("e", eng), self.ecount[eng])
        self.prog[eng].append((fn, waits, tok[0], 1))
        self._update(reads, writes, tok)
        return tok

    def dma(self, q, out, in_, reads=(), writes=(), **kw):
        waits = self._deps(q, reads, writes)
        pool_ = self.dpool[q]
        i = pool_[self.dnext[q] % len(pool_)]
        self.dnext[q] += 1
        s = ("d", i)
        prev = self.dcount[i]
        if prev and self.waited[q].get(s, 0) < prev:
            self.waited[q][s] = prev
            waits.append((s, prev))
        self.dcount[i] += 16
        tok = (s, self.dcount[i])
        self.prog[q].append((lambda e: e.dma_start(out=out, in_=in_, **kw), waits, s, 16))
        self._update(reads, writes, tok)
        return tok

    def wait_all(self, eng, toks):
        waits = []
        for s, v in toks:
            if self.waited[eng].get(s, 0) < v:
                self.waited[eng][s] = v
                waits.append((s, v))
        self.prog[eng].append((None, waits, None, 0))

    def emit(self, block):
        def mk(k):
            def body(e):
                for fn, waits, sem, inc in self.prog[k]:
                    for s, v in waits:
                        e.wait_ge(self.sems[s], v)
                    if fn is not None:
                        ins = fn(e)
                        ins.then_inc(self.sems[sem], inc)
            return body

        block.tensor(mk("pe"))
        block.scalar(mk("act"))
        block.vector(mk("dve"))
        block.gpsimd(mk("pool"))
        block.sync(mk("sp"))


class Pack:
    def __init__(self):
        self.cols = []
        self.off = {}
        self.n = 0

    def add(self, name, arr):
        arr = np.ascontiguousarray(arr, dtype=np.float32).reshape(128, -1)
        self.off[name] = (self.n, arr.shape[1])
        self.cols.append(arr)
        self.n += arr.shape[1]

    def build(self):
        return np.ascontiguousarray(np.concatenate(self.cols, axis=1))


def _rel_bucket_np(rel):
    half = 16
    max_exact = 8
    ret = (rel > 0).astype(np.int32) * half
    n = np.abs(rel)
    nf = np.maximum(n, 1).astype(np.float32)
    large = max_exact + (np.log(nf / max_exact) / math.log(128 / max_exact) * (half - max_exact)).astype(np.int32)
    large = np.minimum(large, half - 1)
    return ret + np.where(n < max_exact, n, large)


def _chunkT(v, nchunk):
    return np.asarray(v, np.float32).reshape(nchunk, 128).T


def pack_params(inp):
    pk = Pack()
    rep = lambda v: np.broadcast_to(np.asarray(v, np.float32).reshape(1, -1), (128, np.asarray(v).size))
    pk.add("g1T", _chunkT(inp["norm1_g"][0], 8))
    pk.add("g2T", _chunkT(inp["norm2_g"][0], 8))
    pk.add("gfb", rep(inp["final_norm_g"]))
    lcw = np.asarray(inp["lru_conv_w"][0], np.float32)
    pk.add("lcw", lcw.reshape(4, 4, 128).transpose(2, 1, 0))
    pk.add("lcb", _chunkT(inp["lru_conv_b"][0], 4))
    pk.add("lba", _chunkT(np.asarray(inp["lru_ba"][0]).reshape(-1), 4))
    pk.add("lbx", _chunkT(np.asarray(inp["lru_bx"][0]).reshape(-1), 4))
    pk.add("llam", _chunkT(inp["lru_lambda"][0], 4))
    fcw = np.asarray(inp["ffn_conv_w"][0], np.float32)
    pk.add("fcw", fcw.reshape(3, 24, 128).transpose(2, 1, 0))
    pk.add("fcb", _chunkT(inp["ffn_conv_b"][0], 24))
    pk.add("subg", np.asarray(inp["diff_subln_g"][0], np.float32).reshape(128, 1))
    rb = np.asarray(inp["rel_bias"], np.float32)
    pk.add("farb", rep(rb[15, :]))
    kk = np.arange(128)[:, None]
    qq = np.arange(128)[None, :]
    bt = np.zeros((128, 4, 256), np.float32)
    rel_d = kk - qq
    rel_s = kk - (qq + 128)
    bd = _rel_bucket_np(rel_d)
    bs = _rel_bucket_np(rel_s)
    masked = (kk // 64) > (qq // 64)
    for h in range(4):
        t = rb[bd, h].copy()
        t[masked] = -30000.0
        bt[:, h, 0:128] = t
        bt[:, h, 128:256] = rb[bs, h]
    pk.add("biasT", bt)
    st = Pack()
    for nm in ("lru_wa", "lru_wx"):
        w = np.asarray(inp[nm][0], np.float32)
        m = np.zeros((128, 4, 128), np.float32)
        for c in range(4):
            m[0:64, c, 0:64] = w[2 * c]
            m[64:128, c, 64:128] = w[2 * c + 1]
        st.add(nm, m)
    for nm in ("diff_lq1", "diff_lk1", "diff_lq2", "diff_lk2"):
        st.add(nm, rep(inp[nm][0]))
    st.add("ident", np.eye(128, dtype=np.float32))
    return pk, st


def build(nseq, S, pk_off, npar, st_off, nst, dbg=False):
    NG = S // G
    NKT = S // 128
    ntok = nseq * S
    nc = bass.Bass("TRN2", target_bir_lowering=False)
    x_d = nc.dram_tensor("x", [ntok, D], F32, kind="ExternalInput").ap()
    win_d = nc.dram_tensor("w_in", [D, 2560], F32, kind="ExternalInput").ap()
    wout_d = nc.dram_tensor("w_out", [D, D], F32, kind="ExternalInput").ap()
    wup_d = nc.dram_tensor("w_up", [D, 6144], F32, kind="ExternalInput").ap()
    wdn_d = nc.dram_tensor("w_down", [3072, D], F32, kind="ExternalInput").ap()
    par_d = nc.dram_tensor("params", [128, npar], F32, kind="ExternalInput").ap()
    stg_d = nc.dram_tensor("stage", [128, nst], F32, kind="ExternalInput").ap()
    out_d = nc.dram_tensor("out", [ntok, D], F32, kind="ExternalOutput").ap()
    NSLAB = 25
    wsl_d = nc.dram_tensor("wslab", [NSLAB, 128, 8, 512], BF16).ap()
    dbg_d = nc.dram_tensor("dbg", [8, 128, 4096], F32, kind="ExternalOutput").ap() if dbg else None

    with ExitStack() as st:
        S_ = Sched(nc, st)
        T = lambda name, shape, dt: st.enter_context(nc.sbuf_tensor(name, shape, dt))
        prm = T("prm", [128, npar], F32)
        ring = T("ring", [128, NRING, 8, 512], BF16)
        KT = T("KT", [128, 4, S], BF16)
        Va = T("Va", [128, NKT, 4, 130], BF16)
        xg = T("xg", [128, 2, NT, 1024], F32)
        hb = T("hb", [128, NT, 1024], BF16)
        hb2 = T("hb2", [128, NT, 1024], BF16)
        hT = T("hT", [128, 8, G], BF16)
        hT2 = T("hT2", [128, 8, G], BF16)
        qT = T("qT", [128, 4, G], BF16)
        mixT = T("mixT", [128, 8, G], BF16)
        aT = T("aT", [128, 24, G], BF16)
        gg = T("gg", [128, 4, G], BF16)
        bdw = T("bdw", [128, 2, 4, 128], BF16)
        ident = T("ident", [128, 128], BF16)
        sm = T("sm", [128, 96], F32)
        xc4 = T("xc4", [128, 4, G], F32)
        xcb = T("xcb", [128, G], BF16)
        thr = T("thr", [128, G], F32)
        thi = T("thi", [128, G], F32)
        la2 = T("la2", [128, G], F32)
        lh = T("lh", [128, G], F32)
        lhalo = T("lhalo", [128, 4, 4], F32)
        lstate = T("lstate", [128, 4], F32)
        PT = T("PT", [128, 2, G], BF16)
        sbn = T("sbn", [128, 3, 256], F32)
        acp = T("acp", [128, 8, 129], F32)
        t1 = T("t1", [128, 128], F32)
        ob = T("ob", [128, NT, 4, 128], BF16)
        facc = T("facc", [128, 2, G], F32)
        fhalo = T("fhalo", [128, 24, 2], F32)
        banks = [st.enter_context(nc.psum_tensor("pb%d" % i, [128, 512], F32)) for i in range(8)]
        block = st.enter_context(nc.Block())

        def pv(name):
            o, n = pk_off[name]
            return prm[:, o:o + n]

        def sv(name):
            o, n = st_off[name]
            return xg[:, 1, 0:2, :].rearrange("p a b -> p (a b)")[:, o:o + n]

        SM = {}
        smn = [0]

        def smalloc(name, n):
            SM[name] = (smn[0], n)
            smn[0] += n
            assert smn[0] <= 96

        def smv(name, a=0, b=None):
            o, n = SM[name]
            if b is None:
                b = n
            return sm[:, o + a:o + b]

        for name, n in [("lam", 1), ("neglam", 1), ("e1", 1), ("e2", 1), ("s1", 1), ("s2", 1), ("cneg", 4),
                        ("nba", 4), ("nbx", 4), ("c2neg", 4), ("dummy", 1), ("subg2", 1), ("mhalf", 1), ("half", 1), ("ss", 4), ("ms", 4), ("rstd", 4),
                        ("rs", 8), ("rs2n", 4), ("sso", 4), ("mso", 4), ("rstdo", 4), ("tmp4", 4), ("ssf", 4), ("msf", 4), ("rstdf", 4), ("ss2", 2)]:
            smalloc(name, n)

        b_prm = Buf("prm")
        b_ring = [Buf("ring%d" % i) for i in range(NRING)]
        b_slabd = [[Buf("slabd%d_%d" % (i, k)) for k in range(2)] for i in range(NSLAB)]
        b_x = [[Buf("x%d_%d" % (i, t)) for t in range(NT)] for i in range(2)]
        b_stage_l = b_x[1][0:2]
        b_hb = [Buf("hb%d" % t) for t in range(NT)]
        b_hb2 = [Buf("hb2_%d" % t) for t in range(NT)]
        b_hT = [Buf("hT%d" % k) for k in range(8)]
        b_hT2 = [Buf("hT2_%d" % k) for k in range(8)]
        b_acp = Buf("acp")
        b_qT = [Buf("qT%d" % h) for h in range(4)]
        b_KT = [[Buf("KT%d_%d" % (h, g)) for g in range(NG)] for h in range(4)]
        b_V = [Buf("V%d" % j) for j in range(NKT)]
        b_mix = [Buf("mix%d" % k) for k in range(8)]
        b_aT = [Buf("aT%d" % k) for k in range(24)]
        b_gg = [Buf("gg%d" % c) for c in range(4)]
        b_ps = [Buf("ps%d" % i) for i in range(8)]
        b_sm = {k: Buf("sm_" + k) for k in SM}
        b_bdw = Buf("bdw")
        b_ident = Buf("ident")
        b_xcb, b_thr, b_thi, b_la, b_la2, b_lu, b_lh = [Buf(n) for n in "xcb thr thi la la2 lu lh".split()]
        b_xc = [Buf("xc%d" % c) for c in range(4)]
        b_lhalo = [Buf("lhalo%d" % c) for c in range(4)]
        b_lstate = [Buf("lstate%d" % c) for c in range(4)]
        b_PT = [Buf("PT0"), Buf("PT1")]
        b_sbn = [Buf("sbn0"), Buf("sbn1"), Buf("sbn2")]
        b_t1 = Buf("t1")
        b_ob = [[Buf("ob%d_%d" % (i, h)) for h in range(4)] for i in range(NT)]
        b_facc = [Buf("facc0"), Buf("facc1")]
        b_fgl = [Buf("fgl0"), Buf("fgl1")]
        b_fhalo = [Buf("fhalo%d" % k) for k in range(24)]
        b_out = Buf("outd")

        op = S_.op
        out_toks = []

        def dump(slot, ap2d, n, bufs):
            if dbg_d is None:
                return
            out_toks.append(S_.dma("pool", dbg_d[slot, :, 0:n], ap2d, reads=bufs, writes=[Buf("dbgw")]))

        rot = [0]
        nbanks = [4]

        def tbank():
            i = rot[0] % nbanks[0]
            rot[0] = i + 1
            return i

        S_.dma("sp", prm[:], par_d, writes=[b_prm])
        S_.dma("sp", xg[:, 1, 0:2, :].rearrange("p a b -> p (a b)")[:, 0:nst], stg_d, writes=b_stage_l)

        op("pool", lambda e: e.memset(smv("mhalf"), -0.5), writes=[b_sm["mhalf"]])
        op("pool", lambda e: e.memset(smv("half"), 0.5), writes=[b_sm["half"]])
        op("pool", lambda e: e.memset(Va[:, :, :, 128:130], 1.0), writes=b_V)
        def cast_slab(si, parts):
            for pi, (c0, ncol, src) in enumerate(parts):
                S_.dma("pool", wsl_d[si, :, :, c0:c0 + ncol], src.rearrange("(kk p) c -> p kk c", p=128),
                       writes=[b_slabd[si][pi]])

        SL_IN, SL_OUT, SL_UP, SL_DN = 0, 5, 7, 19
        for i in range(5):
            cast_slab(SL_IN + i, [(0, 512, win_d[:, i * 512:(i + 1) * 512])])
        deferred_casts = []
        _DEFER = False
        for i in range(2):
            deferred_casts.append(lambda i=i: cast_slab(SL_OUT + i, [(0, 512, wout_d[:, i * 512:(i + 1) * 512])]))
        for s_ in range(6):
            deferred_casts.append(lambda s_=s_: cast_slab(SL_UP + s_, [(0, 512, wup_d[:, s_ * 512:(s_ + 1) * 512])]))
        for s_ in range(6):
            deferred_casts.append(lambda s_=s_: cast_slab(SL_UP + 6 + s_, [(0, 512, wup_d[:, 3072 + s_ * 512:3072 + (s_ + 1) * 512])]))
        for c2 in range(2):
            for ks in range(3):
                deferred_casts.append(lambda c2=c2, ks=ks: cast_slab(
                    SL_DN + c2 * 3 + ks, [(0, 512, wdn_d[ks * 1024:(ks + 1) * 1024, c2 * 512:(c2 + 1) * 512])]))

        if not _DEFER:
            for dc_ in deferred_casts:
                dc_()
            deferred_casts = []
        op("dve", lambda e: e.tensor_copy(out=ident[:], in_=sv("ident")), reads=b_stage_l, writes=[b_ident])
        op("dve", lambda e: e.tensor_copy(out=bdw[:, 0, :, :], in_=sv("lru_wa").rearrange("p (c m) -> p c m", c=4)),
           reads=b_stage_l, writes=[b_bdw])
        op("dve", lambda e: e.tensor_copy(out=bdw[:, 1, :, :], in_=sv("lru_wx").rearrange("p (c m) -> p c m", c=4)),
           reads=b_stage_l, writes=[b_bdw])
        op("dve", lambda e: e.tensor_tensor(out=t1[:, 0:64], in0=sv("diff_lq1"), in1=sv("diff_lk1"), op=ALU.mult),
           reads=b_stage_l, writes=[b_t1])
        op("dve", lambda e: e.tensor_reduce(out=smv("s1"), in_=t1[:, 0:64], axis=AX.X, op=ALU.add),
           reads=[b_t1], writes=[b_sm["s1"]])
        op("dve", lambda e: e.tensor_tensor(out=t1[:, 64:128], in0=sv("diff_lq2"), in1=sv("diff_lk2"), op=ALU.mult),
           reads=b_stage_l, writes=[b_t1])
        op("dve", lambda e: e.tensor_reduce(out=smv("s2"), in_=t1[:, 64:128], axis=AX.X, op=ALU.add),
           reads=[b_t1], writes=[b_sm["s2"]])
        op("act", lambda e: e.activation(out=smv("e1"), in_=smv("s1"), func=AF.Exp), reads=[b_sm["s1"]], writes=[b_sm["e1"]])
        op("act", lambda e: e.activation(out=smv("e2"), in_=smv("s2"), func=AF.Exp), reads=[b_sm["s2"]], writes=[b_sm["e2"]])
        op("dve", lambda e: e.tensor_tensor(out=smv("lam"), in0=smv("e1"), in1=smv("e2"), op=ALU.subtract),
           reads=[b_sm["e1"], b_sm["e2"]], writes=[b_sm["lam"]])
        op("dve", lambda e: e.tensor_scalar(out=smv("neglam"), in0=smv("lam"), scalar1=-1.0, scalar2=-LAMBDA_INIT,
                                            op0=ALU.mult, op1=ALU.add),
           reads=[b_sm["lam"]], writes=[b_sm["neglam"]])
        op("act", lambda e: e.activation(out=smv("tmp4"), in_=pv("llam"), func=AF.Exp, scale=-1.0),
           reads=[b_prm], writes=[b_sm["tmp4"]])
        op("act", lambda e: e.activation(out=smv("tmp4"), in_=smv("tmp4"), func=AF.Ln, bias=1.0, scale=1.0),
           reads=[b_sm["tmp4"]], writes=[b_sm["tmp4"]])
        op("dve", lambda e: e.tensor_scalar(out=smv("cneg"), in0=smv("tmp4"), scalar1=-8.0, scalar2=None, op0=ALU.mult),
           reads=[b_sm["tmp4"]], writes=[b_sm["cneg"]])
        op("dve", lambda e: e.tensor_scalar(out=smv("c2neg"), in0=smv("tmp4"), scalar1=-16.0, scalar2=None, op0=ALU.mult),
           reads=[b_sm["tmp4"]], writes=[b_sm["c2neg"]])
        op("dve", lambda e: e.tensor_scalar(out=smv("nba"), in0=pv("lba"), scalar1=-1.0, scalar2=None, op0=ALU.mult),
           reads=[b_prm], writes=[b_sm["nba"]])
        op("dve", lambda e: e.tensor_scalar(out=smv("nbx"), in0=pv("lbx"), scalar1=-1.0, scalar2=None, op0=ALU.mult),
           reads=[b_prm], writes=[b_sm["nbx"]])
        op("dve", lambda e: e.tensor_scalar(out=smv("subg2"), in0=pv("subg"), scalar1=1.0 - LAMBDA_INIT, scalar2=None,
                                            op0=ALU.mult),
           reads=[b_prm], writes=[b_sm["subg2"]])

        ring_i = [0]

        def load_slab(si):
            r = ring_i[0]
            ring_i[0] = (r + 1) % NRING
            S_.dma("sp", ring[:, r, :, :], wsl_d[si], reads=b_slabd[si], writes=[b_ring[r]])
            return r

        def load_x(seq, g, xi):
            tok0 = seq * S + g * G
            for t in range(NT):
                S_.dma("sp", xg[:, xi, t, :], x_d[tok0 + t * 128:tok0 + (t + 1) * 128, :], writes=[b_x[xi][t]])

        evac_flip = [0]

        def evac_eng():
            evac_flip[0] ^= 1
            return "act" if evac_flip[0] else "dve"

        def rstd_from_ss(ssn, msn, rsn, n, inv_n, eps, use_act=False):
            op("dve", lambda e: e.tensor_scalar(out=smv(msn, 0, n), in0=smv(ssn, 0, n), scalar1=inv_n, scalar2=eps,
                                                op0=ALU.mult, op1=ALU.add),
               reads=[b_sm[ssn]], writes=[b_sm[msn]])
            if use_act:
                op("act", lambda e: e.activation(out=smv(msn, 0, n), in_=smv(msn, 0, n), func=AF.Ln),
                   reads=[b_sm[msn]], writes=[b_sm[msn]])
                op("act", lambda e: e.activation(out=smv(rsn, 0, n), in_=smv(msn, 0, n), func=AF.Exp, scale=-0.5),
                   reads=[b_sm[msn]], writes=[b_sm[rsn]])
                return
            op("pool", lambda e: e.tensor_tensor(out=smv(rsn, 0, n), in0=smv(msn, 0, n),
                                                 in1=smv("mhalf").to_broadcast([128, n]), op=ALU.pow),
               reads=[b_sm[msn], b_sm["mhalf"]], writes=[b_sm[rsn]])

        def norm_chain(xi, hb=hb, b_hb=b_hb):
            for t in range(NT):
                if t < 2:
                    op("act", lambda e, t=t: e.activation(out=hb[:, t, :], in_=xg[:, xi, t, :], func=AF.Square,
                                                          accum_out=smv("ss", t, t + 1)),
                       reads=[b_x[xi][t]], writes=[b_hb[t], b_sm["ss"]])
                else:
                    op("dve", lambda e, t=t: e.scalar_tensor_tensor(out=hb[:, t, :], in0=xg[:, xi, t, :], scalar=1.0,
                                                                    in1=xg[:, xi, t, :], op0=ALU.mult, op1=ALU.mult,
                                                                    accum_out=smv("ss2", t - 2, t - 1)),
                       reads=[b_x[xi][t]], writes=[b_hb[t], b_sm["ss2"]])
            op("dve", lambda e: e.tensor_scalar(out=smv("ms", 0, 2), in0=smv("ss", 0, 2), scalar1=1.0 / D, scalar2=EPS,
                                                op0=ALU.mult, op1=ALU.add),
               reads=[b_sm["ss"]], writes=[b_sm["ms"]])
            op("dve", lambda e: e.tensor_scalar(out=smv("ms", 2, 4), in0=smv("ss2", 0, 2), scalar1=1.0 / D, scalar2=EPS,
                                                op0=ALU.mult, op1=ALU.add),
               reads=[b_sm["ss2"]], writes=[b_sm["ms"]])
            op("act", lambda e: e.activation(out=smv("ms"), in_=smv("ms"), func=AF.Ln), reads=[b_sm["ms"]],
               writes=[b_sm["ms"]])
            op("act", lambda e: e.activation(out=smv("rstd"), in_=smv("ms"), func=AF.Exp, scale=-0.5),
               reads=[b_sm["ms"]], writes=[b_sm["rstd"]])
            for t in range(NT):
                if t < 2:
                    op("act", lambda e, t=t: e.activation(out=hb[:, t, :], in_=xg[:, xi, t, :], func=AF.Identity,
                                                          scale=smv("rstd", t, t + 1)),
                       reads=[b_x[xi][t], b_sm["rstd"]], writes=[b_hb[t]])
                else:
                    op("dve", lambda e, t=t: e.tensor_scalar(out=hb[:, t, :], in0=xg[:, xi, t, :],
                                                             scalar1=smv("rstd", t, t + 1), scalar2=None, op0=ALU.mult),
                       reads=[b_x[xi][t], b_sm["rstd"]], writes=[b_hb[t]])
        def norm_tr_P(k, bi, hb=hb, b_hb=b_hb):
            pbf = banks[bi][:].bitcast(BF16)

            def f(e):
                for t in range(NT):
                    last = e.transpose(pbf[:, t * 128:(t + 1) * 128], hb[:, t, k * 128:(k + 1) * 128], ident[:])
                return last

            op("pe", f, reads=b_hb + [b_ident], writes=[b_ps[bi]])

        def norm_tr_C(k, bi, gname, dstT, b_dst):
            pbf = banks[bi][:].bitcast(BF16)
            gcol = pv(gname)[:, k:k + 1]
            if evac_eng() == "act":
                op("act", lambda e: e.activation(out=dstT[:, k, :], in_=pbf[:, 0:G], func=AF.Identity, scale=gcol),
                   reads=[b_ps[bi], b_prm], writes=[b_dst[k]])
            else:
                op("dve", lambda e: e.tensor_scalar(out=dstT[:, k, :], in0=pbf[:, 0:G], scalar1=gcol, scalar2=None,
                                                    op0=ALU.mult),
                   reads=[b_ps[bi], b_prm], writes=[b_dst[k]])

        def norm_transposes(gname, dstT, b_dst, hb=hb, b_hb=b_hb):
            for k in range(8):
                bi = tbank()
                norm_tr_P(k, bi, hb, b_hb)
                norm_tr_C(k, bi, gname, dstT, b_dst)

        def mm_fm(bi, r, c, src, b_src):
            def f(e):
                for kk in range(8):
                    last = e.matmul(banks[bi][:], lhsT=ring[:, r, kk, c * 128:(c + 1) * 128], rhs=src[:, kk, :],
                                    start=(kk == 0), stop=(kk == 7))
                return last

            op("pe", f, reads=[b_ring[r]] + list(b_src), writes=[b_ps[bi]])

        def conv_from_psum(bi, acc, b_acc, w_ap, b_ap, ntap, halo, b_halo, first, last_out=None, b_last=None):
            ps = banks[bi]
            nh = ntap - 1
            op("act", lambda e: e.activation(out=acc, in_=ps[:], func=AF.Identity, scale=w_ap[:, nh:nh + 1], bias=b_ap),
               reads=[b_ps[bi], b_prm], writes=[b_acc])
            for sft in range(1, ntap):
                j = nh - sft
                if last_out is not None and sft == nh:
                    op("dve", lambda e, sft=sft, j=j: e.scalar_tensor_tensor(out=last_out[:, sft:G], in0=ps[:, 0:G - sft],
                                                                             scalar=w_ap[:, j:j + 1], in1=acc[:, sft:G],
                                                                             op0=ALU.mult, op1=ALU.add),
                       reads=[b_ps[bi], b_prm, b_acc], writes=[b_last])
                else:
                    op("dve", lambda e, sft=sft, j=j: e.scalar_tensor_tensor(out=acc[:, sft:G], in0=ps[:, 0:G - sft],
                                                                             scalar=w_ap[:, j:j + 1], in1=acc[:, sft:G],
                                                                             op0=ALU.mult, op1=ALU.add),
                       reads=[b_ps[bi], b_prm], writes=[b_acc])
            if not first:
                for sft in range(1, ntap):
                    j = nh - sft
                    op("dve", lambda e, sft=sft, j=j: e.scalar_tensor_tensor(out=acc[:, 0:sft], in0=halo[:, nh - sft:nh],
                                                                             scalar=w_ap[:, j:j + 1], in1=acc[:, 0:sft],
                                                                             op0=ALU.mult, op1=ALU.add),
                       reads=[b_halo, b_prm], writes=[b_acc])
            if last_out is not None:
                op("dve", lambda e: e.tensor_copy(out=last_out[:, 0:nh], in_=acc[:, 0:nh]),
                   reads=[b_acc], writes=[b_last])
            op("dve", lambda e: e.tensor_copy(out=halo[:, 0:nh], in_=ps[:, G - nh:G]),
               reads=[b_ps[bi]], writes=[b_halo])

        ngrp = nseq * NG
        groups = [(sq, g) for sq in range(nseq) for g in range(NG)]

        def stage_A(gi):
            seq, g = groups[gi]
            xi = gi % 2
            first = (g == 0)
            nbanks[0] = 8
            if gi == 0:
                norm_chain(xi, hb2, b_hb2)
            norm_transposes("g1T", hT, b_hT, hb2, b_hb2)
            if gi > 0:
                norm_transposes("g2T", hT2, b_hT2)
            if gi == 0:
                dump(0, hT[:, :, :].rearrange("p a b -> p (a b)"), 4096, b_hT)
            r = load_slab(SL_IN + 1)
            for c in range(4):
                bg = tbank()
                mm_fm(bg, r, c, hT, b_hT)
                op("act", lambda e, c=c, bg=bg: e.activation(out=gg[:, c, :], in_=banks[bg][:], func=AF.Gelu_apprx_tanh),
                   reads=[b_ps[bg]], writes=[b_gg[c]])
            op("act", lambda e: e.activation(out=smv("dummy"), in_=smv("half"), func=AF.Ln),
               reads=[b_sm["half"]], writes=[b_sm["dummy"]])
            r = load_slab(SL_IN + 0)
            for c in range(4):
                bi = tbank()
                mm_fm(bi, r, c, hT, b_hT)
                w_ap = pv("lcw")[:, c * 4:(c + 1) * 4]
                conv_from_psum(bi, xc4[:, c, :], b_xc[c], w_ap, pv("lcb")[:, c:c + 1], 4, lhalo[:, c, :], b_lhalo[c], first)
            r = load_slab(SL_IN + 2)
            for h in range(4):
                bi = tbank()
                mm_fm(bi, r, h, hT, b_hT)
                if evac_eng() == "act":
                    op("act", lambda e, h=h, bi=bi: e.activation(out=qT[:, h, :], in_=banks[bi][:], func=AF.Copy,
                                                                 scale=0.125),
                       reads=[b_ps[bi]], writes=[b_qT[h]])
                else:
                    op("dve", lambda e, h=h, bi=bi: e.tensor_scalar(out=qT[:, h, :], in0=banks[bi][:], scalar1=0.125,
                                                                    scalar2=None, op0=ALU.mult),
                       reads=[b_ps[bi]], writes=[b_qT[h]])
            r = load_slab(SL_IN + 3)
            for h in range(4):
                bi = tbank()
                mm_fm(bi, r, h, hT, b_hT)
                dst = KT[:, h, g * G:(g + 1) * G]
                if evac_eng() == "act":
                    op("act", lambda e, dst=dst, bi=bi: e.activation(out=dst, in_=banks[bi][:], func=AF.Copy),
                       reads=[b_ps[bi]], writes=[b_KT[h][g]])
                else:
                    op("dve", lambda e, dst=dst, bi=bi: e.tensor_copy(out=dst, in_=banks[bi][:]),
                       reads=[b_ps[bi]], writes=[b_KT[h][g]])
            r = load_slab(SL_IN + 4)
            for t in range(NT):
                bi = tbank()
                jt = g * NT + t

                def f(e, t=t, bi=bi, r=r):
                    for kk in range(8):
                        last = e.matmul(banks[bi][:], lhsT=hT[:, kk, t * 128:(t + 1) * 128], rhs=ring[:, r, kk, :],
                                        start=(kk == 0), stop=(kk == 7))
                    return last

                op("pe", f, reads=[b_ring[r]] + b_hT, writes=[b_ps[bi]])
                src = banks[bi][:].rearrange("p (h e) -> p h e", h=4)
                dst = Va[:, jt, :, 0:128]
                if evac_eng() == "act":
                    op("act", lambda e, dst=dst, src=src: e.activation(out=dst, in_=src, func=AF.Copy),
                       reads=[b_ps[bi]], writes=[b_V[jt]])
                else:
                    op("dve", lambda e, dst=dst, src=src: e.tensor_copy(out=dst, in_=src),
                       reads=[b_ps[bi]], writes=[b_V[jt]])

        ACC0 = 2

        def acc_region(il, c):
            rr = c * 4 + il
            return ACC0 + rr // 3, (rr % 3) * 130, rr

        def make_B(gi):
            seq, g = groups[gi]
            first = (g == 0)
            nj = g * NT + NT
            info = {}
            aux = "dve" if gi == 0 else "pool"

            def lru_P(c, which, bk):
                if which == 0:
                    op(aux, lambda e: e.tensor_copy(out=xcb[:], in_=xc4[:, c, :]), reads=[b_xc[c]], writes=[b_xcb])
                op("pe", lambda e: e.matmul(banks[bk][:], lhsT=bdw[:, which, c, :], rhs=xcb[:], start=True, stop=True),
                   reads=[b_bdw, b_xcb], writes=[b_ps[bk]])

            def lru_C(c, which, bk):
                if which == 0:
                    op("act", lambda e: e.activation(out=thr[:], in_=banks[bk][:], func=AF.Exp, scale=-1.0,
                                                     bias=smv("nba", c, c + 1)),
                       reads=[b_ps[bk], b_sm["nba"]], writes=[b_thr])
                    op("act", lambda e: e.activation(out=thr[:], in_=thr[:], func=AF.Ln, bias=1.0, scale=1.0),
                       reads=[b_thr], writes=[b_thr])
                    op("act", lambda e: e.activation(out=thr[:], in_=thr[:], func=AF.Exp, scale=-1.0),
                       reads=[b_thr], writes=[b_thr])
                    op("act", lambda e: e.activation(out=la2[:], in_=thr[:], func=AF.Exp, scale=smv("c2neg", c, c + 1)),
                       reads=[b_thr, b_sm["c2neg"]], writes=[b_la2])
                    op("act", lambda e: e.activation(out=thr[:], in_=thr[:], func=AF.Exp, scale=smv("cneg", c, c + 1)),
                       reads=[b_thr, b_sm["cneg"]], writes=[b_thr])
                    op("act", lambda e: e.activation(out=la2[:], in_=la2[:], func=AF.Ln, bias=1.0, scale=-1.0),
                       reads=[b_la2], writes=[b_la2])
                    op("act", lambda e: e.activation(out=la2[:], in_=la2[:], func=AF.Exp, scale=0.5),
                       reads=[b_la2], writes=[b_la2])
                    return
                op("act", lambda e: e.activation(out=thi[:], in_=banks[bk][:], func=AF.Exp, scale=-1.0,
                                                 bias=smv("nbx", c, c + 1)),
                   reads=[b_ps[bk], b_sm["nbx"]], writes=[b_thi])
                op("act", lambda e: e.activation(out=thi[:], in_=thi[:], func=AF.Ln, bias=1.0, scale=1.0),
                   reads=[b_thi], writes=[b_thi])
                op("act", lambda e: e.activation(out=thi[:], in_=thi[:], func=AF.Exp, scale=-1.0),
                   reads=[b_thi], writes=[b_thi])
                op(aux, lambda e: e.tensor_tensor(out=thi[:], in0=thi[:], in1=xc4[:, c, :], op=ALU.mult),
                   reads=[b_thi, b_xc[c]], writes=[b_thi])
                op("dve", lambda e: e.tensor_tensor(out=thi[:], in0=thi[:], in1=la2[:], op=ALU.mult),
                   reads=[b_thi, b_la2], writes=[b_thi])
                init = 0.0 if first else lstate[:, c:c + 1]
                op("dve", lambda e: e.tensor_tensor_scan(out=lh[:], data0=thr[:], data1=thi[:], initial=init,
                                                         op0=ALU.mult, op1=ALU.add),
                   reads=[b_thr, b_thi, b_lstate[c]], writes=[b_lh])
                op("dve", lambda e: e.tensor_copy(out=lstate[:, c:c + 1], in_=lh[:, G - 1:G]),
                   reads=[b_lh], writes=[b_lstate[c]])
                op(aux, lambda e: e.tensor_tensor(out=mixT[:, c, :], in0=gg[:, c, :], in1=lh[:], op=ALU.mult),
                   reads=[b_gg[c], b_lh], writes=[b_mix[c]])

            def emit_qk(key, bk, p):
                h, j, c = key
                i0 = max(0, j - g * NT)
                N = (NT - i0) * 128
                q0 = i0 * 128
                gk = j // NT
                op("pe", lambda e: e.matmul(banks[bk][:, 0:N], lhsT=KT[c * 64:(c + 1) * 64, h, j * 128:(j + 1) * 128],
                                            rhs=qT[c * 64:(c + 1) * 64, h, q0:G], start=True, stop=True),
                   reads=[b_KT[h][gk], b_qT[h]], writes=[b_ps[bk]])
                d0 = g * NT + i0 - j
                if d0 == 0:
                    boff, wn = 0, (256 if i0 < NT - 1 else 128)
                elif d0 == 1:
                    boff, wn = 128, 128
                else:
                    boff, wn = 0, 0
                if wn:
                    bo, nn = pk_off["biasT"]
                    bias_ap = prm[:, bo + h * 256 + boff: bo + h * 256 + boff + wn]
                    op("dve", lambda e: e.tensor_tensor(out=sbn[:, p, 0:wn], in0=banks[bk][:, 0:wn], in1=bias_ap,
                                                        op=ALU.add),
                       reads=[b_ps[bk], b_prm], writes=[b_sbn[p]])
                info[key] = (i0, N, wn)

            def emit_softmax_pv(key, bk, p, p3):
                h, j, c = key
                i0, N, wn = info.pop(key)
                if wn:
                    op("act", lambda e: e.activation(out=PT[:, p, 0:wn], in_=sbn[:, p3, 0:wn], func=AF.Exp),
                       reads=[b_sbn[p3]], writes=[b_PT[p]])
                if N > wn:
                    fo, fn_ = pk_off["farb"]
                    op("act", lambda e: e.activation(out=PT[:, p, wn:N], in_=banks[bk][:, wn:N], func=AF.Exp,
                                                     bias=prm[:, fo + h:fo + h + 1]),
                       reads=[b_ps[bk], b_prm], writes=[b_PT[p]])

                def f(e):
                    last = None
                    for il in range(i0, NT):
                        qi = g * NT + il
                        abk, off, rr = acc_region(il, c)
                        last = e.matmul(banks[abk][:, off:off + 129],
                                        lhsT=PT[:, p, (il - i0) * 128:(il - i0 + 1) * 128],
                                        rhs=Va[:, j, h, 0:129], start=(j == 0 and rr % 3 == 0), stop=(j == qi),
                                        skip_group_check=True)
                    return last

                wb = sorted(set(acc_region(il, c)[0] for il in range(i0, NT)))
                op("pe", f, reads=[b_PT[p], b_V[j]], writes=[b_ps[k_] for k_ in wb])

            def emit_norm(h):
                for abk, r0, nr in ((ACC0, 0, 3), (ACC0 + 1, 3, 3), (ACC0 + 2, 6, 2)):
                    src = banks[abk][:, 0:nr * 130].rearrange("p (r w) -> p r w", w=130)[:, :, 0:129]
                    op("dve", lambda e, src=src, r0=r0, nr=nr: e.tensor_copy(out=acp[:, r0:r0 + nr, :], in_=src),
                       reads=[b_ps[abk]], writes=[b_acp])
                op("dve", lambda e: e.reciprocal(out=smv("rs").rearrange("p (r o) -> p r o", o=1),
                                                 in_=acp[:, :, 128:129]),
                   reads=[b_acp], writes=[b_sm["rs"]])
                op("dve", lambda e: e.tensor_scalar(out=smv("rs2n"), in0=smv("rs", 4, 8), scalar1=smv("neglam"),
                                                    scalar2=None, op0=ALU.mult),
                   reads=[b_sm["rs"], b_sm["neglam"]], writes=[b_sm["rs2n"]])
                O1 = acp[:, 0:4, 0:128]
                O2 = acp[:, 4:8, 0:128]
                rs1b = smv("rs", 0, 4).unsqueeze(2).to_broadcast([128, NT, 128])
                rs2b = smv("rs2n").unsqueeze(2).to_broadcast([128, NT, 128])
                op("dve", lambda e: e.tensor_tensor(out=O1, in0=O1, in1=rs1b, op=ALU.mult),
                   reads=[b_acp, b_sm["rs"]], writes=[b_acp])
                op("dve", lambda e: e.tensor_tensor(out=O2, in0=O2, in1=rs2b, op=ALU.mult),
                   reads=[b_acp, b_sm["rs2n"]], writes=[b_acp])
                op("dve", lambda e: e.tensor_tensor(out=O1, in0=O1, in1=O2, op=ALU.add),
                   reads=[b_acp], writes=[b_acp])
                op("dve", lambda e: e.tensor_tensor(out=O2, in0=O1, in1=O1, op=ALU.mult),
                   reads=[b_acp], writes=[b_acp])
                op("dve", lambda e: e.tensor_reduce(out=smv("sso"), in_=O2, axis=AX.X, op=ALU.add),
                   reads=[b_acp], writes=[b_sm["sso"]])
                rstd_from_ss("sso", "mso", "rstdo", NT, 1.0 / 128.0, SUBLN_EPS, use_act=(gi == 0))
                rsob = smv("rstdo").unsqueeze(2).to_broadcast([128, NT, 128])
                op("dve", lambda e: e.tensor_tensor(out=ob[:, :, h, :], in0=O1, in1=rsob, op=ALU.mult),
                   reads=[b_acp, b_sm["rstdo"]], writes=[b_ob[il][h] for il in range(NT)])
            users = []
            for h in range(4):
                users.append(("lru", h, 0, True))
                users.append(("lru", h, 1, True))
                for j in range(nj):
                    for c in range(2):
                        users.append(("qk", h, (h, j, c), True))
                users.append(("norm", h, None, True))

            def emitP(n, bk):
                kind, h, key, _ = users[n]
                if kind == "lru":
                    lru_P(h, key, bk)
                elif kind == "qk":
                    emit_qk(key, bk, n % 3)


            def emitC(n, bk):
                kind, h, key, _ = users[n]
                if kind == "lru":
                    lru_C(h, key, bk)
                elif kind == "qk":
                    emit_softmax_pv(key, bk, n % 2, n % 3)
                else:
                    emit_norm(h)

            def weight(n):
                kind, h, key, _ = users[n]
                if kind == "lru":
                    return 4.0 if key == 0 else 2.5
                if kind == "norm":
                    return 3.0
                hh, j, c = key
                return 1.6 if j >= g * NT - 1 else 1.0

            return [(lambda bk, n=n: emitP(n, bk), lambda bk, n=n: emitC(n, bk), users[n][3], weight(n))
                    for n in range(len(users))]

        def make_C(gi):
            xi = gi % 2
            ul = []
            nop1 = lambda bk: None

            def tr_P(h, bk):
                pbf = banks[bk][:].bitcast(BF16)

                def f(e):
                    for il in range(NT):
                        last = e.transpose(pbf[:, il * 128:(il + 1) * 128], ob[:, il, h, :], ident[:])
                    return last

                op("pe", f, reads=[b_ob[il][h] for il in range(NT)] + [b_ident], writes=[b_ps[bk]])

            def tr_C(h, bk):
                pbf = banks[bk][:].bitcast(BF16)
                op("dve", lambda e: e.tensor_scalar(out=mixT[:, 4 + h, :], in0=pbf[:, 0:G], scalar1=smv("subg2"),
                                                    scalar2=None, op0=ALU.mult),
                   reads=[b_ps[bk], b_sm["subg2"]], writes=[b_mix[4 + h]])

            if gi + 1 < ngrp:
                ul.append((nop1, lambda bk: norm_chain(1 - xi, hb2, b_hb2), True))
            for h in range(4):
                ul.append((lambda bk, h=h: tr_P(h, bk), lambda bk, h=h: tr_C(h, bk), True))
            cur = {}

            def wo_P(c2, t, bk):
                if t == 0:
                    cur["r"] = load_slab(SL_OUT + c2)
                r = cur["r"]

                def f(e):
                    for kk in range(8):
                        last = e.matmul(banks[bk][:], lhsT=mixT[:, kk, t * 128:(t + 1) * 128], rhs=ring[:, r, kk, :],
                                        start=(kk == 0), stop=(kk == 7))
                    return last

                op("pe", f, reads=[b_ring[r]] + b_mix, writes=[b_ps[bk]])

            def wo_C(c2, t, bk):
                xs = xg[:, xi, t, c2 * 512:(c2 + 1) * 512]
                op("dve", lambda e: e.tensor_tensor(out=xs, in0=banks[bk][:], in1=xs, op=ALU.add),
                   reads=[b_ps[bk], b_x[xi][t]], writes=[b_x[xi][t]])

            for c2 in range(2):
                for t in range(NT):
                    ul.append((lambda bk, c2=c2, t=t: wo_P(c2, t, bk), lambda bk, c2=c2, t=t: wo_C(c2, t, bk),
                               not (c2 == 0 and t == 0)))
            ul.append((nop1, lambda bk: norm_chain(xi), True))
            return ul

        def ffn_units(gi, up_banks, acc_banks):
            seq, g = groups[gi]
            xi = gi % 2
            tok0 = seq * S + g * G
            first = (g == 0)
            units = []
            rot_u = [0]

            def ubank():
                bk = up_banks[rot_u[0] % len(up_banks)]
                rot_u[0] += 1
                return bk

            cur = {}

            def gate_pe(fc):
                if fc % 4 == 0:
                    cur["r"] = load_slab(SL_UP + fc // 4)
                r = cur["r"]
                bG = ubank()
                cur[("g", fc)] = bG
                mm_fm(bG, r, fc % 4, hT2, b_hT2)

            def gate_rest(fc):
                bG = cur.pop(("g", fc))
                cc = fc % 2
                w_ap = pv("fcw")[:, fc * 3:(fc + 1) * 3]
                conv_from_psum(bG, facc[:, cc, :], b_facc[cc], w_ap, pv("fcb")[:, fc:fc + 1], 3, fhalo[:, fc, :],
                               b_fhalo[fc], first, last_out=aT[:, fc, :], b_last=b_aT[fc])

            def unit_gelu(half):
                for k4 in range(3 * half, 3 * half + 3):
                    op("act", lambda e, k4=k4: e.activation(out=aT[:, 4 * k4:4 * k4 + 4, :], in_=aT[:, 4 * k4:4 * k4 + 4, :],
                                                            func=AF.Gelu_apprx_tanh),
                       reads=b_aT[4 * k4:4 * k4 + 4], writes=b_aT[4 * k4:4 * k4 + 4])
                op("act", lambda e: e.activation(out=smv("dummy"), in_=smv("half"), func=AF.Ln),
                   reads=[b_sm["half"]], writes=[b_sm["dummy"]])

            def val_pe(fc):
                if fc % 4 == 0:
                    cur["r"] = load_slab(SL_UP + 6 + fc // 4)
                r = cur["r"]
                bV = ubank()
                cur[("v", fc)] = bV
                mm_fm(bV, r, fc % 4, hT2, b_hT2)

            def val_rest(fc):
                bV = cur.pop(("v", fc))
                op("dve", lambda e: e.tensor_tensor(out=aT[:, fc, :], in0=banks[bV][:], in1=aT[:, fc, :], op=ALU.mult),
                   reads=[b_ps[bV], b_aT[fc]], writes=[b_aT[fc]])

            def unit_down(c2, ks, ts, accs):
                r = load_slab(SL_DN + c2 * 3 + ks)
                for t, bk in zip(ts, accs):
                    def f(e, t=t, bk=bk):
                        for kk in range(8):
                            last = e.matmul(banks[bk][:], lhsT=aT[:, ks * 8 + kk, t * 128:(t + 1) * 128],
                                            rhs=ring[:, r, kk, :], start=(ks == 0 and kk == 0),
                                            stop=(ks == 2 and kk == 7))
                        return last

                    op("pe", f, reads=[b_ring[r]] + b_aT[ks * 8:(ks + 1) * 8], writes=[b_ps[bk]])

            def unit_down_evac(c2, ts, accs):
                for t, bk in zip(ts, accs):
                    xs = xg[:, xi, t, c2 * 512:(c2 + 1) * 512]
                    op("dve", lambda e, xs=xs, bk=bk: e.tensor_tensor(out=xs, in0=banks[bk][:], in1=xs, op=ALU.add),
                       reads=[b_ps[bk], b_x[xi][t]], writes=[b_x[xi][t]])

            def unit_final():
                jk = aT[:, 0:2, :].rearrange("p a b -> p (a b)")
                for t in range(NT):
                    op("act", lambda e, t=t: e.activation(out=jk, in_=xg[:, xi, t, :], func=AF.Square,
                                                          accum_out=smv("ssf", t, t + 1)),
                       reads=[b_x[xi][t]], writes=[b_aT[0], b_aT[1], b_sm["ssf"]])
                rstd_from_ss("ssf", "msf", "rstdf", NT, 1.0 / D, EPS)
                for t in range(NT):
                    op("dve", lambda e, t=t: e.scalar_tensor_tensor(out=xg[:, xi, t, :], in0=xg[:, xi, t, :],
                                                                    scalar=smv("rstdf", t, t + 1), in1=pv("gfb"),
                                                                    op0=ALU.mult, op1=ALU.mult),
                       reads=[b_x[xi][t], b_sm["rstdf"], b_prm], writes=[b_x[xi][t]])
                    tk = S_.dma("sp", out_d[tok0 + t * 128:tok0 + (t + 1) * 128, :], xg[:, xi, t, :],
                                reads=[b_x[xi][t]], writes=[b_out])
                    out_toks.append(tk)
                if gi + 2 < ngrp:
                    load_x(groups[gi + 2][0], groups[gi + 2][1], xi)

            nop = lambda: None
            for half in range(2):
                for fc in range(12 * half, 12 * half + 12):
                    units.append((lambda fc=fc: gate_pe(fc), lambda fc=fc: gate_rest(fc), True))
                units.append((nop, lambda half=half: unit_gelu(half), False, 3))
                for fc in range(12 * half, 12 * half + 12):
                    units.append((lambda fc=fc: val_pe(fc), lambda fc=fc: val_rest(fc), True))
            if gi == DBG_G:
                units.append((nop, lambda: dump(3, aT[:, 0:8, :].rearrange("p a b -> p (a b)"), 4096, b_aT[0:8]), False))
            na = len(acc_banks)
            passes = [list(range(NT))[i:i + na] for i in range(0, NT, na)]
            for c2 in range(2):
                for ts in passes:
                    accs = acc_banks[:len(ts)]
                    for ks in range(3):
                        units.append((lambda c2=c2, ks=ks, ts=ts, accs=accs: unit_down(c2, ks, ts, accs), nop, False))
                    units.append((nop, lambda c2=c2, ts=ts, accs=accs: unit_down_evac(c2, ts, accs), False))
            units.append((nop, unit_final, False))
            return units

        PAIR_A = (0, 1)
        PAIR_B = (5, 6)

        def run_window(B, fu, extra=None, n_fill=None, SLOTS=(0, 1), DEPTH=1):
            extra = list(extra or [])
            nF = len(fu)
            fi = 0
            if B is None:
                for pe_, rest_, hoist_ in [u_[:3] for u_ in fu]:
                    pe_()
                    rest_()
                return
            users = B
            nU = len(users)
            st_ = {"pi": 0}
            NS = len(SLOTS)
            cumw = [0.0]
            for u_ in users:
                cumw.append(cumw[-1] + (u_[3] if len(u_) > 3 else 1.0))

            T_ = nU + 8
            if n_fill and NS == 2:
                T_ = n_fill + 2 + ((n_fill + 2) % 2)

            def slot(m):
                return SLOTS[m % NS] if m < T_ else (5, 0, 1)[(m - T_) % 3]

            def issue(limit, done):
                while st_["pi"] < nU and st_["pi"] <= limit:
                    m = st_["pi"]
                    if users[m][2] or m - 1 <= done:
                        users[m][0](slot(m))
                        st_["pi"] += 1
                    else:
                        break

            for cur in range(nU):
                dpt = 2 if cur >= T_ - 2 else DEPTH
                issue(cur + dpt, cur - 1)
                if extra:
                    extra.pop(0)()
                nfu = n_fill if n_fill else nU
                if cur + 1 >= nfu:
                    target = nF
                else:
                    target = int(math.ceil(cumw[cur + 1] / cumw[nfu] * nF - 1e-9))
                if any(len(u_) > 3 for u_ in fu[fi:target]):
                    target = min(nF, target + 3)
                batch = fu[fi:target]
                fi = max(fi, target)
                deferred = []
                for pe_, rest_, hoist_ in [u_[:3] for u_ in batch]:
                    if hoist_ and len(deferred) < 3:
                        pe_()
                        deferred.append(rest_)
                    else:
                        for r_ in deferred:
                            r_()
                        deferred = []
                        pe_()
                        if hoist_:
                            deferred.append(rest_)
                        else:
                            rest_()
                users[cur][1](slot(cur))
                for r_ in deferred:
                    r_()
                issue(cur + dpt, cur)
            while fi < nF:
                fu[fi][0]()
                fu[fi][1]()
                fi += 1
            for u in extra:
                u()

        load_x(groups[0][0], groups[0][1], 0)
        if ngrp > 1:
            load_x(groups[1][0], groups[1][1], 1)
        for gi in range(ngrp):
            stage_A(gi)
            fu = ffn_units(gi - 1, [5, 6, 7], [5, 6, 7]) if gi > 0 else []
            ub = make_B(gi)
            if gi == 0:
                run_window(ub + make_C(gi), fu, extra=deferred_casts, SLOTS=(0, 1, 5), DEPTH=2)
            else:
                run_window(ub + make_C(gi), fu, n_fill=max(1, len(ub) - 16))
        nbanks[0] = 8
        norm_transposes("g2T", hT2, b_hT2)
        run_window(None, ffn_units(ngrp - 1, [0, 1, 2, 3], [4, 5, 6, 7]))
        S_.wait_all("sp", out_toks)
        S_.emit(block)
    return nc


_CACHE = {}


def _run(inputs, nseq, S, ncores, dbg=False):
    pk, stp = pack_params(inputs)
    params = pk.build()
    stage = stp.build()
    key = (nseq, S)
    if key not in _CACHE:
        _CACHE[key] = build(nseq, S, pk.off, pk.n, stp.off, stp.n, dbg=dbg)
    nc = _CACHE[key]
    x = np.ascontiguousarray(np.asarray(inputs["x"], np.float32))
    f32c = lambda a: np.ascontiguousarray(np.asarray(a, np.float32))
    w_in = f32c(inputs["w_in"][0])
    w_out = f32c(inputs["w_out"][0])
    w_up = f32c(inputs["ffn_w_up"][0])
    w_dn = f32c(inputs["ffn_w_down"][0])
    in_maps = []
    for c in range(ncores):
        xs = x[c * nseq:(c + 1) * nseq].reshape(nseq * S, D)
        in_maps.append({"x": np.ascontiguousarray(xs), "w_in": w_in, "w_out": w_out, "w_up": w_up, "w_down": w_dn,
                        "params": params, "stage": stage})
    res = run_bass_kernel_spmd(nc, in_maps, core_ids=list(range(ncores)))
    outs = [np.asarray(r["out"], np.float32).reshape(nseq, S, D) for r in res.results]
    if dbg:
        global DBG_OUT
        DBG_OUT = np.asarray(res.results[0]["dbg"])
    return np.concatenate(outs, axis=0)


def kernel(**inputs):
    x = inputs["x"]
    B, S, _ = x.shape
    nseq = B // NCORES
    return _run(inputs, nseq, S, NCORES)
```

```python
import math
from contextlib import ExitStack

import numpy as np
import concourse.bass as bass
import concourse.mybir as mybir
from concourse.bass_utils import run_bass_kernel_spmd

F32 = mybir.dt.float32
BF16 = mybir.dt.bfloat16
AF = mybir.ActivationFunctionType
ALU = mybir.AluOpType
AX = mybir.AxisListType

NCORES = 8
D = 1024
G = 512
NT = 4
LAMBDA_INIT = 0.8 - 0.6 * math.exp(-0.3 * 0)
EPS = 1e-6
SUBLN_EPS = 1e-5
NRING = 3
DBG_G = 0


class Buf:
    __slots__ = ("name", "w", "r")

    def __init__(self, name):
        self.name = name
        self.w = None
        self.r = {}


class Sched:
    NDS = 32

    def __init__(self, nc, stack):
        self.nc = nc
        self.eng = {"pe": nc.tensor, "act": nc.scalar, "dve": nc.vector, "pool": nc.gpsimd, "sp": nc.sync}
        self.prog = {k: [] for k in self.eng}
        self.sems = {}
        for k in ["pe", "act", "dve", "pool"]:
            self.sems[("e", k)] = stack.enter_context(nc.semaphore("s_" + k))
        for i in range(self.NDS):
            self.sems[("d", i)] = stack.enter_context(nc.semaphore("d%d" % i))
        self.ecount = {k: 0 for k in self.eng}
        self.dcount = [0] * self.NDS
        self.dpool = {"sp": list(range(0, 24)), "pool": list(range(24, 32)), "act": list(range(24, 32))}
        self.dnext = {"sp": 0, "pool": 0, "act": 0}
        self.waited = {k: {} for k in self.eng}

    def _deps(self, eng, reads, writes):
        need = {}

        def add(s, v):
            if need.get(s, 0) < v:
                need[s] = v

        for b in reads:
            if b.w is not None:
                add(*b.w)
        for b in writes:
            if b.w is not None:
                add(*b.w)
            for s, v in b.r.items():
                add(s, v)
        out = []
        for s, v in need.items():
            if eng == "pe" and s == ("e", "pe"):
                continue
            if self.waited[eng].get(s, 0) >= v:
                continue
            self.waited[eng][s] = v
            out.append((s, v))
        return out

    def _update(self, reads, writes, tok):
        s, v = tok
        for b in reads:
            if b.r.get(s, 0) < v:
                b.r[s] = v
        for b in writes:
            b.w = tok
            b.r = {}

    def op(self, eng, fn, reads=(), writes=()):
        waits = self._deps(eng, reads, writes)
        self.ecount[eng] += 1
        tok = (("e", eng), self.ecount[eng])
        self.prog[eng].append((fn, waits, tok[0], 1))
        self._update(reads, writes, tok)
        return tok

    def dma(self, q, out, in_, reads=(), writes=(), **kw):
        waits = self._deps(q, reads, writes)
        pool_ = self.dpool[q]
        i = pool_[self.dnext[q] % len(pool_)]
        self.dnext[q] += 1
        s = ("d", i)
        prev = self.dcount[i]
        if prev and self.waited[q].get(s, 0) < prev:
            self.waited[q][s] = prev
            waits.append((s, prev))
        self.dcount[i] += 16
        tok = (s, self.dcount[i])
        self.prog[q].append((lambda e: e.dma_start(out=out, in_=in_, **kw), waits, s, 16))
        self._update(reads, writes, tok)
        return tok

    def wait_all(self, eng, toks):
        waits = []
        for s, v in toks:
            if self.waited[eng].get(s, 0) < v:
                self.waited[eng][s] = v
                waits.append((s, v))
        self.prog[eng].append((None, waits, None, 0))

    def emit(self, block):
        def mk(k):
            def body(e):
                for fn, waits, sem, inc in self.prog[k]:
                    for s, v in waits:
                        e.wait_ge(self.sems[s], v)
                    if fn is not None:
                        ins = fn(e)
                        ins.then_inc(self.sems[sem], inc)
            return body

        block.tensor(mk("pe"))
        block.scalar(mk("act"))
        block.vector(mk("dve"))
        block.gpsimd(mk("pool"))
        block.sync(mk("sp"))


class Pack:
    def __init__(self):
        self.cols = []
        self.off = {}
        self.n = 0

    def add(self, name, arr):
        arr = np.ascontiguousarray(arr, dtype=np.float32).reshape(128, -1)
        self.off[name] = (self.n, arr.shape[1])
        self.cols.append(arr)
        self.n += arr.shape[1]

    def build(self):
        return np.ascontiguousarray(np.concatenate(self.cols, axis=1))


def _rel_bucket_np(rel):
    half = 16
    max_exact = 8
    ret = (rel > 0).astype(np.int32) * half
    n = np.abs(rel)
    nf = np.maximum(n, 1).astype(np.float32)
    large = max_exact + (np.log(nf / max_exact) / math.log(128 / max_exact) * (half - max_exact)).astype(np.int32)
    large = np.minimum(large, half - 1)
    return ret + np.where(n < max_exact, n, large)


def _chunkT(v, nchunk):
    return np.asarray(v, np.float32).reshape(nchunk, 128).T


def pack_params(inp):
    pk = Pack()
    rep = lambda v: np.broadcast_to(np.asarray(v, np.float32).reshape(1, -1), (128, np.asarray(v).size))
    pk.add("g1T", _chunkT(inp["norm1_g"][0], 8))
    pk.add("g2T", _chunkT(inp["norm2_g"][0], 8))
    pk.add("gfb", rep(inp["final_norm_g"]))
    lcw = np.asarray(inp["lru_conv_w"][0], np.float32)
    pk.add("lcw", lcw.reshape(4, 4, 128).transpose(2, 1, 0))
    pk.add("lcb", _chunkT(inp["lru_conv_b"][0], 4))
    pk.add("lba", _chunkT(np.asarray(inp["lru_ba"][0]).reshape(-1), 4))
    pk.add("lbx", _chunkT(np.asarray(inp["lru_bx"][0]).reshape(-1), 4))
    pk.add("llam", _chunkT(inp["lru_lambda"][0], 4))
    fcw = np.asarray(inp["ffn_conv_w"][0], np.float32)
    pk.add("fcw", fcw.reshape(3, 24, 128).transpose(2, 1, 0))
    pk.add("fcb", _chunkT(inp["ffn_conv_b"][0], 24))
    pk.add("subg", np.asarray(inp["diff_subln_g"][0], np.float32).reshape(128, 1))
    rb = np.asarray(inp["rel_bias"], np.float32)
    pk.add("farb", rep(rb[15, :]))
    kk = np.arange(128)[:, None]
    qq = np.arange(128)[None, :]
    bt = np.zeros((128, 4, 256), np.float32)
    rel_d = kk - qq
    rel_s = kk - (qq + 128)
    bd = _rel_bucket_np(rel_d)
    bs = _rel_bucket_np(rel_s)
    masked = (kk // 64) > (qq // 64)
    for h in range(4):
        t = rb[bd, h].copy()
        t[masked] = -30000.0
        bt[:, h, 0:128] = t
        bt[:, h, 128:256] = rb[bs, h]
    pk.add("biasT", bt)
    st = Pack()
    for nm in ("lru_wa", "lru_wx"):
        w = np.asarray(inp[nm][0], np.float32)
        m = np.zeros((128, 4, 128), np.float32)
        for c in range(4):
            m[0:64, c, 0:64] = w[2 * c]
            m[64:128, c, 64:128] = w[2 * c + 1]
        st.add(nm, m)
    for nm in ("diff_lq1", "diff_lk1", "diff_lq2", "diff_lk2"):
        st.add(nm, rep(inp[nm][0]))
    st.add("ident", np.eye(128, dtype=np.float32))
    return pk, st


def build(nseq, S, pk_off, npar, st_off, nst, dbg=False):
    NG = S // G
    NKT = S // 128
    ntok = nseq * S
    nc = bass.Bass("TRN2", target_bir_lowering=False)
    x_d = nc.dram_tensor("x", [ntok, D], F32, kind="ExternalInput").ap()
    win_d = nc.dram_tensor("w_in", [D, 2560], F32, kind="ExternalInput").ap()
    wout_d = nc.dram_tensor("w_out", [D, D], F32, kind="ExternalInput").ap()
    wup_d = nc.dram_tensor("w_up", [D, 6144], F32, kind="ExternalInput").ap()
    wdn_d = nc.dram_tensor("w_down", [3072, D], F32, kind="ExternalInput").ap()
    par_d = nc.dram_tensor("params", [128, npar], F32, kind="ExternalInput").ap()
    stg_d = nc.dram_tensor("stage", [128, nst], F32, kind="ExternalInput").ap()
    out_d = nc.dram_tensor("out", [ntok, D], F32, kind="ExternalOutput").ap()
    NSLAB = 25
    wsl_d = nc.dram_tensor("wslab", [NSLAB, 128, 8, 512], BF16).ap()
    dbg_d = nc.dram_tensor("dbg", [8, 128, 4096], F32, kind="ExternalOutput").ap() if dbg else None

    with ExitStack() as st:
        S_ = Sched(nc, st)
        T = lambda name, shape, dt: st.enter_context(nc.sbuf_tensor(name, shape, dt))
        prm = T("prm", [128, npar], F32)
        ring = T("ring", [128, NRING, 8, 512], BF16)
        KT = T("KT", [128, 4, S], BF16)
        Va = T("Va", [128, NKT, 4, 130], BF16)
        xg = T("xg", [128, 2, NT, 1024], F32)
        hb = T("hb", [128, NT, 1024], BF16)
        hb2 = T("hb2", [128, NT, 1024], BF16)
        hT = T("hT", [128, 8, G], BF16)
        hT2 = T("hT2", [128, 8, G], BF16)
        qT = T("qT", [128, 4, G], BF16)
        mixT = T("mixT", [128, 8, G], BF16)
        aT = T("aT", [128, 24, G], BF16)
        gg = T("gg", [128, 4, G], BF16)
        bdw = T("bdw", [128, 2, 4, 128], BF16)
        ident = T("ident", [128, 128], BF16)
        sm = T("sm", [128, 96], F32)
        xc4 = T("xc4", [128, 4, G], F32)
        xcb = T("xcb", [128, G], BF16)
        thr = T("thr", [128, G], F32)
        thi = T("thi", [128, G], F32)
        la2 = T("la2", [128, G], F32)
        lh = T("lh", [128, G], F32)
        lhalo = T("lhalo", [128, 4, 4], F32)
        lstate = T("lstate", [128, 4], F32)
        PT = T("PT", [128, 2, G], BF16)
        sbn = T("sbn", [128, 3, 256], F32)
        acp = T("acp", [128, 8, 129], F32)
        t1 = T("t1", [128, 128], F32)
        ob = T("ob", [128, NT, 4, 128], BF16)
        facc = T("facc", [128, 2, G], F32)
        fhalo = T("fhalo", [128, 24, 2], F32)
        banks = [st.enter_context(nc.psum_tensor("pb%d" % i, [128, 512], F32)) for i in range(8)]
        block = st.enter_context(nc.Block())

        def pv(name):
            o, n = pk_off[name]
            return prm[:, o:o + n]

        def sv(name):
            o, n = st_off[name]
            return xg[:, 1, 0:2, :].rearrange("p a b -> p (a b)")[:, o:o + n]

        SM = {}
        smn = [0]

        def smalloc(name, n):
            SM[name] = (smn[0], n)
            smn[0] += n
            assert smn[0] <= 96

        def smv(name, a=0, b=None):
            o, n = SM[name]
            if b is None:
                b = n
            return sm[:, o + a:o + b]

        for name, n in [("lam", 1), ("neglam", 1), ("e1", 1), ("e2", 1), ("s1", 1), ("s2", 1), ("cneg", 4),
                        ("nba", 4), ("nbx", 4), ("c2neg", 4), ("dummy", 1), ("subg2", 1), ("mhalf", 1), ("half", 1), ("ss", 4), ("ms", 4), ("rstd", 4),
                        ("rs", 8), ("rs2n", 4), ("sso", 4), ("mso", 4), ("rstdo", 4), ("tmp4", 4), ("ssf", 4), ("msf", 4), ("rstdf", 4), ("ss2", 2)]:
            smalloc(name, n)

        b_prm = Buf("prm")
        b_ring = [Buf("ring%d" % i) for i in range(NRING)]
        b_slabd = [[Buf("slabd%d_%d" % (i, k)) for k in range(2)] for i in range(NSLAB)]
        b_x = [[Buf("x%d_%d" % (i, t)) for t in range(NT)] for i in range(2)]
        b_stage_l = b_x[1][0:2]
        b_hb = [Buf("hb%d" % t) for t in range(NT)]
        b_hb2 = [Buf("hb2_%d" % t) for t in range(NT)]
        b_hT = [Buf("hT%d" % k) for k in range(8)]
        b_hT2 = [Buf("hT2_%d" % k) for k in range(8)]
        b_acp = Buf("acp")
        b_qT = [Buf("qT%d" % h) for h in range(4)]
        b_KT = [[Buf("KT%d_%d" % (h, g)) for g in range(NG)] for h in range(4)]
        b_V = [Buf("V%d" % j) for j in range(NKT)]
        b_mix = [Buf("mix%d" % k) for k in range(8)]
        b_aT = [Buf("aT%d" % k) for k in range(24)]
        b_gg = [Buf("gg%d" % c) for c in range(4)]
        b_ps = [Buf("ps%d" % i) for i in range(8)]
        b_sm = {k: Buf("sm_" + k) for k in SM}
        b_bdw = Buf("bdw")
        b_ident = Buf("ident")
        b_xcb, b_thr, b_thi, b_la, b_la2, b_lu, b_lh = [Buf(n) for n in "xcb thr thi la la2 lu lh".split()]
        b_xc = [Buf("xc%d" % c) for c in range(4)]
        b_lhalo = [Buf("lhalo%d" % c) for c in range(4)]
        b_lstate = [Buf("lstate%d" % c) for c in range(4)]
        b_PT = [Buf("PT0"), Buf("PT1")]
        b_sbn = [Buf("sbn0"), Buf("sbn1"), Buf("sbn2")]
        b_t1 = Buf("t1")
        b_ob = [[Buf("ob%d_%d" % (i, h)) for h in range(4)] for i in range(NT)]
        b_facc = [Buf("facc0"), Buf("facc1")]
        b_fgl = [Buf("fgl0"), Buf("fgl1")]
        b_fhalo = [Buf("fhalo%d" % k) for k in range(24)]
        b_out = Buf("outd")

        op = S_.op
        out_toks = []

        def dump(slot, ap2d, n, bufs):
            if dbg_d is None:
                return
            out_toks.append(S_.dma("pool", dbg_d[slot, :, 0:n], ap2d, reads=bufs, writes=[Buf("dbgw")]))

        rot = [0]
        nbanks = [4]

        def tbank():
            i = rot[0] % nbanks[0]
            rot[0] = i + 1
            return i

        S_.dma("sp", prm[:], par_d, writes=[b_prm])
        S_.dma("sp", xg[:, 1, 0:2, :].rearrange("p a b -> p (a b)")[:, 0:nst], stg_d, writes=b_stage_l)

        op("pool", lambda e: e.memset(smv("mhalf"), -0.5), writes=[b_sm["mhalf"]])
        op("pool", lambda e: e.memset(smv("half"), 0.5), writes=[b_sm["half"]])
        op("pool", lambda e: e.memset(Va[:, :, :, 128:130], 1.0), writes=b_V)
        def cast_slab(si, parts):
            for pi, (c0, ncol, src) in enumerate(parts):
                S_.dma("pool", wsl_d[si, :, :, c0:c0 + ncol], src.rearrange("(kk p) c -> p kk c", p=128),
                       writes=[b_slabd[si][pi]])

        SL_IN, SL_OUT, SL_UP, SL_DN = 0, 5, 7, 19
        for i in range(5):
            cast_slab(SL_IN + i, [(0, 512, win_d[:, i * 512:(i + 1) * 512])])
        deferred_casts = []
        _DEFER = False
        for i in range(2):
            deferred_casts.append(lambda i=i: cast_slab(SL_OUT + i, [(0, 512, wout_d[:, i * 512:(i + 1) * 512])]))
        for s_ in range(6):
            deferred_casts.append(lambda s_=s_: cast_slab(SL_UP + s_, [(0, 512, wup_d[:, s_ * 512:(s_ + 1) * 512])]))
        for s_ in range(6):
            deferred_casts.append(lambda s_=s_: cast_slab(SL_UP + 6 + s_, [(0, 512, wup_d[:, 3072 + s_ * 512:3072 + (s_ + 1) * 512])]))
        for c2 in range(2):
            for ks in range(3):
                deferred_casts.append(lambda c2=c2, ks=ks: cast_slab(
                    SL_DN + c2 * 3 + ks, [(0, 512, wdn_d[ks * 1024:(ks + 1) * 1024, c2 * 512:(c2 + 1) * 512])]))

        if not _DEFER:
            for dc_ in deferred_casts:
                dc_()
            deferred_casts = []
        op("dve", lambda e: e.tensor_copy(out=ident[:], in_=sv("ident")), reads=b_stage_l, writes=[b_ident])
        op("dve", lambda e: e.tensor_copy(out=bdw[:, 0, :, :], in_=sv("lru_wa").rearrange("p (c m) -> p c m", c=4)),
           reads=b_stage_l, writes=[b_bdw])
        op("dve", lambda e: e.tensor_copy(out=bdw[:, 1, :, :], in_=sv("lru_wx").rearrange("p (c m) -> p c m", c=4)),
           reads=b_stage_l, writes=[b_bdw])
        op("dve", lambda e: e.tensor_tensor(out=t1[:, 0:64], in0=sv("diff_lq1"), in1=sv("diff_lk1"), op=ALU.mult),
           reads=b_stage_l, writes=[b_t1])
        op("dve", lambda e: e.tensor_reduce(out=smv("s1"), in_=t1[:, 0:64], axis=AX.X, op=ALU.add),
           reads=[b_t1], writes=[b_sm["s1"]])
        op("dve", lambda e: e.tensor_tensor(out=t1[:, 64:128], in0=sv("diff_lq2"), in1=sv("diff_lk2"), op=ALU.mult),
           reads=b_stage_l, writes=[b_t1])
        op("dve", lambda e: e.tensor_reduce(out=smv("s2"), in_=t1[:, 64:128], axis=AX.X, op=ALU.add),
           reads=[b_t1], writes=[b_sm["s2"]])
        op("act", lambda e: e.activation(out=smv("e1"), in_=smv("s1"), func=AF.Exp), reads=[b_sm["s1"]], writes=[b_sm["e1"]])
        op("act", lambda e: e.activation(out=smv("e2"), in_=smv("s2"), func=AF.Exp), reads=[b_sm["s2"]], writes=[b_sm["e2"]])
        op("dve", lambda e: e.tensor_tensor(out=smv("lam"), in0=smv("e1"), in1=smv("e2"), op=ALU.subtract),
           reads=[b_sm["e1"], b_sm["e2"]], writes=[b_sm["lam"]])
        op("dve", lambda e: e.tensor_scalar(out=smv("neglam"), in0=smv("lam"), scalar1=-1.0, scalar2=-LAMBDA_INIT,
                                            op0=ALU.mult, op1=ALU.add),
           reads=[b_sm["lam"]], writes=[b_sm["neglam"]])
        op("act", lambda e: e.activation(out=smv("tmp4"), in_=pv("llam"), func=AF.Exp, scale=-1.0),
           reads=[b_prm], writes=[b_sm["tmp4"]])
        op("act", lambda e: e.activation(out=smv("tmp4"), in_=smv("tmp4"), func=AF.Ln, bias=1.0, scale=1.0),
           reads=[b_sm["tmp4"]], writes=[b_sm["tmp4"]])
        op("dve", lambda e: e.tensor_scalar(out=smv("cneg"), in0=smv("tmp4"), scalar1=-8.0, scalar2=None, op0=ALU.mult),
           reads=[b_sm["tmp4"]], writes=[b_sm["cneg"]])
        op("dve", lambda e: e.tensor_scalar(out=smv("c2neg"), in0=smv("tmp4"), scalar1=-16.0, scalar2=None, op0=ALU.mult),
           reads=[b_sm["tmp4"]], writes=[b_sm["c2neg"]])
        op("dve", lambda e: e.tensor_scalar(out=smv("nba"), in0=pv("lba"), scalar1=-1.0, scalar2=None, op0=ALU.mult),
           reads=[b_prm], writes=[b_sm["nba"]])
        op("dve", lambda e: e.tensor_scalar(out=smv("nbx"), in0=pv("lbx"), scalar1=-1.0, scalar2=None, op0=ALU.mult),
           reads=[b_prm], writes=[b_sm["nbx"]])
        op("dve", lambda e: e.tensor_scalar(out=smv("subg2"), in0=pv("subg"), scalar1=1.0 - LAMBDA_INIT, scalar2=None,
                                            op0=ALU.mult),
           reads=[b_prm], writes=[b_sm["subg2"]])

        ring_i = [0]

        def load_slab(si):
            r = ring_i[0]
            ring_i[0] = (r + 1) % NRING
            S_.dma("sp", ring[:, r, :, :], wsl_d[si], reads=b_slabd[si], writes=[b_ring[r]])
            return r

        def load_x(seq, g, xi):
            tok0 = seq * S + g * G
            for t in range(NT):
                S_.dma("sp", xg[:, xi, t, :], x_d[tok0 + t * 128:tok0 + (t + 1) * 128, :], writes=[b_x[xi][t]])

        evac_flip = [0]

        def evac_eng():
            evac_flip[0] ^= 1
            return "act" if evac_flip[0] else "dve"

        def rstd_from_ss(ssn, msn, rsn, n, inv_n, eps, use_act=False):
            op("dve", lambda e: e.tensor_scalar(out=smv(msn, 0, n), in0=smv(ssn, 0, n), scalar1=inv_n, scalar2=eps,
                                                op0=ALU.mult, op1=ALU.add),
               reads=[b_sm[ssn]], writes=[b_sm[msn]])
            if use_act:
                op("act", lambda e: e.activation(out=smv(msn, 0, n), in_=smv(msn, 0, n), func=AF.Ln),
                   reads=[b_sm[msn]], writes=[b_sm[msn]])
                op("act", lambda e: e.activation(out=smv(rsn, 0, n), in_=smv(msn, 0, n), func=AF.Exp, scale=-0.5),
                   reads=[b_sm[msn]], writes=[b_sm[rsn]])
                return
            op("pool", lambda e: e.tensor_tensor(out=smv(rsn, 0, n), in0=smv(msn, 0, n),
                                                 in1=smv("mhalf").to_broadcast([128, n]), op=ALU.pow),
               reads=[b_sm[msn], b_sm["mhalf"]], writes=[b_sm[rsn]])

        def norm_chain(xi, hb=hb, b_hb=b_hb):
            for t in range(NT):
                if t < 2:
                    op("act", lambda e, t=t: e.activation(out=hb[:, t, :], in_=xg[:, xi, t, :], func=AF.Square,
                                                          accum_out=smv("ss", t, t + 1)),
                       reads=[b_x[xi][t]], writes=[b_hb[t], b_sm["ss"]])
                else:
                    op("dve", lambda e, t=t: e.scalar_tensor_tensor(out=hb[:, t, :], in0=xg[:, xi, t, :], scalar=1.0,
                                                                    in1=xg[:, xi, t, :], op0=ALU.mult, op1=ALU.mult,
                                                                    accum_out=smv("ss2", t - 2, t - 1)),
                       reads=[b_x[xi][t]], writes=[b_hb[t], b_sm["ss2"]])
            op("dve", lambda e: e.tensor_scalar(out=smv("ms", 0, 2), in0=smv("ss", 0, 2), scalar1=1.0 / D, scalar2=EPS,
                                                op0=ALU.mult, op1=ALU.add),
               reads=[b_sm["ss"]], writes=[b_sm["ms"]])
            op("dve", lambda e: e.tensor_scalar(out=smv("ms", 2, 4), in0=smv("ss2", 0, 2), scalar1=1.0 / D, scalar2=EPS,
                                                op0=ALU.mult, op1=ALU.add),
               reads=[b_sm["ss2"]], writes=[b_sm["ms"]])
            op("act", lambda e: e.activation(out=smv("ms"), in_=smv("ms"), func=AF.Ln), reads=[b_sm["ms"]],
               writes=[b_sm["ms"]])
            op("act", lambda e: e.activation(out=smv("rstd"), in_=smv("ms"), func=AF.Exp, scale=-0.5),
               reads=[b_sm["ms"]], writes=[b_sm["rstd"]])
            for t in range(NT):
                if t < 2:
                    op("act", lambda e, t=t: e.activation(out=hb[:, t, :], in_=xg[:, xi, t, :], func=AF.Identity,
                                                          scale=smv("rstd", t, t + 1)),
                       reads=[b_x[xi][t], b_sm["rstd"]], writes=[b_hb[t]])
                else:
                    op("dve", lambda e, t=t: e.tensor_scalar(out=hb[:, t, :], in0=xg[:, xi, t, :],
                                                             scalar1=smv("rstd", t, t + 1), scalar2=None, op0=ALU.mult),
                       reads=[b_x[xi][t], b_sm["rstd"]], writes=[b_hb[t]])
        def norm_tr_P(k, bi, hb=hb, b_hb=b_hb):
            pbf = banks[bi][:].bitcast(BF16)

            def f(e):
                for t in range(NT):
                    last = e.transpose(pbf[:, t * 128:(t + 1) * 128], hb[:, t, k * 128:(k + 1) * 128], ident[:])
                return last

            op("pe", f, reads=b_hb + [b_ident], writes=[b_ps[bi]])

        def norm_tr_C(k, bi, gname, dstT, b_dst):
            pbf = banks[bi][:].bitcast(BF16)
            gcol = pv(gname)[:, k:k + 1]
            if evac_eng() == "act":
                op("act", lambda e: e.activation(out=dstT[:, k, :], in_=pbf[:, 0:G], func=AF.Identity, scale=gcol),
                   reads=[b_ps[bi], b_prm], writes=[b_dst[k]])
            else:
                op("dve", lambda e: e.tensor_scalar(out=dstT[:, k, :], in0=pbf[:, 0:G], scalar1=gcol, scalar2=None,
                                                    op0=ALU.mult),
                   reads=[b_ps[bi], b_prm], writes=[b_dst[k]])

        def norm_transposes(gname, dstT, b_dst, hb=hb, b_hb=b_hb):
            for k in range(8):
                bi = tbank()
                norm_tr_P(k, bi, hb, b_hb)
                norm_tr_C(k, bi, gname, dstT, b_dst)

        def mm_fm(bi, r, c, src, b_src):
            def f(e):
                for kk in range(8):
                    last = e.matmul(banks[bi][:], lhsT=ring[:, r, kk, c * 128:(c + 1) * 128], rhs=src[:, kk, :],
                                    start=(kk == 0), stop=(kk == 7))
                return last

            op("pe", f, reads=[b_ring[r]] + list(b_src), writes=[b_ps[bi]])

        def conv_from_psum(bi, acc, b_acc, w_ap, b_ap, ntap, halo, b_halo, first, last_out=None, b_last=None):
            ps = banks[bi]
            nh = ntap - 1
            op("act", lambda e: e.activation(out=acc, in_=ps[:], func=AF.Identity, scale=w_ap[:, nh:nh + 1], bias=b_ap),
               reads=[b_ps[bi], b_prm], writes=[b_acc])
            for sft in range(1, ntap):
                j = nh - sft
                if last_out is not None and sft == nh:
                    op("dve", lambda e, sft=sft, j=j: e.scalar_tensor_tensor(out=last_out[:, sft:G], in0=ps[:, 0:G - sft],
                                                                             scalar=w_ap[:, j:j + 1], in1=acc[:, sft:G],
                                                                             op0=ALU.mult, op1=ALU.add),
                       reads=[b_ps[bi], b_prm, b_acc], writes=[b_last])
                else:
                    op("dve", lambda e, sft=sft, j=j: e.scalar_tensor_tensor(out=acc[:, sft:G], in0=ps[:, 0:G - sft],
                                                                             scalar=w_ap[:, j:j + 1], in1=acc[:, sft:G],
                                                                             op0=ALU.mult, op1=ALU.add),
                       reads=[b_ps[bi], b_prm], writes=[b_acc])
            if not first:
                for sft in range(1, ntap):
                    j = nh - sft
                    op("dve", lambda e, sft=sft, j=j: e.scalar_tensor_tensor(out=acc[:, 0:sft], in0=halo[:, nh - sft:nh],
                                                                             scalar=w_ap[:, j:j + 1], in1=acc[:, 0:sft],
                                                                             op0=ALU.mult, op1=ALU.add),
                       reads=[b_halo, b_prm], writes=[b_acc])
            if last_out is not None:
                op("dve", lambda e: e.tensor_copy(out=last_out[:, 0:nh], in_=acc[:, 0:nh]),
                   reads=[b_acc], writes=[b_last])
            op("dve", lambda e: e.tensor_copy(out=halo[:, 0:nh], in_=ps[:, G - nh:G]),
               reads=[b_ps[bi]], writes=[b_halo])

        ngrp = nseq * NG
        groups = [(sq, g) for sq in range(nseq) for g in range(NG)]

        def stage_A(gi):
            seq, g = groups[gi]
            xi = gi % 2
            first = (g == 0)
            nbanks[0] = 8
            if gi == 0:
                norm_chain(xi, hb2, b_hb2)
            norm_transposes("g1T", hT, b_hT, hb2, b_hb2)
            if gi > 0:
                norm_transposes("g2T", hT2, b_hT2)
            if gi == 0:
                dump(0, hT[:, :, :].rearrange("p a b -> p (a b)"), 4096, b_hT)
            r = load_slab(SL_IN + 1)
            for c in range(4):
                bg = tbank()
                mm_fm(bg, r, c, hT, b_hT)
                op("act", lambda e, c=c, bg=bg: e.activation(out=gg[:, c, :], in_=banks[bg][:], func=AF.Gelu_apprx_tanh),
                   reads=[b_ps[bg]], writes=[b_gg[c]])
            op("act", lambda e: e.activation(out=smv("dummy"), in_=smv("half"), func=AF.Ln),
               reads=[b_sm["half"]], writes=[b_sm["dummy"]])
            r = load_slab(SL_IN + 0)
            for c in range(4):
                bi = tbank()
                mm_fm(bi, r, c, hT, b_hT)
                w_ap = pv("lcw")[:, c * 4:(c + 1) * 4]
                conv_from_psum(bi, xc4[:, c, :], b_xc[c], w_ap, pv("lcb")[:, c:c + 1], 4, lhalo[:, c, :], b_lhalo[c], first)
            r = load_slab(SL_IN + 2)
            for h in range(4):
                bi = tbank()
                mm_fm(bi, r, h, hT, b_hT)
                if evac_eng() == "act":
                    op("act", lambda e, h=h, bi=bi: e.activation(out=qT[:, h, :], in_=banks[bi][:], func=AF.Copy,
                                                                 scale=0.125),
                       reads=[b_ps[bi]], writes=[b_qT[h]])
                else:
                    op("dve", lambda e, h=h, bi=bi: e.tensor_scalar(out=qT[:, h, :], in0=banks[bi][:], scalar1=0.125,
                                                                    scalar2=None, op0=ALU.mult),
                       reads=[b_ps[bi]], writes=[b_qT[h]])
            r = load_slab(SL_IN + 3)
            for h in range(4):
                bi = tbank()
                mm_fm(bi, r, h, hT, b_hT)
                dst = KT[:, h, g * G:(g + 1) * G]
                if evac_eng() == "act":
                    op("act", lambda e, dst=dst, bi=bi: e.activation(out=dst, in_=banks[bi][:], func=AF.Copy),
                       reads=[b_ps[bi]], writes=[b_KT[h][g]])
                else:
                    op("dve", lambda e, dst=dst, bi=bi: e.tensor_copy(out=dst, in_=banks[bi][:]),
                       reads=[b_ps[bi]], writes=[b_KT[h][g]])
            r = load_slab(SL_IN + 4)
            for t in range(NT):
                bi = tbank()
                jt = g * NT + t

                def f(e, t=t, bi=bi, r=r):
                    for kk in range(8):
                        last = e.matmul(banks[bi][:], lhsT=hT[:, kk, t * 128:(t + 1) * 128], rhs=ring[:, r, kk, :],
                                        start=(kk == 0), stop=(kk == 7))
                    return last

                op("pe", f, reads=[b_ring[r]] + b_hT, writes=[b_ps[bi]])
                src = banks[bi][:].rearrange("p (h e) -> p h e", h=4)
                dst = Va[:, jt, :, 0:128]
                if evac_eng() == "act":
                    op("act", lambda e, dst=dst, src=src: e.activation(out=dst, in_=src, func=AF.Copy),
                       reads=[b_ps[bi]], writes=[b_V[jt]])
                else:
                    op("dve", lambda e, dst=dst, src=src: e.tensor_copy(out=dst, in_=src),
                       reads=[b_ps[bi]], writes=[b_V[jt]])

        ACC0 = 2

        def acc_region(il, c):
            rr = c * 4 + il
            return ACC0 + rr // 3, (rr % 3) * 130, rr

        def make_B(gi):
            seq, g = groups[gi]
            first = (g == 0)
            nj = g * NT + NT
            info = {}
            aux = "dve" if gi == 0 else "pool"

            def lru_P(c, which, bk):
                if which == 0:
                    op(aux, lambda e: e.tensor_copy(out=xcb[:], in_=xc4[:, c, :]), reads=[b_xc[c]], writes=[b_xcb])
                op("pe", lambda e: e.matmul(banks[bk][:], lhsT=bdw[:, which, c, :], rhs=xcb[:], start=True, stop=True),
                   reads=[b_bdw, b_xcb], writes=[b_ps[bk]])

            def lru_C(c, which, bk):
                if which == 0:
                    op("act", lambda e: e.activation(out=thr[:], in_=banks[bk][:], func=AF.Exp, scale=-1.0,
                                                     bias=smv("nba", c, c + 1)),
                       reads=[b_ps[bk], b_sm["nba"]], writes=[b_thr])
                    op("act", lambda e: e.activation(out=thr[:], in_=thr[:], func=AF.Ln, bias=1.0, scale=1.0),
                       reads=[b_thr], writes=[b_thr])
                    op("act", lambda e: e.activation(out=thr[:], in_=thr[:], func=AF.Exp, scale=-1.0),
                       reads=[b_thr], writes=[b_thr])
                    op("act", lambda e: e.activation(out=la2[:], in_=thr[:], func=AF.Exp, scale=smv("c2neg", c, c + 1)),
                       reads=[b_thr, b_sm["c2neg"]], writes=[b_la2])
                    op("act", lambda e: e.activation(out=thr[:], in_=thr[:], func=AF.Exp, scale=smv("cneg", c, c + 1)),
                       reads=[b_thr, b_sm["cneg"]], writes=[b_thr])
                    op("act", lambda e: e.activation(out=la2[:], in_=la2[:], func=AF.Ln, bias=1.0, scale=-1.0),
                       reads=[b_la2], writes=[b_la2])
                    op("act", lambda e: e.activation(out=la2[:], in_=la2[:], func=AF.Exp, scale=0.5),
                       reads=[b_la2], writes=[b_la2])
                    return
                op("act", lambda e: e.activation(out=thi[:], in_=banks[bk][:], func=AF.Exp, scale=-1.0,
                                                 bias=smv("nbx", c, c + 1)),
                   reads=[b_ps[bk], b_sm["nbx"]], writes=[b_thi])
                op("act", lambda e: e.activation(out=thi[:], in_=thi[:], func=AF.Ln, bias=1.0, scale=1.0),
                   reads=[b_thi], writes=[b_thi])
                op("act", lambda e: e.activation(out=thi[:], in_=thi[:], func=AF.Exp, scale=-1.0),
                   reads=[b_thi], writes=[b_thi])
                op(aux, lambda e: e.tensor_tensor(out=thi[:], in0=thi[:], in1=xc4[:, c, :], op=ALU.mult),
                   reads=[b_thi, b_xc[c]], writes=[b_thi])
                op("dve", lambda e: e.tensor_tensor(out=thi[:], in0=thi[:], in1=la2[:], op=ALU.mult),
                   reads=[b_thi, b_la2], writes=[b_thi])
                init = 0.0 if first else lstate[:, c:c + 1]
                op("dve", lambda e: e.tensor_tensor_scan(out=lh[:], data0=thr[:], data1=thi[:], initial=init,
                                                         op0=ALU.mult, op1=ALU.add),
                   reads=[b_thr, b_thi, b_lstate[c]], writes=[b_lh])
                op("dve", lambda e: e.tensor_copy(out=lstate[:, c:c + 1], in_=lh[:, G - 1:G]),
                   reads=[b_lh], writes=[b_lstate[c]])
                op(aux, lambda e: e.tensor_tensor(out=mixT[:, c, :], in0=gg[:, c, :], in1=lh[:], op=ALU.mult),
                   reads=[b_gg[c], b_lh], writes=[b_mix[c]])

            def emit_qk(key, bk, p):
                h, j, c = key
                i0 = max(0, j - g * NT)
                N = (NT - i0) * 128
                q0 = i0 * 128
                gk = j // NT
                op("pe", lambda e: e.matmul(banks[bk][:, 0:N], lhsT=KT[c * 64:(c + 1) * 64, h, j * 128:(j + 1) * 128],
                                            rhs=qT[c * 64:(c + 1) * 64, h, q0:G], start=True, stop=True),
                   reads=[b_KT[h][gk], b_qT[h]], writes=[b_ps[bk]])
                d0 = g * NT + i0 - j
                if d0 == 0:
                    boff, wn = 0, (256 if i0 < NT - 1 else 128)
                elif d0 == 1:
                    boff, wn = 128, 128
                else:
                    boff, wn = 0, 0
                if wn:
                    bo, nn = pk_off["biasT"]
                    bias_ap = prm[:, bo + h * 256 + boff: bo + h * 256 + boff + wn]
                    op("dve", lambda e: e.tensor_tensor(out=sbn[:, p, 0:wn], in0=banks[bk][:, 0:wn], in1=bias_ap,
                                                        op=ALU.add),
                       reads=[b_ps[bk], b_prm], writes=[b_sbn[p]])
                info[key] = (i0, N, wn)

            def emit_softmax_pv(key, bk, p, p3):
                h, j, c = key
                i0, N, wn = info.pop(key)
                if wn:
                    op("act", lambda e: e.activation(out=PT[:, p, 0:wn], in_=sbn[:, p3, 0:wn], func=AF.Exp),
                       reads=[b_sbn[p3]], writes=[b_PT[p]])
                if N > wn:
                    fo, fn_ = pk_off["farb"]
                    op("act", lambda e: e.activation(out=PT[:, p, wn:N], in_=banks[bk][:, wn:N], func=AF.Exp,
                                                     bias=prm[:, fo + h:fo + h + 1]),
                       reads=[b_ps[bk], b_prm], writes=[b_PT[p]])

                def f(e):
                    last = None
                    for il in range(i0, NT):
                        qi = g * NT + il
                        abk, off, rr = acc_region(il, c)
                        last = e.matmul(banks[abk][:, off:off + 129],
                                        lhsT=PT[:, p, (il - i0) * 128:(il - i0 + 1) * 128],
                                        rhs=Va[:, j, h, 0:129], start=(j == 0 and rr % 3 == 0), stop=(j == qi),
                                        skip_group_check=True)
                    return last

                wb = sorted(set(acc_region(il, c)[0] for il in range(i0, NT)))
                op("pe", f, reads=[b_PT[p], b_V[j]], writes=[b_ps[k_] for k_ in wb])

            def emit_norm(h):
                for abk, r0, nr in ((ACC0, 0, 3), (ACC0 + 1, 3, 3), (ACC0 + 2, 6, 2)):
                    src = banks[abk][:, 0:nr * 130].rearrange("p (r w) -> p r w", w=130)[:, :, 0:129]
                    op("dve", lambda e, src=src, r0=r0, nr=nr: e.tensor_copy(out=acp[:, r0:r0 + nr, :], in_=src),
                       reads=[b_ps[abk]], writes=[b_acp])
                op("dve", lambda e: e.reciprocal(out=smv("rs").rearrange("p (r o) -> p r o", o=1),
                                                 in_=acp[:, :, 128:129]),
                   reads=[b_acp], writes=[b_sm["rs"]])
                op("dve", lambda e: e.tensor_scalar(out=smv("rs2n"), in0=smv("rs", 4, 8), scalar1=smv("neglam"),
                                                    scalar2=None, op0=ALU.mult),
                   reads=[b_sm["rs"], b_sm["neglam"]], writes=[b_sm["rs2n"]])
                O1 = acp[:, 0:4, 0:128]
                O2 = acp[:, 4:8, 0:128]
                rs1b = smv("rs", 0, 4).unsqueeze(2).to_broadcast([128, NT, 128])
                rs2b = smv("rs2n").unsqueeze(2).to_broadcast([128, NT, 128])
                op("dve", lambda e: e.tensor_tensor(out=O1, in0=O1, in1=rs1b, op=ALU.mult),
                   reads=[b_acp, b_sm["rs"]], writes=[b_acp])
                op("dve", lambda e: e.tensor_tensor(out=O2, in0=O2, in1=rs2b, op=ALU.mult),
                   reads=[b_acp, b_sm["rs2n"]], writes=[b_acp])
                op("dve", lambda e: e.tensor_tensor(out=O1, in0=O1, in1=O2, op=ALU.add),
                   reads=[b_acp], writes=[b_acp])
                op("dve", lambda e: e.tensor_tensor(out=O2, in0=O1, in1=O1, op=ALU.mult),
                   reads=[b_acp], writes=[b_acp])
                op("dve", lambda e: e.tensor_reduce(out=smv("sso"), in_=O2, axis=AX.X, op=ALU.add),
                   reads=[b_acp], writes=[b_sm["sso"]])
                rstd_from_ss("sso", "mso", "rstdo", NT, 1.0 / 128.0, SUBLN_EPS, use_act=(gi == 0))
                rsob = smv("rstdo").unsqueeze(2).to_broadcast([128, NT, 128])
                op("dve", lambda e: e.tensor_tensor(out=ob[:, :, h, :], in0=O1, in1=rsob, op=ALU.mult),
                   reads=[b_acp, b_sm["rstdo"]], writes=[b_ob[il][h] for il in range(NT)])
            users = []
            for h in range(4):
                users.append(("lru", h, 0, True))
                users.append(("lru", h, 1, True))
                for j in range(nj):
                    for c in range(2):
                        users.append(("qk", h, (h, j, c), True))
                users.append(("norm", h, None, True))

            def emitP(n, bk):
                kind, h, key, _ = users[n]
                if kind == "lru":
                    lru_P(h, key, bk)
                elif kind == "qk":
                    emit_qk(key, bk, n % 3)


            def emitC(n, bk):
                kind, h, key, _ = users[n]
                if kind == "lru":
                    lru_C(h, key, bk)
                elif kind == "qk":
                    emit_softmax_pv(key, bk, n % 2, n % 3)
                else:
                    emit_norm(h)

            def weight(n):
                kind, h, key, _ = users[n]
                if kind == "lru":
                    return 4.0 if key == 0 else 2.5
                if kind == "norm":
                    return 3.0
                hh, j, c = key
                return 1.6 if j >= g * NT - 1 else 1.0

            return [(lambda bk, n=n: emitP(n, bk), lambda bk, n=n: emitC(n, bk), users[n][3], weight(n))
                    for n in range(len(users))]

        def make_C(gi):
            xi = gi % 2
            ul = []
            nop1 = lambda bk: None

            def tr_P(h, bk):
                pbf = banks[bk][:].bitcast(BF16)

                def f(e):
                    for il in range(NT):
                        last = e.transpose(pbf[:, il * 128:(il + 1) * 128], ob[:, il, h, :], ident[:])
                    return last

                op("pe", f, reads=[b_ob[il][h] for il in range(NT)] + [b_ident], writes=[b_ps[bk]])

            def tr_C(h, bk):
                pbf = banks[bk][:].bitcast(BF16)
                op("act", lambda e: e.activation(out=mixT[:, 4 + h, :], in_=pbf[:, 0:G], func=AF.Identity,
                                                 scale=smv("subg2")),
                   reads=[b_ps[bk], b_sm["subg2"]], writes=[b_mix[4 + h]])

            for h in range(4):
                ul.append((lambda bk, h=h: tr_P(h, bk), lambda bk, h=h: tr_C(h, bk), True))
            if gi + 1 < ngrp:
                ul.append((nop1, lambda bk: norm_chain(1 - xi, hb2, b_hb2), True))
            cur = {}

            def wo_P(c2, t, bk):
                if t == 0:
                    cur["r"] = load_slab(SL_OUT + c2)
                r = cur["r"]

                def f(e):
                    for kk in range(8):
                        last = e.matmul(banks[bk][:], lhsT=mixT[:, kk, t * 128:(t + 1) * 128], rhs=ring[:, r, kk, :],
                                        start=(kk == 0), stop=(kk == 7))
                    return last

                op("pe", f, reads=[b_ring[r]] + b_mix, writes=[b_ps[bk]])

            def wo_C(c2, t, bk):
                xs = xg[:, xi, t, c2 * 512:(c2 + 1) * 512]
                op("dve", lambda e: e.tensor_tensor(out=xs, in0=banks[bk][:], in1=xs, op=ALU.add),
                   reads=[b_ps[bk], b_x[xi][t]], writes=[b_x[xi][t]])

            for c2 in range(2):
                for t in range(NT):
                    ul.append((lambda bk, c2=c2, t=t: wo_P(c2, t, bk), lambda bk, c2=c2, t=t: wo_C(c2, t, bk),
                               not (c2 == 0 and t == 0)))
            ul.append((nop1, lambda bk: norm_chain(xi), True))
            return ul

        def ffn_units(gi, up_banks, acc_banks):
            seq, g = groups[gi]
            xi = gi % 2
            tok0 = seq * S + g * G
            first = (g == 0)
            units = []
            rot_u = [0]

            def ubank():
                bk = up_banks[rot_u[0] % len(up_banks)]
                rot_u[0] += 1
                return bk

            cur = {}

            def gate_pe(fc):
                if fc % 4 == 0:
                    cur["r"] = load_slab(SL_UP + fc // 4)
                r = cur["r"]
                bG = ubank()
                cur[("g", fc)] = bG
                mm_fm(bG, r, fc % 4, hT2, b_hT2)

            def gate_rest(fc):
                bG = cur.pop(("g", fc))
                cc = fc % 2
                w_ap = pv("fcw")[:, fc * 3:(fc + 1) * 3]
                conv_from_psum(bG, facc[:, cc, :], b_facc[cc], w_ap, pv("fcb")[:, fc:fc + 1], 3, fhalo[:, fc, :],
                               b_fhalo[fc], first, last_out=aT[:, fc, :], b_last=b_aT[fc])

            def unit_gelu(half):
                for k4 in range(3 * half, 3 * half + 3):
                    op("act", lambda e, k4=k4: e.activation(out=aT[:, 4 * k4:4 * k4 + 4, :], in_=aT[:, 4 * k4:4 * k4 + 4, :],
                                                            func=AF.Gelu_apprx_tanh),
                       reads=b_aT[4 * k4:4 * k4 + 4], writes=b_aT[4 * k4:4 * k4 + 4])
                op("act", lambda e: e.activation(out=smv("dummy"), in_=smv("half"), func=AF.Ln),
                   reads=[b_sm["half"]], writes=[b_sm["dummy"]])

            def val_pe(fc):
                if fc % 4 == 0:
                    cur["r"] = load_slab(SL_UP + 6 + fc // 4)
                r = cur["r"]
                bV = ubank()
                cur[("v", fc)] = bV
                mm_fm(bV, r, fc % 4, hT2, b_hT2)

            def val_rest(fc):
                bV = cur.pop(("v", fc))
                op("dve", lambda e: e.tensor_tensor(out=aT[:, fc, :], in0=banks[bV][:], in1=aT[:, fc, :], op=ALU.mult),
                   reads=[b_ps[bV], b_aT[fc]], writes=[b_aT[fc]])

            def unit_down(c2, ks, ts, accs):
                r = load_slab(SL_DN + c2 * 3 + ks)
                for t, bk in zip(ts, accs):
                    def f(e, t=t, bk=bk):
                        for kk in range(8):
                            last = e.matmul(banks[bk][:], lhsT=aT[:, ks * 8 + kk, t * 128:(t + 1) * 128],
                                            rhs=ring[:, r, kk, :], start=(ks == 0 and kk == 0),
                                            stop=(ks == 2 and kk == 7))
                        return last

                    op("pe", f, reads=[b_ring[r]] + b_aT[ks * 8:(ks + 1) * 8], writes=[b_ps[bk]])

            def unit_down_evac(c2, ts, accs):
                for t, bk in zip(ts, accs):
                    xs = xg[:, xi, t, c2 * 512:(c2 + 1) * 512]
                    op("dve", lambda e, xs=xs, bk=bk: e.tensor_tensor(out=xs, in0=banks[bk][:], in1=xs, op=ALU.add),
                       reads=[b_ps[bk], b_x[xi][t]], writes=[b_x[xi][t]])

            def unit_final():
                jk = aT[:, 0:2, :].rearrange("p a b -> p (a b)")
                for t in range(NT):
                    op("act", lambda e, t=t: e.activation(out=jk, in_=xg[:, xi, t, :], func=AF.Square,
                                                          accum_out=smv("ssf", t, t + 1)),
                       reads=[b_x[xi][t]], writes=[b_aT[0], b_aT[1], b_sm["ssf"]])
                rstd_from_ss("ssf", "msf", "rstdf", NT, 1.0 / D, EPS)
                for t in range(NT):
                    op("dve", lambda e, t=t: e.scalar_tensor_tensor(out=xg[:, xi, t, :], in0=xg[:, xi, t, :],
                                                                    scalar=smv("rstdf", t, t + 1), in1=pv("gfb"),
                                                                    op0=ALU.mult, op1=ALU.mult),
                       reads=[b_x[xi][t], b_sm["rstdf"], b_prm], writes=[b_x[xi][t]])
                    tk = S_.dma("sp", out_d[tok0 + t * 128:tok0 + (t + 1) * 128, :], xg[:, xi, t, :],
                                reads=[b_x[xi][t]], writes=[b_out])
                    out_toks.append(tk)
                if gi + 2 < ngrp:
                    load_x(groups[gi + 2][0], groups[gi + 2][1], xi)

            nop = lambda: None
            for half in range(2):
                for fc in range(12 * half, 12 * half + 12):
                    units.append((lambda fc=fc: gate_pe(fc), lambda fc=fc: gate_rest(fc), True))
                units.append((nop, lambda half=half: unit_gelu(half), False, 3))
                for fc in range(12 * half, 12 * half + 12):
                    units.append((lambda fc=fc: val_pe(fc), lambda fc=fc: val_rest(fc), True))
            if gi == DBG_G:
                units.append((nop, lambda: dump(3, aT[:, 0:8, :].rearrange("p a b -> p (a b)"), 4096, b_aT[0:8]), False))
            na = len(acc_banks)
            passes = [list(range(NT))[i:i + na] for i in range(0, NT, na)]
            for c2 in range(2):
                for ts in passes:
                    accs = acc_banks[:len(ts)]
                    for ks in range(3):
                        units.append((lambda c2=c2, ks=ks, ts=ts, accs=accs: unit_down(c2, ks, ts, accs), nop, False))
                    units.append((nop, lambda c2=c2, ts=ts, accs=accs: unit_down_evac(c2, ts, accs), False))
            units.append((nop, unit_final, False))
            return units

        PAIR_A = (0, 1)
        PAIR_B = (5, 6)

        def run_window(B, fu, extra=None, n_fill=None, SLOTS=(0, 1), DEPTH=1):
            extra = list(extra or [])
            nF = len(fu)
            fi = 0
            if B is None:
                for pe_, rest_, hoist_ in [u_[:3] for u_ in fu]:
                    pe_()
                    rest_()
                return
            users = B
            nU = len(users)
            st_ = {"pi": 0}
            NS = len(SLOTS)
            cumw = [0.0]
            for u_ in users:
                cumw.append(cumw[-1] + (u_[3] if len(u_) > 3 else 1.0))

            T_ = nU + 8
            if n_fill and NS == 2:
                T_ = n_fill + 2 + ((n_fill + 2) % 2)

            def slot(m):
                return SLOTS[m % NS] if m < T_ else (5, 0, 1)[(m - T_) % 3]

            def issue(limit, done):
                while st_["pi"] < nU and st_["pi"] <= limit:
                    m = st_["pi"]
                    if users[m][2] or m - 1 <= done:
                        users[m][0](slot(m))
                        st_["pi"] += 1
                    else:
                        break

            for cur in range(nU):
                dpt = 2 if cur >= T_ - 2 else DEPTH
                issue(cur + dpt, cur - 1)
                if extra:
                    extra.pop(0)()
                nfu = n_fill if n_fill else nU
                if cur + 1 >= nfu:
                    target = nF
                else:
                    target = int(math.ceil(cumw[cur + 1] / cumw[nfu] * nF - 1e-9))
                if any(len(u_) > 3 for u_ in fu[fi:target]):
                    target = min(nF, target + 3)
                batch = fu[fi:target]
                fi = max(fi, target)
                deferred = []
                for pe_, rest_, hoist_ in [u_[:3] for u_ in batch]:
                    if hoist_ and len(deferred) < 3:
                        pe_()
                        deferred.append(rest_)
                    else:
                        for r_ in deferred:
                            r_()
                        deferred = []
                        pe_()
                        if hoist_:
                            deferred.append(rest_)
                        else:
                            rest_()
                users[cur][1](slot(cur))
                for r_ in deferred:
                    r_()
                issue(cur + dpt, cur)
            while fi < nF:
                fu[fi][0]()
                fu[fi][1]()
                fi += 1
            for u in extra:
                u()

        load_x(groups[0][0], groups[0][1], 0)
        if ngrp > 1:
            load_x(groups[1][0], groups[1][1], 1)
        for gi in range(ngrp):
            stage_A(gi)
            fu = ffn_units(gi - 1, [5, 6, 7], [5, 6, 7]) if gi > 0 else []
            ub = make_B(gi)
            if gi == 0:
                run_window(ub + make_C(gi), fu, extra=deferred_casts, SLOTS=(0, 1, 5), DEPTH=2)
            else:
                run_window(ub + make_C(gi), fu, n_fill=max(1, len(ub) - 16))
        nbanks[0] = 8
        norm_transposes("g2T", hT2, b_hT2)
        run_window(None, ffn_units(ngrp - 1, [0, 1, 2, 3], [4, 5, 6, 7]))
        S_.wait_all("sp", out_toks)
        S_.emit(block)
    return nc


_CACHE = {}


def _run(inputs, nseq, S, ncores, dbg=False):
    pk, stp = pack_params(inputs)
    params = pk.build()
    stage = stp.build()
    key = (nseq, S)
    if key not in _CACHE:
        _CACHE[key] = build(nseq, S, pk.off, pk.n, stp.off, stp.n, dbg=dbg)
    nc = _CACHE[key]
    x = np.ascontiguousarray(np.asarray(inputs["x"], np.float32))
    f32c = lambda a: np.ascontiguousarray(np.asarray(a, np.float32))
    w_in = f32c(inputs["w_in"][0])
    w_out = f32c(inputs["w_out"][0])
    w_up = f32c(inputs["ffn_w_up"][0])
    w_dn = f32c(inputs["ffn_w_down"][0])
    in_maps = []
    for c in range(ncores):
        xs = x[c * nseq:(c + 1) * nseq].reshape(nseq * S, D)
        in_maps.append({"x": np.ascontiguousarray(xs), "w_in": w_in, "w_out": w_out, "w_up": w_up, "w_down": w_dn,
                        "params": params, "stage": stage})
    res = run_bass_kernel_spmd(nc, in_maps, core_ids=list(range(ncores)))
    outs = [np.asarray(r["out"], np.float32).reshape(nseq, S, D) for r in res.results]
    if dbg:
        global DBG_OUT
        DBG_OUT = np.asarray(res.results[0]["dbg"])
    return np.concatenate(outs, axis=0)


def kernel(**inputs):
    x = inputs["x"]
    B, S, _ = x.shape
    nseq = B // NCORES
    return _run(inputs, nseq, S, NCORES)
```

```python
import math
from contextlib import ExitStack

import numpy as np
import concourse.bass as bass
import concourse.mybir as mybir
from concourse.bass_utils import run_bass_kernel_spmd

F32 = mybir.dt.float32
BF16 = mybir.dt.bfloat16
AF = mybir.ActivationFunctionType
ALU = mybir.AluOpType
AX = mybir.AxisListType

NCORES = 8
D = 1024
G = 512
NT = 4
LAMBDA_INIT = 0.8 - 0.6 * math.exp(-0.3 * 0)
EPS = 1e-6
SUBLN_EPS = 1e-5
NRING = 3
DBG_G = 0


class Buf:
    __slots__ = ("name", "w", "r")

    def __init__(self, name):
        self.name = name
        self.w = None
        self.r = {}


class Sched:
    NDS = 32

    def __init__(self, nc, stack):
        self.nc = nc
        self.eng = {"pe": nc.tensor, "act": nc.scalar, "dve": nc.vector, "pool": nc.gpsimd, "sp": nc.sync}
        self.prog = {k: [] for k in self.eng}
        self.sems = {}
        for k in ["pe", "act", "dve", "pool"]:
            self.sems[("e", k)] = stack.enter_context(nc.semaphore("s_" + k))
        for i in range(self.NDS):
            self.sems[("d", i)] = stack.enter_context(nc.semaphore("d%d" % i))
        self.ecount = {k: 0 for k in self.eng}
        self.dcount = [0] * self.NDS
        self.dpool = {"sp": list(range(0, 24)), "pool": list(range(24, 32)), "act": list(range(24, 32))}
        self.dnext = {"sp": 0, "pool": 0, "act": 0}
        self.waited = {k: {} for k in self.eng}

    def _deps(self, eng, reads, writes):
        need = {}

        def add(s, v):
            if need.get(s, 0) < v:
                need[s] = v

        for b in reads:
            if b.w is not None:
                add(*b.w)
        for b in writes:
            if b.w is not None:
                add(*b.w)
            for s, v in b.r.items():
                add(s, v)
        out = []
        for s, v in need.items():
            if eng == "pe" and s == ("e", "pe"):
                continue
            if self.waited[eng].get(s, 0) >= v:
                continue
            self.waited[eng][s] = v
            out.append((s, v))
        return out

    def _update(self, reads, writes, tok):
        s, v = tok
        for b in reads:
            if b.r.get(s, 0) < v:
                b.r[s] = v
        for b in writes:
            b.w = tok
            b.r = {}

    def op(self, eng, fn, reads=(), writes=()):
        waits = self._deps(eng, reads, writes)
        self.ecount[eng] += 1
        tok = (("e", eng), self.ecount[eng])
        self.prog[eng].append((fn, waits, tok[0], 1))
        self._update(reads, writes, tok)
        return tok

    def dma(self, q, out, in_, reads=(), writes=(), **kw):
        waits = self._deps(q, reads, writes)
        pool_ = self.dpool[q]
        i = pool_[self.dnext[q] % len(pool_)]
        self.dnext[q] += 1
        s = ("d", i)
        prev = self.dcount[i]
        if prev and self.waited[q].get(s, 0) < prev:
            self.waited[q][s] = prev
            waits.append((s, prev))
        self.dcount[i] += 16
        tok = (s, self.dcount[i])
        self.prog[q].append((lambda e: e.dma_start(out=out, in_=in_, **kw), waits, s, 16))
        self._update(reads, writes, tok)
        return tok

    def wait_all(self, eng, toks):
        waits = []
        for s, v in toks:
            if self.waited[eng].get(s, 0) < v:
                self.waited[eng][s] = v
                waits.append((s, v))
        self.prog[eng].append((None, waits, None, 0))

    def emit(self, block):
        def mk(k):
            def body(e):
                for fn, waits, sem, inc in self.prog[k]:
                    for s, v in waits:
                        e.wait_ge(self.sems[s], v)
                    if fn is not None:
                        ins = fn(e)
                        ins.then_inc(self.sems[sem], inc)
            return body

        block.tensor(mk("pe"))
        block.scalar(mk("act"))
        block.vector(mk("dve"))
        block.gpsimd(mk("pool"))
        block.sync(mk("sp"))


class Pack:
    def __init__(self):
        self.cols = []
        self.off = {}
        self.n = 0

    def add(self, name, arr):
        arr = np.ascontiguousarray(arr, dtype=np.float32).reshape(128, -1)
        self.off[name] = (self.n, arr.shape[1])
        self.cols.append(arr)
        self.n += arr.shape[1]

    def build(self):
        return np.ascontiguousarray(np.concatenate(self.cols, axis=1))


def _rel_bucket_np(rel):
    half = 16
    max_exact = 8
    ret = (rel > 0).astype(np.int32) * half
    n = np.abs(rel)
    nf = np.maximum(n, 1).astype(np.float32)
    large = max_exact + (np.log(nf / max_exact) / math.log(128 / max_exact) * (half - max_exact)).astype(np.int32)
    large = np.minimum(large, half - 1)
    return ret + np.where(n < max_exact, n, large)


def _chunkT(v, nchunk):
    return np.asarray(v, np.float32).reshape(nchunk, 128).T


def pack_params(inp):
    pk = Pack()
    rep = lambda v: np.broadcast_to(np.asarray(v, np.float32).reshape(1, -1), (128, np.asarray(v).size))
    pk.add("g1T", _chunkT(inp["norm1_g"][0], 8))
    pk.add("g2T", _chunkT(inp["norm2_g"][0], 8))
    pk.add("gfb", rep(inp["final_norm_g"]))
    lcw = np.asarray(inp["lru_conv_w"][0], np.float32)
    pk.add("lcw", lcw.reshape(4, 4, 128).transpose(2, 1, 0))
    pk.add("lcb", _chunkT(inp["lru_conv_b"][0], 4))
    pk.add("lba", _chunkT(np.asarray(inp["lru_ba"][0]).reshape(-1), 4))
    pk.add("lbx", _chunkT(np.asarray(inp["lru_bx"][0]).reshape(-1), 4))
    pk.add("llam", _chunkT(inp["lru_lambda"][0], 4))
    fcw = np.asarray(inp["ffn_conv_w"][0], np.float32)
    pk.add("fcw", fcw.reshape(3, 24, 128).transpose(2, 1, 0))
    pk.add("fcb", _chunkT(inp["ffn_conv_b"][0], 24))
    pk.add("subg", np.asarray(inp["diff_subln_g"][0], np.float32).reshape(128, 1))
    rb = np.asarray(inp["rel_bias"], np.float32)
    pk.add("farb", rep(rb[15, :]))
    kk = np.arange(128)[:, None]
    qq = np.arange(128)[None, :]
    bt = np.zeros((128, 4, 256), np.float32)
    rel_d = kk - qq
    rel_s = kk - (qq + 128)
    bd = _rel_bucket_np(rel_d)
    bs = _rel_bucket_np(rel_s)
    masked = (kk // 64) > (qq // 64)
    for h in range(4):
        t = rb[bd, h].copy()
        t[masked] = -30000.0
        bt[:, h, 0:128] = t
        bt[:, h, 128:256] = rb[bs, h]
    pk.add("biasT", bt)
    st = Pack()
    for nm in ("lru_wa", "lru_wx"):
        w = np.asarray(inp[nm][0], np.float32)
        m = np.zeros((128, 4, 128), np.float32)
        for c in range(4):
            m[0:64, c, 0:64] = w[2 * c]
            m[64:128, c, 64:128] = w[2 * c + 1]
        st.add(nm, m)
    for nm in ("diff_lq1", "diff_lk1", "diff_lq2", "diff_lk2"):
        st.add(nm, rep(inp[nm][0]))
    st.add("ident", np.eye(128, dtype=np.float32))
    return pk, st


def build(nseq, S, pk_off, npar, st_off, nst, dbg=False):
    NG = S // G
    NKT = S // 128
    ntok = nseq * S
    nc = bass.Bass("TRN2", target_bir_lowering=False)
    x_d = nc.dram_tensor("x", [ntok, D], F32, kind="ExternalInput").ap()
    win_d = nc.dram_tensor("w_in", [D, 2560], F32, kind="ExternalInput").ap()
    wout_d = nc.dram_tensor("w_out", [D, D], F32, kind="ExternalInput").ap()
    wup_d = nc.dram_tensor("w_up", [D, 6144], F32, kind="ExternalInput").ap()
    wdn_d = nc.dram_tensor("w_down", [3072, D], F32, kind="ExternalInput").ap()
    par_d = nc.dram_tensor("params", [128, npar], F32, kind="ExternalInput").ap()
    stg_d = nc.dram_tensor("stage", [128, nst], F32, kind="ExternalInput").ap()
    out_d = nc.dram_tensor("out", [ntok, D], F32, kind="ExternalOutput").ap()
    NSLAB = 25
    wsl_d = nc.dram_tensor("wslab", [NSLAB, 128, 8, 512], BF16).ap()
    dbg_d = nc.dram_tensor("dbg", [8, 128, 4096], F32, kind="ExternalOutput").ap() if dbg else None

    with ExitStack() as st:
        S_ = Sched(nc, st)
        T = lambda name, shape, dt: st.enter_context(nc.sbuf_tensor(name, shape, dt))
        prm = T("prm", [128, npar], F32)
        ring = T("ring", [128, NRING, 8, 512], BF16)
        KT = T("KT", [128, 4, S], BF16)
        Va = T("Va", [128, NKT, 4, 130], BF16)
        xg = T("xg", [128, 2, NT, 1024], F32)
        hb = T("hb", [128, NT, 1024], BF16)
        hb2 = T("hb2", [128, NT, 1024], BF16)
        hT = T("hT", [128, 8, G], BF16)
        hT2 = T("hT2", [128, 8, G], BF16)
        qT = T("qT", [128, 4, G], BF16)
        mixT = T("mixT", [128, 8, G], BF16)
        aT = T("aT", [128, 24, G], BF16)
        gg = T("gg", [128, 4, G], BF16)
        bdw = T("bdw", [128, 2, 4, 128], BF16)
        ident = T("ident", [128, 128], BF16)
        sm = T("sm", [128, 96], F32)
        xc4 = T("xc4", [128, 4, G], F32)
        xcb = T("xcb", [128, G], BF16)
        thr = T("thr", [128, G], F32)
        thi = T("thi", [128, G], F32)
        la2 = T("la2", [128, G], F32)
        lh = T("lh", [128, G], F32)
        lhalo = T("lhalo", [128, 4, 4], F32)
        lstate = T("lstate", [128, 4], F32)
        PT = T("PT", [128, 2, G], BF16)
        sbn = T("sbn", [128, 3, 256], F32)
        acp = T("acp", [128, 8, 129], F32)
        t1 = T("t1", [128, 128], F32)
        ob = T("ob", [128, NT, 4, 128], BF16)
        facc = T("facc", [128, 2, G], F32)
        fhalo = T("fhalo", [128, 24, 2], F32)
        banks = [st.enter_context(nc.psum_tensor("pb%d" % i, [128, 512], F32)) for i in range(8)]
        block = st.enter_context(nc.Block())

        def pv(name):
            o, n = pk_off[name]
            return prm[:, o:o + n]

        def sv(name):
            o, n = st_off[name]
            return xg[:, 1, 0:2, :].rearrange("p a b -> p (a b)")[:, o:o + n]

        SM = {}
        smn = [0]

        def smalloc(name, n):
            SM[name] = (smn[0], n)
            smn[0] += n
            assert smn[0] <= 96

        def smv(name, a=0, b=None):
            o, n = SM[name]
            if b is None:
                b = n
            return sm[:, o + a:o + b]

        for name, n in [("lam", 1), ("neglam", 1), ("e1", 1), ("e2", 1), ("s1", 1), ("s2", 1), ("cneg", 4),
                        ("nba", 4), ("nbx", 4), ("c2neg", 4), ("dummy", 1), ("subg2", 1), ("mhalf", 1), ("half", 1), ("ss", 4), ("ms", 4), ("rstd", 4),
                        ("rs", 8), ("rs2n", 4), ("sso", 4), ("mso", 4), ("rstdo", 4), ("tmp4", 4), ("ssf", 4), ("msf", 4), ("rstdf", 4), ("ss2", 2)]:
            smalloc(name, n)

        b_prm = Buf("prm")
        b_ring = [Buf("ring%d" % i) for i in range(NRING)]
        b_slabd = [[Buf("slabd%d_%d" % (i, k)) for k in range(2)] for i in range(NSLAB)]
        b_x = [[Buf("x%d_%d" % (i, t)) for t in range(NT)] for i in range(2)]
        b_stage_l = b_x[1][0:2]
        b_hb = [Buf("hb%d" % t) for t in range(NT)]
        b_hb2 = [Buf("hb2_%d" % t) for t in range(NT)]
        b_hT = [Buf("hT%d" % k) for k in range(8)]
        b_hT2 = [Buf("hT2_%d" % k) for k in range(8)]
        b_acp = Buf("acp")
        b_qT = [Buf("qT%d" % h) for h in range(4)]
        b_KT = [[Buf("KT%d_%d" % (h, g)) for g in range(NG)] for h in range(4)]
        b_V = [Buf("V%d" % j) for j in range(NKT)]
        b_mix = [Buf("mix%d" % k) for k in range(8)]
        b_aT = [Buf("aT%d" % k) for k in range(24)]
        b_gg = [Buf("gg%d" % c) for c in range(4)]
        b_ps = [Buf("ps%d" % i) for i in range(8)]
        b_sm = {k: Buf("sm_" + k) for k in SM}
        b_bdw = Buf("bdw")
        b_ident = Buf("ident")
        b_xcb, b_thr, b_thi, b_la, b_la2, b_lu, b_lh = [Buf(n) for n in "xcb thr thi la la2 lu lh".split()]
        b_xc = [Buf("xc%d" % c) for c in range(4)]
        b_lhalo = [Buf("lhalo%d" % c) for c in range(4)]
        b_lstate = [Buf("lstate%d" % c) for c in range(4)]
        b_PT = [Buf("PT0"), Buf("PT1")]
        b_sbn = [Buf("sbn0"), Buf("sbn1"), Buf("sbn2")]
        b_t1 = Buf("t1")
        b_ob = [[Buf("ob%d_%d" % (i, h)) for h in range(4)] for i in range(NT)]
        b_facc = [Buf("facc0"), Buf("facc1")]
        b_fgl = [Buf("fgl0"), Buf("fgl1")]
        b_fhalo = [Buf("fhalo%d" % k) for k in range(24)]
        b_out = Buf("outd")

        op = S_.op
        out_toks = []

        def dump(slot, ap2d, n, bufs):
            if dbg_d is None:
                return
            out_toks.append(S_.dma("pool", dbg_d[slot, :, 0:n], ap2d, reads=bufs, writes=[Buf("dbgw")]))

        rot = [0]
        nbanks = [4]

        def tbank():
            i = rot[0] % nbanks[0]
            rot[0] = i + 1
            return i

        S_.dma("sp", prm[:], par_d, writes=[b_prm])
        S_.dma("sp", xg[:, 1, 0:2, :].rearrange("p a b -> p (a b)")[:, 0:nst], stg_d, writes=b_stage_l)

        op("pool", lambda e: e.memset(smv("mhalf"), -0.5), writes=[b_sm["mhalf"]])
        op("pool", lambda e: e.memset(smv("half"), 0.5), writes=[b_sm["half"]])
        op("pool", lambda e: e.memset(Va[:, :, :, 128:130], 1.0), writes=b_V)
        def cast_slab(si, parts):
            for pi, (c0, ncol, src) in enumerate(parts):
                S_.dma("pool", wsl_d[si, :, :, c0:c0 + ncol], src.rearrange("(kk p) c -> p kk c", p=128),
                       writes=[b_slabd[si][pi]])

        SL_IN, SL_OUT, SL_UP, SL_DN = 0, 5, 7, 19
        for i in range(5):
            cast_slab(SL_IN + i, [(0, 512, win_d[:, i * 512:(i + 1) * 512])])
        deferred_casts = []
        _DEFER = False
        for i in range(2):
            deferred_casts.append(lambda i=i: cast_slab(SL_OUT + i, [(0, 512, wout_d[:, i * 512:(i + 1) * 512])]))
        for s_ in range(6):
            deferred_casts.append(lambda s_=s_: cast_slab(SL_UP + s_, [(0, 512, wup_d[:, s_ * 512:(s_ + 1) * 512])]))
        for s_ in range(6):
            deferred_casts.append(lambda s_=s_: cast_slab(SL_UP + 6 + s_, [(0, 512, wup_d[:, 3072 + s_ * 512:3072 + (s_ + 1) * 512])]))
        for c2 in range(2):
            for ks in range(3):
                deferred_casts.append(lambda c2=c2, ks=ks: cast_slab(
                    SL_DN + c2 * 3 + ks, [(0, 512, wdn_d[ks * 1024:(ks + 1) * 1024, c2 * 512:(c2 + 1) * 512])]))

        if not _DEFER:
            for dc_ in deferred_casts:
                dc_()
            deferred_casts = []
        op("dve", lambda e: e.tensor_copy(out=ident[:], in_=sv("ident")), reads=b_stage_l, writes=[b_ident])
        op("dve", lambda e: e.tensor_copy(out=bdw[:, 0, :, :], in_=sv("lru_wa").rearrange("p (c m) -> p c m", c=4)),
           reads=b_stage_l, writes=[b_bdw])
        op("dve", lambda e: e.tensor_copy(out=bdw[:, 1, :, :], in_=sv("lru_wx").rearrange("p (c m) -> p c m", c=4)),
           reads=b_stage_l, writes=[b_bdw])
        op("dve", lambda e: e.tensor_tensor(out=t1[:, 0:64], in0=sv("diff_lq1"), in1=sv("diff_lk1"), op=ALU.mult),
           reads=b_stage_l, writes=[b_t1])
        op("dve", lambda e: e.tensor_reduce(out=smv("s1"), in_=t1[:, 0:64], axis=AX.X, op=ALU.add),
           reads=[b_t1], writes=[b_sm["s1"]])
        op("dve", lambda e: e.tensor_tensor(out=t1[:, 64:128], in0=sv("diff_lq2"), in1=sv("diff_lk2"), op=ALU.mult),
           reads=b_stage_l, writes=[b_t1])
        op("dve", lambda e: e.tensor_reduce(out=smv("s2"), in_=t1[:, 64:128], axis=AX.X, op=ALU.add),
           reads=[b_t1], writes=[b_sm["s2"]])
        op("act", lambda e: e.activation(out=smv("e1"), in_=smv("s1"), func=AF.Exp), reads=[b_sm["s1"]], writes=[b_sm["e1"]])
        op("act", lambda e: e.activation(out=smv("e2"), in_=smv("s2"), func=AF.Exp), reads=[b_sm["s2"]], writes=[b_sm["e2"]])
        op("dve", lambda e: e.tensor_tensor(out=smv("lam"), in0=smv("e1"), in1=smv("e2"), op=ALU.subtract),
           reads=[b_sm["e1"], b_sm["e2"]], writes=[b_sm["lam"]])
        op("dve", lambda e: e.tensor_scalar(out=smv("neglam"), in0=smv("lam"), scalar1=-1.0, scalar2=-LAMBDA_INIT,
                                            op0=ALU.mult, op1=ALU.add),
           reads=[b_sm["lam"]], writes=[b_sm["neglam"]])
        op("act", lambda e: e.activation(out=smv("tmp4"), in_=pv("llam"), func=AF.Exp, scale=-1.0),
           reads=[b_prm], writes=[b_sm["tmp4"]])
        op("act", lambda e: e.activation(out=smv("tmp4"), in_=smv("tmp4"), func=AF.Ln, bias=1.0, scale=1.0),
           reads=[b_sm["tmp4"]], writes=[b_sm["tmp4"]])
        op("dve", lambda e: e.tensor_scalar(out=smv("cneg"), in0=smv("tmp4"), scalar1=-8.0, scalar2=None, op0=ALU.mult),
           reads=[b_sm["tmp4"]], writes=[b_sm["cneg"]])
        op("dve", lambda e: e.tensor_scalar(out=smv("c2neg"), in0=smv("tmp4"), scalar1=-16.0, scalar2=None, op0=ALU.mult),
           reads=[b_sm["tmp4"]], writes=[b_sm["c2neg"]])
        op("dve", lambda e: e.tensor_scalar(out=smv("nba"), in0=pv("lba"), scalar1=-1.0, scalar2=None, op0=ALU.mult),
           reads=[b_prm], writes=[b_sm["nba"]])
        op("dve", lambda e: e.tensor_scalar(out=smv("nbx"), in0=pv("lbx"), scalar1=-1.0, scalar2=None, op0=ALU.mult),
           reads=[b_prm], writes=[b_sm["nbx"]])
        op("dve", lambda e: e.tensor_scalar(out=smv("subg2"), in0=pv("subg"), scalar1=1.0 - LAMBDA_INIT, scalar2=None,
                                            op0=ALU.mult),
           reads=[b_prm], writes=[b_sm["subg2"]])

        ring_i = [0]

        def load_slab(si):
            r = ring_i[0]
            ring_i[0] = (r + 1) % NRING
            S_.dma("sp", ring[:, r, :, :], wsl_d[si], reads=b_slabd[si], writes=[b_ring[r]])
            return r

        def load_x(seq, g, xi):
            tok0 = seq * S + g * G
            for t in range(NT):
                S_.dma("sp", xg[:, xi, t, :], x_d[tok0 + t * 128:tok0 + (t + 1) * 128, :], writes=[b_x[xi][t]])

        evac_flip = [0]

        def evac_eng():
            evac_flip[0] ^= 1
            return "act" if evac_flip[0] else "dve"

        def rstd_from_ss(ssn, msn, rsn, n, inv_n, eps, use_act=False):
            op("dve", lambda e: e.tensor_scalar(out=smv(msn, 0, n), in0=smv(ssn, 0, n), scalar1=inv_n, scalar2=eps,
                                                op0=ALU.mult, op1=ALU.add),
               reads=[b_sm[ssn]], writes=[b_sm[msn]])
            if use_act:
                op("act", lambda e: e.activation(out=smv(msn, 0, n), in_=smv(msn, 0, n), func=AF.Ln),
                   reads=[b_sm[msn]], writes=[b_sm[msn]])
                op("act", lambda e: e.activation(out=smv(rsn, 0, n), in_=smv(msn, 0, n), func=AF.Exp, scale=-0.5),
                   reads=[b_sm[msn]], writes=[b_sm[rsn]])
                return
            op("pool", lambda e: e.tensor_tensor(out=smv(rsn, 0, n), in0=smv(msn, 0, n),
                                                 in1=smv("mhalf").to_broadcast([128, n]), op=ALU.pow),
               reads=[b_sm[msn], b_sm["mhalf"]], writes=[b_sm[rsn]])

        def norm_chain(xi, hb=hb, b_hb=b_hb):
            for t in range(NT):
                if t < 2:
                    op("act", lambda e, t=t: e.activation(out=hb[:, t, :], in_=xg[:, xi, t, :], func=AF.Square,
                                                          accum_out=smv("ss", t, t + 1)),
                       reads=[b_x[xi][t]], writes=[b_hb[t], b_sm["ss"]])
                else:
                    op("dve", lambda e, t=t: e.scalar_tensor_tensor(out=hb[:, t, :], in0=xg[:, xi, t, :], scalar=1.0,
                                                                    in1=xg[:, xi, t, :], op0=ALU.mult, op1=ALU.mult,
                                                                    accum_out=smv("ss2", t - 2, t - 1)),
                       reads=[b_x[xi][t]], writes=[b_hb[t], b_sm["ss2"]])
            op("dve", lambda e: e.tensor_scalar(out=smv("ms", 0, 2), in0=smv("ss", 0, 2), scalar1=1.0 / D, scalar2=EPS,
                                                op0=ALU.mult, op1=ALU.add),
               reads=[b_sm["ss"]], writes=[b_sm["ms"]])
            op("dve", lambda e: e.tensor_scalar(out=smv("ms", 2, 4), in0=smv("ss2", 0, 2), scalar1=1.0 / D, scalar2=EPS,
                                                op0=ALU.mult, op1=ALU.add),
               reads=[b_sm["ss2"]], writes=[b_sm["ms"]])
            op("act", lambda e: e.activation(out=smv("ms"), in_=smv("ms"), func=AF.Ln), reads=[b_sm["ms"]],
               writes=[b_sm["ms"]])
            op("act", lambda e: e.activation(out=smv("rstd"), in_=smv("ms"), func=AF.Exp, scale=-0.5),
               reads=[b_sm["ms"]], writes=[b_sm["rstd"]])
            for t in range(NT):
                if t < 2:
                    op("act", lambda e, t=t: e.activation(out=hb[:, t, :], in_=xg[:, xi, t, :], func=AF.Identity,
                                                          scale=smv("rstd", t, t + 1)),
                       reads=[b_x[xi][t], b_sm["rstd"]], writes=[b_hb[t]])
                else:
                    op("dve", lambda e, t=t: e.tensor_scalar(out=hb[:, t, :], in0=xg[:, xi, t, :],
                                                             scalar1=smv("rstd", t, t + 1), scalar2=None, op0=ALU.mult),
                       reads=[b_x[xi][t], b_sm["rstd"]], writes=[b_hb[t]])
        def norm_tr_P(k, bi, hb=hb, b_hb=b_hb):
            pbf = banks[bi][:].bitcast(BF16)

            def f(e):
                for t in range(NT):
                    last = e.transpose(pbf[:, t * 128:(t + 1) * 128], hb[:, t, k * 128:(k + 1) * 128], ident[:])
                return last

            op("pe", f, reads=b_hb + [b_ident], writes=[b_ps[bi]])

        def norm_tr_C(k, bi, gname, dstT, b_dst):
            pbf = banks[bi][:].bitcast(BF16)
            gcol = pv(gname)[:, k:k + 1]
            if evac_eng() == "act":
                op("act", lambda e: e.activation(out=dstT[:, k, :], in_=pbf[:, 0:G], func=AF.Identity, scale=gcol),
                   reads=[b_ps[bi], b_prm], writes=[b_dst[k]])
            else:
                op("dve", lambda e: e.tensor_scalar(out=dstT[:, k, :], in0=pbf[:, 0:G], scalar1=gcol, scalar2=None,
                                                    op0=ALU.mult),
                   reads=[b_ps[bi], b_prm], writes=[b_dst[k]])

        def norm_transposes(gname, dstT, b_dst, hb=hb, b_hb=b_hb):
            for k in range(8):
                bi = tbank()
                norm_tr_P(k, bi, hb, b_hb)
                norm_tr_C(k, bi, gname, dstT, b_dst)

        def mm_fm(bi, r, c, src, b_src):
            def f(e):
                for kk in range(8):
                    last = e.matmul(banks[bi][:], lhsT=ring[:, r, kk, c * 128:(c + 1) * 128], rhs=src[:, kk, :],
                                    start=(kk == 0), stop=(kk == 7))
                return last

            op("pe", f, reads=[b_ring[r]] + list(b_src), writes=[b_ps[bi]])

        def conv_from_psum(bi, acc, b_acc, w_ap, b_ap, ntap, halo, b_halo, first, last_out=None, b_last=None):
            ps = banks[bi]
            nh = ntap - 1
            op("act", lambda e: e.activation(out=acc, in_=ps[:], func=AF.Identity, scale=w_ap[:, nh:nh + 1], bias=b_ap),
               reads=[b_ps[bi], b_prm], writes=[b_acc])
            for sft in range(1, ntap):
                j = nh - sft
                if last_out is not None and sft == nh:
                    op("dve", lambda e, sft=sft, j=j: e.scalar_tensor_tensor(out=last_out[:, sft:G], in0=ps[:, 0:G - sft],
                                                                             scalar=w_ap[:, j:j + 1], in1=acc[:, sft:G],
                                                                             op0=ALU.mult, op1=ALU.add),
                       reads=[b_ps[bi], b_prm, b_acc], writes=[b_last])
                else:
                    op("dve", lambda e, sft=sft, j=j: e.scalar_tensor_tensor(out=acc[:, sft:G], in0=ps[:, 0:G - sft],
                                                                             scalar=w_ap[:, j:j + 1], in1=acc[:, sft:G],
                                                                             op0=ALU.mult, op1=ALU.add),
                       reads=[b_ps[bi], b_prm], writes=[b_acc])
            if not first:
                for sft in range(1, ntap):
                    j = nh - sft
                    op("dve", lambda e, sft=sft, j=j: e.scalar_tensor_tensor(out=acc[:, 0:sft], in0=halo[:, nh - sft:nh],
                                                                             scalar=w_ap[:, j:j + 1], in1=acc[:, 0:sft],
                                                                             op0=ALU.mult, op1=ALU.add),
                       reads=[b_halo, b_prm], writes=[b_acc])
            if last_out is not None:
                op("dve", lambda e: e.tensor_copy(out=last_out[:, 0:nh], in_=acc[:, 0:nh]),
                   reads=[b_acc], writes=[b_last])
            op("dve", lambda e: e.tensor_copy(out=halo[:, 0:nh], in_=ps[:, G - nh:G]),
               reads=[b_ps[bi]], writes=[b_halo])

        ngrp = nseq * NG
        groups = [(sq, g) for sq in range(nseq) for g in range(NG)]

        def stage_A(gi):
            seq, g = groups[gi]
            xi = gi % 2
            first = (g == 0)
            nbanks[0] = 8
            if gi == 0:
                norm_chain(xi, hb2, b_hb2)
            norm_transposes("g1T", hT, b_hT, hb2, b_hb2)
            if gi > 0:
                norm_transposes("g2T", hT2, b_hT2)
            if gi == 0:
                dump(0, hT[:, :, :].rearrange("p a b -> p (a b)"), 4096, b_hT)
            r = load_slab(SL_IN + 1)
            for c in range(4):
                bg = tbank()
                mm_fm(bg, r, c, hT, b_hT)
                op("act", lambda e, c=c, bg=bg: e.activation(out=gg[:, c, :], in_=banks[bg][:], func=AF.Gelu_apprx_tanh),
                   reads=[b_ps[bg]], writes=[b_gg[c]])
            op("act", lambda e: e.activation(out=smv("dummy"), in_=smv("half"), func=AF.Ln),
               reads=[b_sm["half"]], writes=[b_sm["dummy"]])
            r = load_slab(SL_IN + 0)
            for c in range(4):
                bi = tbank()
                mm_fm(bi, r, c, hT, b_hT)
                w_ap = pv("lcw")[:, c * 4:(c + 1) * 4]
                conv_from_psum(bi, xc4[:, c, :], b_xc[c], w_ap, pv("lcb")[:, c:c + 1], 4, lhalo[:, c, :], b_lhalo[c], first)
            r = load_slab(SL_IN + 2)
            for h in range(4):
                bi = tbank()
                mm_fm(bi, r, h, hT, b_hT)
                if evac_eng() == "act":
                    op("act", lambda e, h=h, bi=bi: e.activation(out=qT[:, h, :], in_=banks[bi][:], func=AF.Copy,
                                                                 scale=0.125),
                       reads=[b_ps[bi]], writes=[b_qT[h]])
                else:
                    op("dve", lambda e, h=h, bi=bi: e.tensor_scalar(out=qT[:, h, :], in0=banks[bi][:], scalar1=0.125,
                                                                    scalar2=None, op0=ALU.mult),
                       reads=[b_ps[bi]], writes=[b_qT[h]])
            r = load_slab(SL_IN + 3)
            for h in range(4):
                bi = tbank()
                mm_fm(bi, r, h, hT, b_hT)
                dst = KT[:, h, g * G:(g + 1) * G]
                if evac_eng() == "act":
                    op("act", lambda e, dst=dst, bi=bi: e.activation(out=dst, in_=banks[bi][:], func=AF.Copy),
                       reads=[b_ps[bi]], writes=[b_KT[h][g]])
                else:
                    op("dve", lambda e, dst=dst, bi=bi: e.tensor_copy(out=dst, in_=banks[bi][:]),
                       reads=[b_ps[bi]], writes=[b_KT[h][g]])
            r = load_slab(SL_IN + 4)
            for t in range(NT):
                bi = tbank()
                jt = g * NT + t

                def f(e, t=t, bi=bi, r=r):
                    for kk in range(8):
                        last = e.matmul(banks[bi][:], lhsT=hT[:, kk, t * 128:(t + 1) * 128], rhs=ring[:, r, kk, :],
                                        start=(kk == 0), stop=(kk == 7))
                    return last

                op("pe", f, reads=[b_ring[r]] + b_hT, writes=[b_ps[bi]])
                src = banks[bi][:].rearrange("p (h e) -> p h e", h=4)
                dst = Va[:, jt, :, 0:128]
                if evac_eng() == "act":
                    op("act", lambda e, dst=dst, src=src: e.activation(out=dst, in_=src, func=AF.Copy),
                       reads=[b_ps[bi]], writes=[b_V[jt]])
                else:
                    op("dve", lambda e, dst=dst, src=src: e.tensor_copy(out=dst, in_=src),
                       reads=[b_ps[bi]], writes=[b_V[jt]])

        ACC0 = 2

        def acc_region(il, c):
            rr = c * 4 + il
            return ACC0 + rr // 3, (rr % 3) * 130, rr

        def make_B(gi):
            seq, g = groups[gi]
            first = (g == 0)
            nj = g * NT + NT
            info = {}
            aux = "dve" if gi == 0 else "pool"

            def lru_P(c, which, bk):
                if which == 0:
                    op(aux, lambda e: e.tensor_copy(out=xcb[:], in_=xc4[:, c, :]), reads=[b_xc[c]], writes=[b_xcb])
                op("pe", lambda e: e.matmul(banks[bk][:], lhsT=bdw[:, which, c, :], rhs=xcb[:], start=True, stop=True),
                   reads=[b_bdw, b_xcb], writes=[b_ps[bk]])

            def lru_C(c, piece, bk):
                if piece == 0:
                    op("act", lambda e: e.activation(out=thr[:], in_=banks[bk][:], func=AF.Exp, scale=-1.0,
                                                     bias=smv("nba", c, c + 1)),
                       reads=[b_ps[bk], b_sm["nba"]], writes=[b_thr])
                    op("act", lambda e: e.activation(out=thr[:], in_=thr[:], func=AF.Ln, bias=1.0, scale=1.0),
                       reads=[b_thr], writes=[b_thr])
                    op("act", lambda e: e.activation(out=thr[:], in_=thr[:], func=AF.Exp, scale=-1.0),
                       reads=[b_thr], writes=[b_thr])
                elif piece == 1:
                    op("act", lambda e: e.activation(out=la2[:], in_=thr[:], func=AF.Exp, scale=smv("c2neg", c, c + 1)),
                       reads=[b_thr, b_sm["c2neg"]], writes=[b_la2])
                    op("act", lambda e: e.activation(out=thr[:], in_=thr[:], func=AF.Exp, scale=smv("cneg", c, c + 1)),
                       reads=[b_thr, b_sm["cneg"]], writes=[b_thr])
                elif piece == 2:
                    op("act", lambda e: e.activation(out=la2[:], in_=la2[:], func=AF.Ln, bias=1.0, scale=-1.0),
                       reads=[b_la2], writes=[b_la2])
                    op("act", lambda e: e.activation(out=la2[:], in_=la2[:], func=AF.Exp, scale=0.5),
                       reads=[b_la2], writes=[b_la2])
                elif piece == 3:
                    op("act", lambda e: e.activation(out=thi[:], in_=banks[bk][:], func=AF.Exp, scale=-1.0,
                                                     bias=smv("nbx", c, c + 1)),
                       reads=[b_ps[bk], b_sm["nbx"]], writes=[b_thi])
                    op("act", lambda e: e.activation(out=thi[:], in_=thi[:], func=AF.Ln, bias=1.0, scale=1.0),
                       reads=[b_thi], writes=[b_thi])
                    op("act", lambda e: e.activation(out=thi[:], in_=thi[:], func=AF.Exp, scale=-1.0),
                       reads=[b_thi], writes=[b_thi])
                else:
                    op(aux, lambda e: e.tensor_tensor(out=thi[:], in0=thi[:], in1=xc4[:, c, :], op=ALU.mult),
                       reads=[b_thi, b_xc[c]], writes=[b_thi])
                    op("dve", lambda e: e.tensor_tensor(out=thi[:], in0=thi[:], in1=la2[:], op=ALU.mult),
                       reads=[b_thi, b_la2], writes=[b_thi])
                    init = 0.0 if first else lstate[:, c:c + 1]
                    op("dve", lambda e: e.tensor_tensor_scan(out=lh[:], data0=thr[:], data1=thi[:], initial=init,
                                                             op0=ALU.mult, op1=ALU.add),
                       reads=[b_thr, b_thi, b_lstate[c]], writes=[b_lh])
                    op("dve", lambda e: e.tensor_copy(out=lstate[:, c:c + 1], in_=lh[:, G - 1:G]),
                       reads=[b_lh], writes=[b_lstate[c]])
                    op(aux, lambda e: e.tensor_tensor(out=mixT[:, c, :], in0=gg[:, c, :], in1=lh[:], op=ALU.mult),
                       reads=[b_gg[c], b_lh], writes=[b_mix[c]])

            def emit_qk(key, bk, p):
                h, j, c = key
                i0 = max(0, j - g * NT)
                N = (NT - i0) * 128
                q0 = i0 * 128
                gk = j // NT
                op("pe", lambda e: e.matmul(banks[bk][:, 0:N], lhsT=KT[c * 64:(c + 1) * 64, h, j * 128:(j + 1) * 128],
                                            rhs=qT[c * 64:(c + 1) * 64, h, q0:G], start=True, stop=True),
                   reads=[b_KT[h][gk], b_qT[h]], writes=[b_ps[bk]])
                d0 = g * NT + i0 - j
                if d0 == 0:
                    boff, wn = 0, (256 if i0 < NT - 1 else 128)
                elif d0 == 1:
                    boff, wn = 128, 128
                else:
                    boff, wn = 0, 0
                if wn:
                    bo, nn = pk_off["biasT"]
                    bias_ap = prm[:, bo + h * 256 + boff: bo + h * 256 + boff + wn]
                    op("dve", lambda e: e.tensor_tensor(out=sbn[:, p, 0:wn], in0=banks[bk][:, 0:wn], in1=bias_ap,
                                                        op=ALU.add),
                       reads=[b_ps[bk], b_prm], writes=[b_sbn[p]])
                info[key] = (i0, N, wn)

            def emit_softmax_pv(key, bk, p, p3):
                h, j, c = key
                i0, N, wn = info.pop(key)
                if wn:
                    op("act", lambda e: e.activation(out=PT[:, p, 0:wn], in_=sbn[:, p3, 0:wn], func=AF.Exp),
                       reads=[b_sbn[p3]], writes=[b_PT[p]])
                if N > wn:
                    fo, fn_ = pk_off["farb"]
                    op("act", lambda e: e.activation(out=PT[:, p, wn:N], in_=banks[bk][:, wn:N], func=AF.Exp,
                                                     bias=prm[:, fo + h:fo + h + 1]),
                       reads=[b_ps[bk], b_prm], writes=[b_PT[p]])

                def f(e):
                    last = None
                    for il in range(i0, NT):
                        qi = g * NT + il
                        abk, off, rr = acc_region(il, c)
                        last = e.matmul(banks[abk][:, off:off + 129],
                                        lhsT=PT[:, p, (il - i0) * 128:(il - i0 + 1) * 128],
                                        rhs=Va[:, j, h, 0:129], start=(j == 0 and rr % 3 == 0), stop=(j == qi),
                                        skip_group_check=True)
                    return last

                wb = sorted(set(acc_region(il, c)[0] for il in range(i0, NT)))
                op("pe", f, reads=[b_PT[p], b_V[j]], writes=[b_ps[k_] for k_ in wb])

            def emit_norm(h):
                for abk, r0, nr in ((ACC0, 0, 3), (ACC0 + 1, 3, 3), (ACC0 + 2, 6, 2)):
                    src = banks[abk][:, 0:nr * 130].rearrange("p (r w) -> p r w", w=130)[:, :, 0:129]
                    op("dve", lambda e, src=src, r0=r0, nr=nr: e.tensor_copy(out=acp[:, r0:r0 + nr, :], in_=src),
                       reads=[b_ps[abk]], writes=[b_acp])
                op("dve", lambda e: e.reciprocal(out=smv("rs").rearrange("p (r o) -> p r o", o=1),
                                                 in_=acp[:, :, 128:129]),
                   reads=[b_acp], writes=[b_sm["rs"]])
                op("dve", lambda e: e.tensor_scalar(out=smv("rs2n"), in0=smv("rs", 4, 8), scalar1=smv("neglam"),
                                                    scalar2=None, op0=ALU.mult),
                   reads=[b_sm["rs"], b_sm["neglam"]], writes=[b_sm["rs2n"]])
                O1 = acp[:, 0:4, 0:128]
                O2 = acp[:, 4:8, 0:128]
                rs1b = smv("rs", 0, 4).unsqueeze(2).to_broadcast([128, NT, 128])
                rs2b = smv("rs2n").unsqueeze(2).to_broadcast([128, NT, 128])
                op("dve", lambda e: e.tensor_tensor(out=O1, in0=O1, in1=rs1b, op=ALU.mult),
                   reads=[b_acp, b_sm["rs"]], writes=[b_acp])
                op("dve", lambda e: e.tensor_tensor(out=O2, in0=O2, in1=rs2b, op=ALU.mult),
                   reads=[b_acp, b_sm["rs2n"]], writes=[b_acp])
                op("dve", lambda e: e.tensor_tensor(out=O1, in0=O1, in1=O2, op=ALU.add),
                   reads=[b_acp], writes=[b_acp])
                op("dve", lambda e: e.tensor_tensor(out=O2, in0=O1, in1=O1, op=ALU.mult),
                   reads=[b_acp], writes=[b_acp])
                op("dve", lambda e: e.tensor_reduce(out=smv("sso"), in_=O2, axis=AX.X, op=ALU.add),
                   reads=[b_acp], writes=[b_sm["sso"]])
                rstd_from_ss("sso", "mso", "rstdo", NT, 1.0 / 128.0, SUBLN_EPS, use_act=(gi == 0))
                rsob = smv("rstdo").unsqueeze(2).to_broadcast([128, NT, 128])
                op("dve", lambda e: e.tensor_tensor(out=ob[:, :, h, :], in0=O1, in1=rsob, op=ALU.mult),
                   reads=[b_acp, b_sm["rstdo"]], writes=[b_ob[il][h] for il in range(NT)])
            users = []
            for h in range(4):
                qks = [("qk", h, (h, j, c), True) for j in range(nj) for c in range(2)]
                for piece in range(5):
                    users.append(("lru", h, piece, True))
                    users.extend(qks[2 * piece:2 * piece + 2] if piece < 4 else qks[8:])
                users.append(("norm", h, None, True))

            def emitP(n, bk):
                kind, h, key, _ = users[n]
                if kind == "lru":
                    if key == 0:
                        lru_P(h, 0, bk)
                    elif key == 3:
                        lru_P(h, 1, bk)
                elif kind == "qk":
                    emit_qk(key, bk, n % 3)


            def emitC(n, bk):
                kind, h, key, _ = users[n]
                if kind == "lru":
                    lru_C(h, key, bk)
                elif kind == "qk":
                    emit_softmax_pv(key, bk, n % 2, n % 3)
                else:
                    emit_norm(h)

            def weight(n):
                kind, h, key, _ = users[n]
                if kind == "lru":
                    return 1.5
                if kind == "norm":
                    return 3.0
                hh, j, c = key
                return 1.6 if j >= g * NT - 1 else 1.0

            return [(lambda bk, n=n: emitP(n, bk), lambda bk, n=n: emitC(n, bk), users[n][3], weight(n))
                    for n in range(len(users))]

        def make_C(gi):
            xi = gi % 2
            ul = []
            nop1 = lambda bk: None

            def tr_P(h, bk):
                pbf = banks[bk][:].bitcast(BF16)

                def f(e):
                    for il in range(NT):
                        last = e.transpose(pbf[:, il * 128:(il + 1) * 128], ob[:, il, h, :], ident[:])
                    return last

                op("pe", f, reads=[b_ob[il][h] for il in range(NT)] + [b_ident], writes=[b_ps[bk]])

            def tr_C(h, bk):
                pbf = banks[bk][:].bitcast(BF16)
                op("act", lambda e: e.activation(out=mixT[:, 4 + h, :], in_=pbf[:, 0:G], func=AF.Identity,
                                                 scale=smv("subg2")),
                   reads=[b_ps[bk], b_sm["subg2"]], writes=[b_mix[4 + h]])

            for h in range(4):
                ul.append((lambda bk, h=h: tr_P(h, bk), lambda bk, h=h: tr_C(h, bk), True))
            if gi + 1 < ngrp:
                ul.append((nop1, lambda bk: norm_chain(1 - xi, hb2, b_hb2), True))
            cur = {}

            def wo_P(c2, t, bk):
                if t == 0:
                    cur["r"] = load_slab(SL_OUT + c2)
                r = cur["r"]

                def f(e):
                    for kk in range(8):
                        last = e.matmul(banks[bk][:], lhsT=mixT[:, kk, t * 128:(t + 1) * 128], rhs=ring[:, r, kk, :],
                                        start=(kk == 0), stop=(kk == 7))
                    return last

                op("pe", f, reads=[b_ring[r]] + b_mix, writes=[b_ps[bk]])

            def wo_C(c2, t, bk):
                xs = xg[:, xi, t, c2 * 512:(c2 + 1) * 512]
                op("dve", lambda e: e.tensor_tensor(out=xs, in0=banks[bk][:], in1=xs, op=ALU.add),
                   reads=[b_ps[bk], b_x[xi][t]], writes=[b_x[xi][t]])

            for c2 in range(2):
                for t in range(NT):
                    ul.append((lambda bk, c2=c2, t=t: wo_P(c2, t, bk), lambda bk, c2=c2, t=t: wo_C(c2, t, bk),
                               not (c2 == 0 and t == 0)))
            ul.append((nop1, lambda bk: norm_chain(xi), True))
            return ul

        def ffn_units(gi, up_banks, acc_banks):
            seq, g = groups[gi]
            xi = gi % 2
            tok0 = seq * S + g * G
            first = (g == 0)
            units = []
            rot_u = [0]

            def ubank():
                bk = up_banks[rot_u[0] % len(up_banks)]
                rot_u[0] += 1
                return bk

            cur = {}

            def gate_pe(fc):
                if fc % 4 == 0:
                    cur["r"] = load_slab(SL_UP + fc // 4)
                r = cur["r"]
                bG = ubank()
                cur[("g", fc)] = bG
                mm_fm(bG, r, fc % 4, hT2, b_hT2)

            def gate_rest(fc):
                bG = cur.pop(("g", fc))
                cc = fc % 2
                w_ap = pv("fcw")[:, fc * 3:(fc + 1) * 3]
                conv_from_psum(bG, facc[:, cc, :], b_facc[cc], w_ap, pv("fcb")[:, fc:fc + 1], 3, fhalo[:, fc, :],
                               b_fhalo[fc], first, last_out=aT[:, fc, :], b_last=b_aT[fc])

            def unit_gelu(half):
                for k4 in range(3 * half, 3 * half + 3):
                    op("act", lambda e, k4=k4: e.activation(out=aT[:, 4 * k4:4 * k4 + 4, :], in_=aT[:, 4 * k4:4 * k4 + 4, :],
                                                            func=AF.Gelu_apprx_tanh),
                       reads=b_aT[4 * k4:4 * k4 + 4], writes=b_aT[4 * k4:4 * k4 + 4])
                op("act", lambda e: e.activation(out=smv("dummy"), in_=smv("half"), func=AF.Ln),
                   reads=[b_sm["half"]], writes=[b_sm["dummy"]])

            def val_pe(fc):
                if fc % 4 == 0:
                    cur["r"] = load_slab(SL_UP + 6 + fc // 4)
                r = cur["r"]
                bV = ubank()
                cur[("v", fc)] = bV
                mm_fm(bV, r, fc % 4, hT2, b_hT2)

            def val_rest(fc):
                bV = cur.pop(("v", fc))
                op("dve", lambda e: e.tensor_tensor(out=aT[:, fc, :], in0=banks[bV][:], in1=aT[:, fc, :], op=ALU.mult),
                   reads=[b_ps[bV], b_aT[fc]], writes=[b_aT[fc]])

            def unit_down(c2, ks, ts, accs):
                r = load_slab(SL_DN + c2 * 3 + ks)
                for t, bk in zip(ts, accs):
                    def f(e, t=t, bk=bk):
                        for kk in range(8):
                            last = e.matmul(banks[bk][:], lhsT=aT[:, ks * 8 + kk, t * 128:(t + 1) * 128],
                                            rhs=ring[:, r, kk, :], start=(ks == 0 and kk == 0),
                                            stop=(ks == 2 and kk == 7))
                        return last

                    op("pe", f, reads=[b_ring[r]] + b_aT[ks * 8:(ks + 1) * 8], writes=[b_ps[bk]])

            def unit_down_evac(c2, ts, accs):
                for t, bk in zip(ts, accs):
                    xs = xg[:, xi, t, c2 * 512:(c2 + 1) * 512]
                    op("dve", lambda e, xs=xs, bk=bk: e.tensor_tensor(out=xs, in0=banks[bk][:], in1=xs, op=ALU.add),
                       reads=[b_ps[bk], b_x[xi][t]], writes=[b_x[xi][t]])

            def unit_final():
                jk = aT[:, 0:2, :].rearrange("p a b -> p (a b)")
                for t in range(NT):
                    op("act", lambda e, t=t: e.activation(out=jk, in_=xg[:, xi, t, :], func=AF.Square,
                                                          accum_out=smv("ssf", t, t + 1)),
                       reads=[b_x[xi][t]], writes=[b_aT[0], b_aT[1], b_sm["ssf"]])
                rstd_from_ss("ssf", "msf", "rstdf", NT, 1.0 / D, EPS)
                for t in range(NT):
                    op("dve", lambda e, t=t: e.scalar_tensor_tensor(out=xg[:, xi, t, :], in0=xg[:, xi, t, :],
                                                                    scalar=smv("rstdf", t, t + 1), in1=pv("gfb"),
                                                                    op0=ALU.mult, op1=ALU.mult),
                       reads=[b_x[xi][t], b_sm["rstdf"], b_prm], writes=[b_x[xi][t]])
                    tk = S_.dma("sp", out_d[tok0 + t * 128:tok0 + (t + 1) * 128, :], xg[:, xi, t, :],
                                reads=[b_x[xi][t]], writes=[b_out])
                    out_toks.append(tk)
                if gi + 2 < ngrp:
                    load_x(groups[gi + 2][0], groups[gi + 2][1], xi)

            nop = lambda: None
            for half in range(2):
                for fc in range(12 * half, 12 * half + 12):
                    units.append((lambda fc=fc: gate_pe(fc), lambda fc=fc: gate_rest(fc), True))
                units.append((nop, lambda half=half: unit_gelu(half), False, 3))
                for fc in range(12 * half, 12 * half + 12):
                    units.append((lambda fc=fc: val_pe(fc), lambda fc=fc: val_rest(fc), True))
            if gi == DBG_G:
                units.append((nop, lambda: dump(3, aT[:, 0:8, :].rearrange("p a b -> p (a b)"), 4096, b_aT[0:8]), False))
            na = len(acc_banks)
            passes = [list(range(NT))[i:i + na] for i in range(0, NT, na)]
            for c2 in range(2):
                for ts in passes:
                    accs = acc_banks[:len(ts)]
                    for ks in range(3):
                        units.append((lambda c2=c2, ks=ks, ts=ts, accs=accs: unit_down(c2, ks, ts, accs), nop, False))
                    units.append((nop, lambda c2=c2, ts=ts, accs=accs: unit_down_evac(c2, ts, accs), False))
            units.append((nop, unit_final, False))
            return units

        PAIR_A = (0, 1)
        PAIR_B = (5, 6)

        def run_window(B, fu, extra=None, n_fill=None, SLOTS=(0, 1), DEPTH=1):
            extra = list(extra or [])
            nF = len(fu)
            fi = 0
            if B is None:
                for pe_, rest_, hoist_ in [u_[:3] for u_ in fu]:
                    pe_()
                    rest_()
                return
            users = B
            nU = len(users)
            st_ = {"pi": 0}
            NS = len(SLOTS)
            cumw = [0.0]
            for u_ in users:
                cumw.append(cumw[-1] + (u_[3] if len(u_) > 3 else 1.0))

            T_ = nU + 8
            if n_fill and NS == 2:
                T_ = n_fill + 2 + ((n_fill + 2) % 2)

            def slot(m):
                return SLOTS[m % NS] if m < T_ else (5, 0, 1)[(m - T_) % 3]

            def issue(limit, done):
                while st_["pi"] < nU and st_["pi"] <= limit:
                    m = st_["pi"]
                    if users[m][2] or m - 1 <= done:
                        users[m][0](slot(m))
                        st_["pi"] += 1
                    else:
                        break

            for cur in range(nU):
                dpt = 2 if cur >= T_ - 2 else DEPTH
                issue(cur + dpt, cur - 1)
                if extra:
                    extra.pop(0)()
                nfu = n_fill if n_fill else nU
                if cur + 1 >= nfu:
                    target = nF
                else:
                    target = int(math.ceil(cumw[cur + 1] / cumw[nfu] * nF - 1e-9))
                if any(len(u_) > 3 for u_ in fu[fi:target]):
                    target = min(nF, target + 3)
                batch = fu[fi:target]
                fi = max(fi, target)
                deferred = []
                for pe_, rest_, hoist_ in [u_[:3] for u_ in batch]:
                    if hoist_ and len(deferred) < 3:
                        pe_()
                        deferred.append(rest_)
                    else:
                        for r_ in deferred:
                            r_()
                        deferred = []
                        pe_()
                        if hoist_:
                            deferred.append(rest_)
                        else:
                            rest_()
                users[cur][1](slot(cur))
                for r_ in deferred:
                    r_()
                issue(cur + dpt, cur)
            while fi < nF:
                fu[fi][0]()
                fu[fi][1]()
                fi += 1
            for u in extra:
                u()

        load_x(groups[0][0], groups[0][1], 0)
        if ngrp > 1:
            load_x(groups[1][0], groups[1][1], 1)
        for gi in range(ngrp):
            stage_A(gi)
            fu = ffn_units(gi - 1, [5, 6, 7], [5, 6, 7]) if gi > 0 else []
            ub = make_B(gi)
            if gi == 0:
                run_window(ub + make_C(gi), fu, extra=deferred_casts, SLOTS=(0, 1, 5), DEPTH=2)
            else:
                run_window(ub + make_C(gi), fu, n_fill=max(1, len(ub) - 16))
        nbanks[0] = 8
        norm_transposes("g2T", hT2, b_hT2)
        run_window(None, ffn_units(ngrp - 1, [0, 1, 2, 3], [4, 5, 6, 7]))
        S_.wait_all("sp", out_toks)
        S_.emit(block)
    return nc


_CACHE = {}


def _run(inputs, nseq, S, ncores, dbg=False):
    pk, stp = pack_params(inputs)
    params = pk.build()
    stage = stp.build()
    key = (nseq, S)
    if key not in _CACHE:
        _CACHE[key] = build(nseq, S, pk.off, pk.n, stp.off, stp.n, dbg=dbg)
    nc = _CACHE[key]
    x = np.ascontiguousarray(np.asarray(inputs["x"], np.float32))
    f32c = lambda a: np.ascontiguousarray(np.asarray(a, np.float32))
    w_in = f32c(inputs["w_in"][0])
    w_out = f32c(inputs["w_out"][0])
    w_up = f32c(inputs["ffn_w_up"][0])
    w_dn = f32c(inputs["ffn_w_down"][0])
    in_maps = []
    for c in range(ncores):
        xs = x[c * nseq:(c + 1) * nseq].reshape(nseq * S, D)
        in_maps.append({"x": np.ascontiguousarray(xs), "w_in": w_in, "w_out": w_out, "w_up": w_up, "w_down": w_dn,
                        "params": params, "stage": stage})
    res = run_bass_kernel_spmd(nc, in_maps, core_ids=list(range(ncores)))
    outs = [np.asarray(r["out"], np.float32).reshape(nseq, S, D) for r in res.results]
    if dbg:
        global DBG_OUT
        DBG_OUT = np.asarray(res.results[0]["dbg"])
    return np.concatenate(outs, axis=0)


def kernel(**inputs):
    x = inputs["x"]
    B, S, _ = x.shape
    nseq = B // NCORES
    return _run(inputs, nseq, S, NCORES)
```
